# Optimizing a Trainium2 kernel written in Bass

```python
import math
import jax, jax.numpy as jnp
from jax import lax
import numpy as np

D_MODEL = 1024
BATCH = 2
SEQ = 8192
DEPTH = 4
DEC_BATCH = 128
DEC_SEQ = 1
PAST_LEN = 8192
PAGE_SIZE = 128

N_MIXERS = 4
MIX_MLSTM, MIX_DIL, MIX_HGRN, MIX_SWA = 0, 1, 2, 3
N_MLSTM = (DEPTH - MIX_MLSTM + N_MIXERS - 1) // N_MIXERS
N_DIL = (DEPTH - MIX_DIL + N_MIXERS - 1) // N_MIXERS
N_HGRN = (DEPTH - MIX_HGRN + N_MIXERS - 1) // N_MIXERS
N_SWA = (DEPTH - MIX_SWA + N_MIXERS - 1) // N_MIXERS

D_FF = 4 * D_MODEL
ALPHA = (2 * DEPTH) ** 0.25
BETA = (8 * DEPTH) ** -0.25
LN_EPS = 1e-5
NORM_EPS = 1e-6
ROPE_THETA = 10000.0

ML_HEADS = 4
ML_DK = D_MODEL // (2 * ML_HEADS)
ML_DV = D_MODEL // ML_HEADS
ML_CHUNK = 64
ML_IN = 2 * ML_HEADS * ML_DK + 2 * ML_HEADS * ML_DV + 2 * ML_HEADS

DIL_GROUPS = ((128, 1), (512, 4), (2048, 16))
N_DIL_GROUPS = len(DIL_GROUPS)
DIL_HEADS = 8
DIL_HD = 64
DIL_IN = N_DIL_GROUPS * 3 * DIL_HEADS * DIL_HD

HG_HEADS = 8
HG_DK = D_MODEL // HG_HEADS
HG_DV = D_MODEL // HG_HEADS
HG_CHUNK = 16
HG_IN = 2 * HG_HEADS * HG_DK + 2 * HG_HEADS * HG_DV

SWA_HEADS = 16
SWA_KV_HEADS = 2
SWA_HD = 64
SWA_WINDOW = 128
SWA_IN = (SWA_HEADS + 2 * SWA_KV_HEADS) * SWA_HD

kernel_name = 'hybrid_mlstm_dilated_hgrn2_swa_step'


def layer_norm(x, g, b):
    xf = x.astype(jnp.float32)
    mu = xf.mean(-1, keepdims=True)
    var = jnp.mean(jnp.square(xf - mu), -1, keepdims=True)
    y = (xf - mu) * lax.rsqrt(var + LN_EPS) * g.astype(jnp.float32) + b.astype(jnp.float32)
    return y.astype(x.dtype)


def rms_norm_heads(h, g):
    nh, dv = h.shape[-2:]
    y = h * lax.rsqrt(jnp.mean(jnp.square(h), -1, keepdims=True) + NORM_EPS)
    return y * g.astype(jnp.float32).reshape(nh, dv)


def rope(x, pos):
    half = x.shape[-1] // 2
    inv_freq = jnp.power(ROPE_THETA, -jnp.arange(half, dtype=jnp.float32) / half)
    ang = pos.astype(jnp.float32)[:, None] * inv_freq[None, :]
    cos = jnp.cos(ang)[None, :, None, :]
    sin = jnp.sin(ang)[None, :, None, :]
    xf = x.astype(jnp.float32)
    x1, x2 = xf[..., :half], xf[..., half:]
    return jnp.concatenate([x1 * cos - x2 * sin, x2 * cos + x1 * sin], axis=-1)


def to_chunks(x, L):
    b, t = x.shape[:2]
    x = x.reshape((b, t // L, L) + x.shape[2:])
    return jnp.transpose(x, (1, 0, 3, 2) + tuple(range(4, x.ndim)))


def from_chunks(y):
    y = jnp.transpose(y, (1, 0, 3, 2) + tuple(range(4, y.ndim)))
    return y.reshape((y.shape[0], y.shape[1] * y.shape[2]) + y.shape[3:])


def sq_relu_mlp(x, w1, w2):
    return jnp.square(jax.nn.relu(x @ w1)) @ w2


def mlstm_scan(q, k, v, ig, lf, C0, n0, m0):
    T = q.shape[1]
    L = math.gcd(T, ML_CHUNK)
    causal = jnp.tril(jnp.ones((L, L), dtype=bool))

    def step(carry, blk):
        C, n, m = carry
        qc, kc, vc, ic, fc = blk
        b = jnp.cumsum(fc, axis=-1)
        log_intra = jnp.where(causal, b[..., :, None] - b[..., None, :] + ic[..., None, :], -jnp.inf)
        log_prev = b + m[..., None]
        m_t = jnp.maximum(log_prev, log_intra.max(-1))
        w_intra = jnp.exp(log_intra - m_t[..., None])
        w_prev = jnp.exp(log_prev - m_t)
        a = jnp.einsum('bhtd,bhsd->bhts', qc, kc) * w_intra
        num = jnp.einsum('bhts,bhsv->bhtv', a, vc) + w_prev[..., None] * jnp.einsum('bhtd,bhdv->bhtv', qc, C)
        den = a.sum(-1) + w_prev * jnp.einsum('bhtd,bhd->bht', qc, n)
        h = num / jnp.maximum(jnp.abs(den), jnp.exp(-m_t))[..., None]
        m_new = m_t[..., -1]
        w_end = jnp.exp(b[..., -1:] - b + ic - m_new[..., None])
        decay = jnp.exp(b[..., -1] + m - m_new)
        C_new = decay[..., None, None] * C + jnp.einsum('bhs,bhsd,bhsv->bhdv', w_end, kc, vc)
        n_new = decay[..., None] * n + jnp.einsum('bhs,bhsd->bhd', w_end, kc)
        return (C_new, n_new, m_new), h

    blocks = (to_chunks(q, L), to_chunks(k, L), to_chunks(v, L), to_chunks(ig, L), to_chunks(lf, L))
    (C, n, m), h = lax.scan(step, (C0, n0, m0), blocks)
    return from_chunks(h), C, n, m


def mlstm_mixer(x, C0, n0, m0, w_in, b_gates, norm_g, w_out):
    bsz, T, _ = x.shape
    qk = ML_HEADS * ML_DK
    vd = ML_HEADS * ML_DV
    p = (x @ w_in).astype(jnp.float32)
    q = p[..., :qk].reshape(bsz, T, ML_HEADS, ML_DK)
    k = p[..., qk:2 * qk].reshape(bsz, T, ML_HEADS, ML_DK) * ML_DK ** -0.5
    v = p[..., 2 * qk:2 * qk + vd].reshape(bsz, T, ML_HEADS, ML_DV)
    og = p[..., 2 * qk + vd:2 * qk + 2 * vd]
    gates = p[..., 2 * qk + 2 * vd:] + b_gates.astype(jnp.float32)
    ig = gates[..., :ML_HEADS]
    lf = jax.nn.log_sigmoid(gates[..., ML_HEADS:])
    h, C, n, m = mlstm_scan(q, k, v, ig, lf, C0.astype(jnp.float32), n0.astype(jnp.float32), m0.astype(jnp.float32))
    h = rms_norm_heads(h, norm_g).reshape(bsz, T, vd) * jax.nn.sigmoid(og)
    return h.astype(x.dtype) @ w_out, C, n, m


def banded_causal_attention(q, k, v, span, sinks=None):
    bsz, T, hq, dh = q.shape
    hk = k.shape[2]
    grp = hq // hk
    W = span
    Tp = -(-T // W) * W
    if Tp != T:
        padw = ((0, 0), (0, Tp - T), (0, 0), (0, 0))
        q, k, v = jnp.pad(q, padw), jnp.pad(k, padw), jnp.pad(v, padw)
    nb = Tp // W
    qb = q.reshape(bsz, nb, W, hk, grp, dh) * dh ** -0.5
    kb = k.reshape(bsz, nb, W, hk, dh)
    vb = v.reshape(bsz, nb, W, hk, dh)
    shift = ((0, 0), (1, 0), (0, 0), (0, 0), (0, 0))
    kk = jnp.concatenate([jnp.pad(kb, shift)[:, :-1], kb], axis=2)
    vv = jnp.concatenate([jnp.pad(vb, shift)[:, :-1], vb], axis=2)
    s = jnp.einsum('bnqhgd,bnkhd->bnhgqk', qb, kk)
    qi = jnp.arange(W)[:, None]
    ki = jnp.arange(2 * W)[None, :]
    dist = W + qi - ki
    band = (dist >= 0) & (dist <= W)
    valid = band[None] & ((jnp.arange(nb)[:, None, None] > 0) | (ki >= W)[None])
    s = jnp.where(valid[None, :, None, None], s, -jnp.inf)
    m = s.max(-1)
    if sinks is not None:
        sk = sinks.astype(jnp.float32).reshape(hk, grp)[None, None, :, :, None]
        m = jnp.maximum(m, sk)
    p = jnp.exp(s - m[..., None])
    l = p.sum(-1)
    if sinks is not None:
        l = l + jnp.exp(sk - m)
    o = jnp.einsum('bnhgqk,bnkhd->bnqhgd', p, vv) / jnp.transpose(l, (0, 1, 4, 2, 3))[..., None]
    lse = jnp.transpose(m + jnp.log(l), (0, 1, 4, 2, 3))
    return o.reshape(bsz, Tp, hq, dh)[:, :T], lse.reshape(bsz, Tp, hq)[:, :T]


def dil_project(x, pos, w_in):
    bsz, T, _ = x.shape
    p = (x @ w_in).reshape(bsz, T, N_DIL_GROUPS, 3, DIL_HEADS, DIL_HD)
    return [(rope(p[:, :, g, 0], pos), rope(p[:, :, g, 1], pos), p[:, :, g, 2].astype(jnp.float32))
            for g in range(N_DIL_GROUPS)]


def dil_merge(outs, lses, x, w_out):
    alpha = jax.nn.softmax(jnp.stack(lses, 0), axis=0)
    y = jnp.sum(alpha[..., None] * jnp.stack(outs, 0), axis=0)
    bsz, T = y.shape[:2]
    return y.reshape(bsz, T, DIL_HEADS * DIL_HD).astype(x.dtype) @ w_out


def dilated_group_prompt(q, k, v, dilation, span):
    bsz, T, nh, dh = q.shape
    ts = T // dilation
    def fold(a):
        return a.reshape(bsz, ts, dilation, nh, dh).transpose(0, 2, 1, 3, 4).reshape(bsz * dilation, ts, nh, dh)
    o, lse = banded_causal_attention(fold(q), fold(k), fold(v), span)
    o = o.reshape(bsz, dilation, ts, nh, dh).transpose(0, 2, 1, 3, 4).reshape(bsz, T, nh, dh)
    lse = lse.reshape(bsz, dilation, ts, nh).transpose(0, 2, 1, 3).reshape(bsz, T, nh)
    return o, lse


def dilated_mixer_prompt(x, pos, w_in, w_out):
    T = x.shape[1]
    outs, lses, kvs = [], [], []
    for g, (q, k, v) in enumerate(dil_project(x, pos, w_in)):
        win, dil = DIL_GROUPS[g]
        o, lse = dilated_group_prompt(q, k, v, dil, win // dil)
        outs.append(o)
        lses.append(lse)
        kvs.append(jnp.stack([k, v], axis=2)[:, T - min(win, T):])
    return dil_merge(outs, lses, x, w_out), kvs


def dilated_group_sample(q, k, v, kv_buf, dilation, span):
    nbat, S, nh, dh = q.shape
    L = kv_buf.shape[1]
    buf = kv_buf.astype(jnp.float32)
    kc = jnp.concatenate([buf[:, :, 0], k], axis=1)
    vc = jnp.concatenate([buf[:, :, 1], v], axis=1)
    idx = L + jnp.arange(S)[:, None] - dilation * jnp.arange(span + 1)[None, :]
    valid = idx >= 0
    idx = jnp.maximum(idx, 0)
    kg = kc[:, idx]
    vg = vc[:, idx]
    s = jnp.einsum('nshd,nsjhd->nshj', q * dh ** -0.5, kg)
    s = jnp.where(valid[None, :, None, :], s, -jnp.inf)
    m = s.max(-1)
    p = jnp.exp(s - m[..., None])
    l = p.sum(-1)
    o = jnp.einsum('nshj,nsjhd->nshd', p, vg) / l[..., None]
    new_buf = jnp.concatenate([buf, jnp.stack([k, v], axis=2)], axis=1)[:, S:]
    return o, m + jnp.log(l), new_buf


def dilated_mixer_sample(x, pos, bufs, w_in, w_out):
    outs, lses, kvs = [], [], []
    for g, (q, k, v) in enumerate(dil_project(x, pos, w_in)):
        win, dil = DIL_GROUPS[g]
        o, lse, nb = dilated_group_sample(q, k, v, bufs[g], dil, win // dil)
        outs.append(o)
        lses.append(lse)
        kvs.append(nb)
    return dil_merge(outs, lses, x, w_out), kvs


def hgrn2_scan(q, k, lf, v, S0):
    T = q.shape[1]
    L = math.gcd(T, HG_CHUNK)
    causal = jnp.tril(jnp.ones((L, L), dtype=bool))

    def step(S, blk):
        qc, kc, fc, vc = blk
        b = jnp.cumsum(fc, axis=2)
        rel = jnp.where(causal[:, :, None], b[:, :, :, None, :] - b[:, :, None, :, :], -jnp.inf)
        a = jnp.einsum('bhtd,bhtsd,bhsd->bhts', qc, jnp.exp(rel), kc)
        o = jnp.einsum('bhts,bhsv->bhtv', a, vc) + jnp.einsum('bhtd,bhdv->bhtv', qc * jnp.exp(b), S)
        b_end = b[:, :, -1]
        S_new = jnp.exp(b_end)[..., None] * S + jnp.einsum('bhsd,bhsv->bhdv', jnp.exp(b_end[:, :, None, :] - b) * kc, vc)
        return S_new, o

    blocks = (to_chunks(q, L), to_chunks(k, L), to_chunks(lf, L), to_chunks(v, L))
    S, o = lax.scan(step, S0, blocks)
    return from_chunks(o), S


def hgrn2_mixer(x, S0, w_in, b_f, lb_logits, layer_idx, norm_g, w_out):
    bsz, T, _ = x.shape
    wk = HG_HEADS * HG_DK
    wv = HG_HEADS * HG_DV
    p = (x @ w_in).astype(jnp.float32)
    q = jax.nn.silu(p[..., :wk])
    fz = p[..., wk:2 * wk] + b_f.astype(jnp.float32)
    i = p[..., 2 * wk:2 * wk + wv]
    g = p[..., 2 * wk + wv:]
    cum = jnp.cumsum(jax.nn.softmax(lb_logits.astype(jnp.float32), axis=0), axis=0)
    lb = cum[layer_idx] - cum[0]
    fg = lb + (1.0 - lb) * jax.nn.sigmoid(fz)
    lf = jnp.log(fg).reshape(bsz, T, HG_HEADS, HG_DK)
    k = (1.0 - fg).reshape(bsz, T, HG_HEADS, HG_DK)
    o, S = hgrn2_scan(q.reshape(bsz, T, HG_HEADS, HG_DK), k, lf, i.reshape(bsz, T, HG_HEADS, HG_DV), S0.astype(jnp.float32))
    o = rms_norm_heads(o, norm_g).reshape(bsz, T, wv) * jax.nn.sigmoid(g)
    return o.astype(x.dtype) @ w_out, S


def swa_project(x, pos, w_in):
    bsz, T, _ = x.shape
    p = x @ w_in
    nq = SWA_HEADS * SWA_HD
    nk = SWA_KV_HEADS * SWA_HD
    q = rope(p[..., :nq].reshape(bsz, T, SWA_HEADS, SWA_HD), pos)
    k = rope(p[..., nq:nq + nk].reshape(bsz, T, SWA_KV_HEADS, SWA_HD), pos)
    v = p[..., nq + nk:].reshape(bsz, T, SWA_KV_HEADS, SWA_HD).astype(jnp.float32)
    return q, k, v


def swa_mixer_prompt(x, pos, w_in, sinks, w_out):
    bsz, T, _ = x.shape
    q, k, v = swa_project(x, pos, w_in)
    o, _ = banded_causal_attention(q, k, v, SWA_WINDOW, sinks)
    kv = jnp.stack([k, v], axis=2)[:, T - min(SWA_WINDOW, T):]
    return o.reshape(bsz, T, SWA_HEADS * SWA_HD).astype(x.dtype) @ w_out, kv


def swa_mixer_sample(x, pos, kv_buf, w_in, sinks, w_out):
    nbat, S, _ = x.shape
    q, k, v = swa_project(x, pos, w_in)
    L = kv_buf.shape[1]
    grp = SWA_HEADS // SWA_KV_HEADS
    buf = kv_buf.astype(jnp.float32)
    kc = jnp.concatenate([buf[:, :, 0], k], axis=1)
    vc = jnp.concatenate([buf[:, :, 1], v], axis=1)
    qg = q.reshape(nbat, S, SWA_KV_HEADS, grp, SWA_HD) * SWA_HD ** -0.5
    s = jnp.einsum('nshgd,nkhd->nhgsk', qg, kc)
    dist = L + jnp.arange(S)[:, None] - jnp.arange(L + S)[None, :]
    valid = (dist >= 0) & (dist <= SWA_WINDOW)
    s = jnp.where(valid, s, -jnp.inf)
    sk = sinks.astype(jnp.float32).reshape(SWA_KV_HEADS, grp)[None, :, :, None]
    m = jnp.maximum(s.max(-1), sk)
    p = jnp.exp(s - m[..., None])
    l = p.sum(-1) + jnp.exp(sk - m)
    o = jnp.einsum('nhgsk,nkhd->nshgd', p, vc) / jnp.transpose(l, (0, 3, 1, 2))[..., None]
    new_buf = jnp.concatenate([buf, jnp.stack([k, v], axis=2)], axis=1)[:, S:]
    return o.reshape(nbat, S, SWA_HEADS * SWA_HD).astype(x.dtype) @ w_out, new_buf


def setup_inputs(seed: int = 0) -> dict:
    key = jax.random.key(seed)
    keys = iter(jax.random.split(key, 40))

    def nrm(shape, scale=1.0):
        return scale * jax.random.normal(next(keys), shape, jnp.float32)

    dil_lens = [min(w, PAST_LEN) for w, _ in DIL_GROUPS]
    swa_len = min(SWA_WINDOW, PAST_LEN)
    return dict(
        x_prompt=nrm((BATCH, SEQ, D_MODEL)),
        x_sample=nrm((DEC_BATCH, DEC_SEQ, D_MODEL)),
        state_mlstm_C=nrm((N_MLSTM, DEC_BATCH, ML_HEADS, ML_DK, ML_DV), 0.5),
        state_mlstm_n=nrm((N_MLSTM, DEC_BATCH, ML_HEADS, ML_DK), 0.5),
        state_mlstm_m=nrm((N_MLSTM, DEC_BATCH, ML_HEADS), 0.5),
        cache_dil_kv0=nrm((N_DIL, DEC_BATCH, dil_lens[0], 2, DIL_HEADS, DIL_HD)),
        cache_dil_kv1=nrm((N_DIL, DEC_BATCH, dil_lens[1], 2, DIL_HEADS, DIL_HD)),
        cache_dil_kv2=nrm((N_DIL, DEC_BATCH, dil_lens[2], 2, DIL_HEADS, DIL_HD)),
        state_hgrn_S=nrm((N_HGRN, DEC_BATCH, HG_HEADS, HG_DK, HG_DV)),
        cache_swa_kv=nrm((N_SWA, DEC_BATCH, swa_len, 2, SWA_KV_HEADS, SWA_HD)),
        mlstm_w_in=nrm((N_MLSTM, D_MODEL, ML_IN), D_MODEL ** -0.5),
        mlstm_b_gates=jnp.concatenate([nrm((N_MLSTM, ML_HEADS), 0.1), 3.0 + nrm((N_MLSTM, ML_HEADS), 0.1)], axis=-1),
        mlstm_norm_g=1.0 + nrm((N_MLSTM, ML_HEADS * ML_DV), 0.02),
        mlstm_w_out=nrm((N_MLSTM, ML_HEADS * ML_DV, D_MODEL), BETA * (ML_HEADS * ML_DV) ** -0.5),
        dil_w_in=nrm((N_DIL, D_MODEL, DIL_IN), D_MODEL ** -0.5),
        dil_w_out=nrm((N_DIL, DIL_HEADS * DIL_HD, D_MODEL), BETA * (DIL_HEADS * DIL_HD) ** -0.5),
        hgrn_w_in=nrm((N_HGRN, D_MODEL, HG_IN), D_MODEL ** -0.5),
        hgrn_b_f=nrm((N_HGRN, HG_HEADS * HG_DK), 0.1),
        hgrn_lb_logits=nrm((DEPTH, HG_HEADS * HG_DK), 0.5),
        hgrn_norm_g=1.0 + nrm((N_HGRN, HG_HEADS * HG_DV), 0.02),
        hgrn_w_out=nrm((N_HGRN, HG_HEADS * HG_DV, D_MODEL), BETA * (HG_HEADS * HG_DV) ** -0.5),
        swa_w_in=nrm((N_SWA, D_MODEL, SWA_IN), D_MODEL ** -0.5),
        swa_sinks=nrm((N_SWA, SWA_HEADS), 0.5),
        swa_w_out=nrm((N_SWA, SWA_HEADS * SWA_HD, D_MODEL), BETA * (SWA_HEADS * SWA_HD) ** -0.5),
        ln1_g=1.0 + nrm((DEPTH, D_MODEL), 0.02),
        ln1_b=nrm((DEPTH, D_MODEL), 0.02),
        ln2_g=1.0 + nrm((DEPTH, D_MODEL), 0.02),
        ln2_b=nrm((DEPTH, D_MODEL), 0.02),
        mlp_w1=nrm((DEPTH, D_MODEL, D_FF), D_MODEL ** -0.5),
        mlp_w2=nrm((DEPTH, D_FF, D_MODEL), BETA * D_FF ** -0.5),
    )


def reference(x_prompt, x_sample, state_mlstm_C, state_mlstm_n, state_mlstm_m,
              cache_dil_kv0, cache_dil_kv1, cache_dil_kv2, state_hgrn_S, cache_swa_kv,
              mlstm_w_in, mlstm_b_gates, mlstm_norm_g, mlstm_w_out,
              dil_w_in, dil_w_out,
              hgrn_w_in, hgrn_b_f, hgrn_lb_logits, hgrn_norm_g, hgrn_w_out,
              swa_w_in, swa_sinks, swa_w_out,
              ln1_g, ln1_b, ln2_g, ln2_b, mlp_w1, mlp_w2):
    hp, hs = x_prompt, x_sample
    bp = x_prompt.shape[0]
    pos_p = jnp.arange(x_prompt.shape[1], dtype=jnp.int32)
    pos_s = PAST_LEN + jnp.arange(x_sample.shape[1], dtype=jnp.int32)
    dil_caches = (cache_dil_kv0, cache_dil_kv1, cache_dil_kv2)
    ml_C_p, ml_C_s, ml_n_p, ml_n_s, ml_m_p, ml_m_s = [], [], [], [], [], []
    dil_p = [[] for _ in DIL_GROUPS]
    dil_s = [[] for _ in DIL_GROUPS]
    hg_p, hg_s, swa_p, swa_s = [], [], [], []
    for i in range(DEPTH):
        kind, occ = i % N_MIXERS, i // N_MIXERS
        if kind == MIX_MLSTM:
            zC = jnp.zeros((bp, ML_HEADS, ML_DK, ML_DV), jnp.float32)
            zn = jnp.zeros((bp, ML_HEADS, ML_DK), jnp.float32)
            zm = jnp.zeros((bp, ML_HEADS), jnp.float32)
            mix_p, C, n, m = mlstm_mixer(hp, zC, zn, zm, mlstm_w_in[occ], mlstm_b_gates[occ], mlstm_norm_g[occ], mlstm_w_out[occ])
            ml_C_p.append(C)
            ml_n_p.append(n)
            ml_m_p.append(m)
            mix_s, C, n, m = mlstm_mixer(hs, state_mlstm_C[occ], state_mlstm_n[occ], state_mlstm_m[occ],
                                         mlstm_w_in[occ], mlstm_b_gates[occ], mlstm_norm_g[occ], mlstm_w_out[occ])
            ml_C_s.append(C)
            ml_n_s.append(n)
            ml_m_s.append(m)
        elif kind == MIX_DIL:
            mix_p, kvs = dilated_mixer_prompt(hp, pos_p, dil_w_in[occ], dil_w_out[occ])
            for g in range(N_DIL_GROUPS):
                dil_p[g].append(kvs[g])
            mix_s, kvs = dilated_mixer_sample(hs, pos_s, [c[occ] for c in dil_caches], dil_w_in[occ], dil_w_out[occ])
            for g in range(N_DIL_GROUPS):
                dil_s[g].append(kvs[g])
        elif kind == MIX_HGRN:
            zS = jnp.zeros((bp, HG_HEADS, HG_DK, HG_DV), jnp.float32)
            mix_p, S_new = hgrn2_mixer(hp, zS, hgrn_w_in[occ], hgrn_b_f[occ], hgrn_lb_logits, i, hgrn_norm_g[occ], hgrn_w_out[occ])
            hg_p.append(S_new)
            mix_s, S_new = hgrn2_mixer(hs, state_hgrn_S[occ], hgrn_w_in[occ], hgrn_b_f[occ], hgrn_lb_logits, i, hgrn_norm_g[occ], hgrn_w_out[occ])
            hg_s.append(S_new)
        else:
            mix_p, kv = swa_mixer_prompt(hp, pos_p, swa_w_in[occ], swa_sinks[occ], swa_w_out[occ])
            swa_p.append(kv)
            mix_s, kv = swa_mixer_sample(hs, pos_s, cache_swa_kv[occ], swa_w_in[occ], swa_sinks[occ], swa_w_out[occ])
            swa_s.append(kv)
        hp = layer_norm(ALPHA * hp + mix_p, ln1_g[i], ln1_b[i])
        hs = layer_norm(ALPHA * hs + mix_s, ln1_g[i], ln1_b[i])
        hp = layer_norm(ALPHA * hp + sq_relu_mlp(hp, mlp_w1[i], mlp_w2[i]), ln2_g[i], ln2_b[i])
        hs = layer_norm(ALPHA * hs + sq_relu_mlp(hs, mlp_w1[i], mlp_w2[i]), ln2_g[i], ln2_b[i])
    return (hp, hs,
            jnp.stack(ml_C_p), jnp.stack(ml_C_s), jnp.stack(ml_n_p), jnp.stack(ml_n_s),
            jnp.stack(ml_m_p), jnp.stack(ml_m_s),
            jnp.stack(dil_p[0]), jnp.stack(dil_s[0]), jnp.stack(dil_p[1]), jnp.stack(dil_s[1]),
            jnp.stack(dil_p[2]), jnp.stack(dil_s[2]),
            jnp.stack(hg_p), jnp.stack(hg_s), jnp.stack(swa_p), jnp.stack(swa_s))
```

```python
import contextlib
import numpy as np
import concourse.bass as bass
import concourse.mybir as mybir
from concourse.bass_utils import run_bass_kernel_spmd

F32 = mybir.dt.float32
BF16 = mybir.dt.bfloat16
AF = mybir.ActivationFunctionType
ALU = mybir.AluOpType
AX = mybir.AxisListType

D = 1024
ALPHA = 8.0 ** 0.25
LN_EPS = 1e-5
NORM_EPS = 1e-6
PAST_LEN = 8192
DIL = ((128, 1), (512, 4), (2048, 16))

C_ID, C_U, C_BU, C_BO, C_ONES = [i * 128 for i in range(5)]
C_CI = 5 * 128
C_EP = C_CI + 4
CW = C_EP + 256
M_ID, M_U, M_L, M_U4, M_R4, M_L4, M_U16, M_R16, M_L16 = [i * 128 for i in range(9)]
MW = 9 * 128


def host_consts():
    c = np.zeros((128, CW), np.float32)
    m = np.zeros((128, MW), np.float32)
    s = np.arange(128)[:, None]
    t = np.arange(128)[None, :]
    U = (s <= t)
    L = (s >= t)
    c[:, C_ID:C_ID + 128] = (s == t)
    c[:, C_U:C_U + 128] = U
    same = (s // 32 == t // 32)
    c[:, C_BU:C_BU + 128] = same & U
    c[:, C_BO:C_BO + 128] = same
    c[:, C_ONES:C_ONES + 128] = 1.0
    for k in range(4):
        c[32 * k:32 * k + 32, C_CI + k] = 1.0
    for k in range(16):
        c[:, C_EP + k * 16 + k] = 1.0
    m[:, M_ID:M_ID + 128] = (s == t)
    m[:, M_U:M_U + 128] = U
    m[:, M_L:M_L + 128] = L
    for dil, cu, cr, cl in ((4, M_U4, M_R4, M_L4), (16, M_U16, M_R16, M_L16)):
        R = ((t - s) % dil == 0)
        m[:, cu:cu + 128] = U & R
        m[:, cr:cr + 128] = R
        m[:, cl:cl + 128] = L & R
    return c, m


def host_rope(seq):
    pos = np.concatenate([np.arange(seq), [PAST_LEN]]).astype(np.float32)
    inv = np.power(np.float32(10000.0), -np.arange(32, dtype=np.float32) / np.float32(32)).astype(np.float32)
    ang = (pos[:, None] * inv[None, :]).astype(np.float32)
    cs, sn = np.cos(ang).astype(np.float32), np.sin(ang).astype(np.float32)
    return np.concatenate([cs, cs, -sn, sn], axis=1).astype(np.float32)


class Buf:
    def __init__(self, name, h):
        self.name = name
        self.h = h
        self.w = {}
        self.r = {}

    def __getitem__(self, idx):
        return self.h[idx]


class Cur:
    def __init__(self):
        object.__setattr__(self, 'b', None)

    def set(self, b):
        object.__setattr__(self, 'b', b)

    def __getitem__(self, i):
        return self.b[i]

    def __getattr__(self, n):
        return getattr(self.b, n)

    def __setattr__(self, n, v):
        setattr(self.b, n, v)


class Sched:
    NDMA = 40
    NSW = 6
    SAME_ENGINE_SYNC = True

    def __init__(self, nc):
        self.nc = nc
        self.eng = {'pe': nc.tensor, 'act': nc.scalar, 'dve': nc.vector, 'pool': nc.gpsimd, 'sp': nc.sync}
        self.sems, self.count = {}, {}
        self.known = {e: {} for e in self.eng}
        self.dma_rr = 0
        self.n_own = 0
        self.sw_rr = 0
        self.n_inst = 0

    def __enter__(self):
        self.stack = contextlib.ExitStack()
        self.stack.__enter__()
        for e in self.eng:
            self.sems[e] = self.stack.enter_context(self.nc.semaphore("prog_" + e))
            self.count[e] = 0
        for i in range(self.NDMA):
            k = "dma%d" % i
            self.sems[k] = self.stack.enter_context(self.nc.semaphore(k))
            self.count[k] = 0
        for i in range(self.NSW):
            k = "dmasw%d" % i
            self.sems[k] = self.stack.enter_context(self.nc.semaphore(k))
            self.count[k] = 0
        self.block = self.stack.enter_context(self.nc.Block())
        return self

    def __exit__(self, *a):
        return self.stack.__exit__(*a)

    def sb(self, name, shape, dtype):
        return Buf(name, self.stack.enter_context(self.nc.sbuf_tensor("sb_" + name, list(shape), dtype)))

    def ps(self, name, shape, dtype):
        return Buf(name, self.stack.enter_context(self.nc.psum_tensor("ps_" + name, list(shape), dtype)))

    def _wait(self, e, key, val):
        if key == e and (e == 'pe' or not self.SAME_ENGINE_SYNC):
            return
        if self.known[e].get(key, 0) >= val:
            return
        self.eng[e].wait_ge(self.sems[key], val)
        self.known[e][key] = val

    def _deps(self, e, reads, writes):
        for b in reads:
            for k, v in b.w.items():
                self._wait(e, k, v)
        for b in writes:
            for k, v in b.w.items():
                self._wait(e, k, v)
            for k, v in b.r.items():
                self._wait(e, k, v)

    def _record(self, key, val, reads, writes, acc_w=False):
        for b in reads:
            if b.r.get(key, 0) < val:
                b.r[key] = val
        for b in writes:
            if acc_w:
                b.w[key] = val
            else:
                b.w = {key: val}
                b.r = {}

    def op(self, e, fn, reads=(), writes=()):
        self._deps(e, reads, writes)
        ins = fn(self.eng[e])
        self.count[e] += 1
        ins.then_inc(self.sems[e], 1)
        self._record(e, self.count[e], reads, writes)
        self.n_inst += 1

    def dve(self, fn, reads=(), writes=()):
        self.op('dve', fn, reads, writes)

    def act(self, fn, reads=(), writes=()):
        self.op('act', fn, reads, writes)

    def pe(self, fn, reads=(), writes=()):
        self.op('pe', fn, reads, writes)

    def dma(self, q, out, in_, reads=(), writes=(), acc_w=False, own_sem=False, **kw):
        if own_sem:
            k = "dmax%d" % self.n_own
            self.n_own += 1
            self.sems[k] = self.stack.enter_context(self.nc.semaphore(k))
            self.count[k] = 0
        elif q == 'pool':
            k = "dmasw%d" % self.sw_rr
            self.sw_rr = (self.sw_rr + 1) % self.NSW
        else:
            k = "dma%d" % self.dma_rr
            self.dma_rr = (self.dma_rr + 1) % self.NDMA
        if self.count[k] > 0:
            self._wait(q, k, self.count[k])
        self._deps(q, reads, writes)
        ins = self.eng[q].dma_start(out=out, in_=in_, **kw)
        self.count[k] += 16
        ins.then_inc(self.sems[k], 16)
        self._record(k, self.count[k], reads, writes, acc_w)
        self.n_inst += 1

    def finish(self):
        for k in list(self.sems):
            if k.startswith("dma") and self.count[k] > 0:
                self._wait('sp', k, self.count[k])
        for e in ('pe', 'act', 'dve', 'pool'):
            if self.count[e] > 0:
                self.eng['sp'].wait_ge(self.sems[e], self.count[e])


def build(SEQ, NS):
    NT = SEQ // 128
    nc = bass.Bass("TRN2", target_bir_lowering=False)

    def din(name, shape):
        return nc.dram_tensor(name, list(shape), F32, kind="ExternalInput").ap()

    def dout(name, shape):
        return nc.dram_tensor(name, list(shape), F32, kind="ExternalOutput").ap()

    xp = din("xp", [SEQ, D]); xs = din("xs", [NS, D])
    st_C = din("st_C", [NS, 4, 128, 256]); st_n = din("st_n", [NS, 4, 128]); st_m = din("st_m", [NS, 4])
    kvin = [din("kv%d" % g, [NS, DIL[g][0], 2, 8, 64]) for g in range(3)]
    st_S = din("st_S", [NS, 8, 128, 128]); swain = din("swakv", [NS, 128, 2, 2, 64])
    ml_win = din("ml_win", [D, 3080]); ml_bg = din("ml_bg", [8]); ml_ng = din("ml_ng", [D]); ml_wout = din("ml_wout", [D, D])
    dl_win = din("dl_win", [D, 4608]); dl_wout = din("dl_wout", [512, D])
    hg_win = din("hg_win", [D, 4096]); hg_bf = din("hg_bf", [D]); hg_lb = din("hg_lb", [4, D]); hg_ng = din("hg_ng", [D]); hg_wout = din("hg_wout", [D, D])
    sw_win = din("sw_win", [D, 1280]); sw_sink = din("sw_sink", [16]); sw_wout = din("sw_wout", [D, D])
    ln_g = [din("ln1_g", [4, D]), din("ln2_g", [4, D])]
    ln_b = [din("ln1_b", [4, D]), din("ln2_b", [4, D])]
    w1 = din("w1", [4, D, 4096]); w2 = din("w2", [4, 4096, D])
    cst = din("cst", [128, CW]); cstm = din("cstm", [128, MW]); ropet = din("rope", [SEQ + 1, 128])

    def dscr(name, shape):
        return nc.dram_tensor(name, list(shape), BF16).ap()

    wb = {}
    for nm, ap_ in (("ml_win", ml_win), ("ml_wout", ml_wout), ("dl_win", dl_win), ("dl_wout", dl_wout), ("hg_win", hg_win),
                    ("hg_wout", hg_wout), ("sw_win", sw_win), ("sw_wout", sw_wout)):
        wb[nm] = ap_
    for l in range(4):
        wb["w1_%d" % l] = w1[l]; wb["w2_%d" % l] = w2[l]
    unit_ix = {}
    for nm, Wf in wb.items():
        K_, N_ = Wf.shape
        for c0 in range(0, N_, 512):
            for k0 in range(0, K_, 1024):
                unit_ix[(nm, k0, c0)] = len(unit_ix)
    wscr = dscr("wscr", [len(unit_ix), 128, 4096])
    yp = dout("yp", [SEQ, D]); ys = dout("ys", [NS, D])
    oC_p = dout("oC_p", [4, 128, 256]); oC_s = dout("oC_s", [NS, 4, 128, 256])
    on_p = dout("on_p", [4, 128]); on_s = dout("on_s", [NS, 4, 128])
    om_p = dout("om_p", [1, 4]); om_s = dout("om_s", [NS, 4])
    okv_p = [dout("okv%d_p" % g, [min(DIL[g][0], SEQ), 2, 8, 64]) for g in range(3)]
    okv_s = [dout("okv%d_s" % g, [NS, DIL[g][0], 2, 8, 64]) for g in range(3)]
    oS_p = dout("oS_p", [8, 128, 128]); oS_s = dout("oS_s", [NS, 8, 128, 128])
    osw_p = dout("osw_p", [128, 2, 2, 64]); osw_s = dout("osw_s", [NS, 128, 2, 2, 64])

    S = Sched(nc)
    with S:
        sb, dve, act, pe = S.sb, S.dve, S.act, S.pe
        PB = [S.ps("pb%d" % i, [128, 512], F32) for i in range(8)]
        CST = sb("cst", [128, CW], F32)
        CB = sb("cstb", [128, MW], BF16)
        NSUB = 4 if NT % 4 == 0 else (2 if NT % 2 == 0 else 1)
        XS = [sb("X%d" % i, [128, D], F32) for i in range(NSUB)]
        X = Cur()
        X.set(XS[0])
        XT = sb("XT", [128, 8, 128], BF16)
        HT = sb("HT", [128, 8, 128], BF16)
        NW = 3
        WR = [sb("wr%d" % i, [128, 8, 512], BF16) for i in range(NW)]
        T1 = sb("T1", [128, D], F32); T2 = sb("T2", [128, D], F32); T3 = sb("T3", [128, D], F32)
        T4 = sb("T4", [128, D], F32)
        HB = sb("HB", [128, D], BF16)
        ROPE = sb("ROPE", [128, 128], F32)
        sm = sb("sm", [128, 64], F32)
        ML_NG = ml_ng; HG_NG = hg_ng; BFB = T2
        OML = sb("OML", [128, D], F32)
        LBB = None
        MLBG = sb("MLBG", [128, 8], F32); ESINK = sb("ESINK", [128, 16], F32)
        wstate = {'i': 0}

        def cs(c0, n=128, p=128):
            return CST[0:p, c0:c0 + n]

        def cb(c0, n=128):
            return CB[:, c0:c0 + n]

        S.dma('sp', CST[:], cst, writes=[CST])
        S.dma('pool', CB[:], cstm, writes=[CB])
        S.dma('sp', MLBG[:], ml_bg.partition_broadcast(128), writes=[MLBG])
        S.dma('sp', ESINK[:], sw_sink.partition_broadcast(128), writes=[ESINK])
        act(lambda e: e.activation(ESINK[:], ESINK[:], AF.Exp), [ESINK], [ESINK])
        TMX = XS[0]
        for i, T in enumerate((T1, T2, T3, T4)):
            S.dma('sp', T[:], hg_lb[i].partition_broadcast(128), writes=[T])
        dve(lambda e: e.tensor_max(TMX[:], T1[:], T2[:]), [T1, T2], [TMX])
        dve(lambda e: e.tensor_max(OML[:], T3[:], T4[:]), [T3, T4], [OML])
        dve(lambda e: e.tensor_max(TMX[:], TMX[:], OML[:]), [TMX, OML], [TMX])
        for T in (T1, T2, T3, T4):
            dve(lambda e: e.tensor_sub(T[:], T[:], TMX[:]), [T, TMX], [T])
            act(lambda e: e.activation(T[:], T[:], AF.Exp), [T], [T])
        dve(lambda e: e.tensor_add(TMX[:], T2[:], T3[:]), [T2, T3], [TMX])
        dve(lambda e: e.tensor_add(OML[:], T1[:], T4[:]), [T1, T4], [OML])
        dve(lambda e: e.tensor_add(T1[:], TMX[:], OML[:]), [TMX, OML], [T1])
        dve(lambda e: e.reciprocal(T1[:], T1[:]), [T1], [T1])
        dve(lambda e: e.tensor_mul(OML[:], OML[:], T1[:]), [OML, T1], [OML])

        shift_jobs = []
        for src_, dst_, L_ in ((kvin[2], okv_s[2], 2048), (kvin[1], okv_s[1], 512), (kvin[0], okv_s[0], 128), (swain, osw_s, 128)):
            for s_ in range(NS):
                a_ = src_[s_, 1:L_].rearrange("l a h d -> (l a h d)").rearrange("(o i) -> o i", o=128)
                b_ = dst_[s_, 0:L_ - 1].rearrange("l a h d -> (l a h d)").rearrange("(o i) -> o i", o=128)
                shift_jobs.append((b_, a_))

        def issue_shift(n):
            for _ in range(n):
                if shift_jobs:
                    b_, a_ = shift_jobs.pop(0)
                    S.dma('sp', b_, a_)

        WBUF = Buf("wscratch", None)

        def precast():
            for nm, Wf in wb.items():
                K_, N_ = Wf.shape
                for c0 in range(0, N_, 512):
                    w = min(512, N_ - c0)
                    for k0 in range(0, K_, 1024):
                        kc = min(8, (K_ - k0) // 128)
                        wu = WR[wstate['i'] % NW]
                        wstate['i'] += 1
                        S.dma('pool', wu[:, 0:kc, 0:w], Wf[k0:k0 + kc * 128, c0:c0 + w].rearrange("(k p) n -> p k n", p=128),
                              writes=[wu])
                        S.dma('sp', wscr[unit_ix[(nm, k0, c0)]][:, 0:kc * w].rearrange("p (k n) -> p k n", k=kc), wu[:, 0:kc, 0:w],
                              reads=[wu], writes=[WBUF], acc_w=True)

        def load_unit(nm, k0, kc, c0, w):
            wu = WR[wstate['i'] % NW]
            wstate['i'] += 1
            src = wscr[unit_ix[(nm, k0, c0)]][:, 0:kc * w].rearrange("p (k n) -> p k n", k=kc)
            S.dma('pool', wu[:, 0:kc, 0:w], src, reads=[WBUF], writes=[wu])
            return wu

        precast()

        def to_fm(src, nt, KC=8, dst=None, bf=False, dsts=None, c0=0):
            dst = dst or XT
            for half in range((KC + 3) // 4):
                n = min(4, KC - 4 * half)
                bank = PB[2 + half % 2]
                if bf:
                    pv = bank[:].bitcast(BF16).rearrange("p (a b) -> p a b", a=8)
                    idn = cb(M_ID)[0:nt, 0:nt]
                else:
                    pv = bank[:].rearrange("p (a b) -> p a b", a=4)
                    idn = cs(C_ID)[0:nt, 0:nt]
                for j in range(n):
                    kc = 4 * half + j
                    pe(lambda e: e.transpose(pv[:, j, 0:nt], src[0:nt, kc * 128:(kc + 1) * 128], idn),
                       [src, CB if bf else CST], [bank])
                if dsts is not None:
                    dbuf = dsts[half][0]
                    o = dsts[half][1][:, 0:n, c0:c0 + nt]
                else:
                    dbuf = dst
                    o = dst[:, 4 * half:4 * half + n, 0:nt]
                if half % 2 == 0:
                    act(lambda e: e.copy(o, pv[:, 0:n, 0:nt]), [bank], [dbuf])
                else:
                    dve(lambda e: e.tensor_copy(o, pv[:, 0:n, 0:nt]), [bank], [dbuf])

        lin = {'i': 0, 'j': 0}

        def linear_tm(xT, KC, Wd, N, nt, consumer):
            for si, c0 in enumerate(range(0, N, 512)):
                w = min(512, N - c0)
                bank = PB[lin['i'] % 2]
                lin['i'] += 1
                wu = load_unit(Wd, 0, KC, c0, w)
                for kc in range(KC):
                    pe(lambda e: e.matmul(bank[0:nt, 0:w], xT[:, kc, 0:nt], wu[:, kc, 0:w],
                                          start=(kc == 0), stop=(kc == KC - 1)), [xT, wu], [bank])
                consumer(si, c0, w, bank, bank[0:nt, 0:w])

        def resid_consumer(nt):
            def f(si, c0, w, bank, ap):
                dve(lambda e: e.scalar_tensor_tensor(X[0:nt, c0:c0 + w], X[0:nt, c0:c0 + w], ALPHA, ap,
                                                     ALU.mult, ALU.add), [X, bank], [X])
            return f

        def layer_norm(nt, which, layer, load=True):
            G, B = T3, T4
            if load:
                S.dma('sp', G[:], ln_g[which][layer].partition_broadcast(128), writes=[G])
                S.dma('sp', B[:], ln_b[which][layer].partition_broadcast(128), writes=[B])
            st = sm[0:nt, 0:12].rearrange("p (a b) -> p a b", a=2)
            for hf in range(2):
                dve(lambda e: e.bn_stats(st[:, hf, :], X[0:nt, hf * 512:(hf + 1) * 512]), [X], [sm])
            dve(lambda e: e.bn_aggr(sm[0:nt, 12:14], st), [sm], [sm])
            dve(lambda e: e.tensor_scalar_add(sm[0:nt, 14:15], sm[0:nt, 13:14], LN_EPS), [sm], [sm])
            act(lambda e: e.activation(sm[0:nt, 14:15], sm[0:nt, 14:15], AF.Sqrt), [sm], [sm])
            dve(lambda e: e.reciprocal(sm[0:nt, 15:16], sm[0:nt, 14:15]), [sm], [sm])
            dve(lambda e: e.tensor_scalar(X[0:nt, :], X[0:nt, :], sm[0:nt, 12:13], sm[0:nt, 15:16],
                                          ALU.subtract, ALU.mult), [X, sm], [X])
            dve(lambda e: e.tensor_mul(X[0:nt, :], X[0:nt, :], G[0:nt, :]), [X, G], [X])
            dve(lambda e: e.tensor_add(X[0:nt, :], X[0:nt, :], B[0:nt, :]), [X, B], [X])

        def mlp(nsub, nt, layer):
            NTOK = (nsub - 1) * 128 + nt
            xtb = [(QR, QR[:].bitcast(BF16).rearrange("p (k t) -> p k t", k=4)),
                   (SIGO, SIGO[:].bitcast(BF16).rearrange("p (k t) -> p k t", k=4))]
            htb = [(HK, HK[:].bitcast(BF16).rearrange("p (k t) -> p k t", k=4)),
                   (HG, HG[:].bitcast(BF16).rearrange("p (k t) -> p k t", k=4))]
            for st in range(nsub):
                X.set(XS[st])
                to_fm(X, nt, dsts=xtb, c0=st * 128)
                act(lambda e: e.mul(X[0:nt, :], X[0:nt, :], ALPHA), [X], [X])
            for grp in range(4):
                for u in range(2):
                    wu = load_unit("w1_%d" % layer, 0, 8, grp * 1024 + u * 512, 512)
                    for j in range(4):
                        bank = PB[4 + (lin['i'] % 4)]
                        lin['i'] += 1
                        for kc in range(8):
                            pe(lambda e: e.matmul(bank[:, 0:NTOK], wu[:, kc, j * 128:(j + 1) * 128], xtb[kc // 4][1][:, kc % 4, 0:NTOK],
                                                  start=(kc == 0), stop=(kc == 7)), [wu, xtb[kc // 4][0]], [bank])
                        tv = T1 if j % 2 == 0 else T2
                        act(lambda e: e.activation(tv[:, 0:NTOK], bank[:, 0:NTOK], AF.Relu), [bank], [tv])
                        dve(lambda e: e.tensor_mul(htb[u][1][:, j, 0:NTOK], tv[:, 0:NTOK], tv[:, 0:NTOK]), [tv], [htb[u][0]])
                for sl in range(2):
                    wu = load_unit("w2_%d" % layer, grp * 1024, 8, sl * 512, 512)
                    for st in range(nsub):
                        X.set(XS[st])
                        bank = PB[lin['j'] % 4]
                        lin['j'] += 1
                        for kc in range(8):
                            pe(lambda e: e.matmul(bank[0:nt, :], htb[kc // 4][1][:, kc % 4, st * 128:st * 128 + nt], wu[:, kc, :],
                                                  start=(kc == 0), stop=(kc == 7)), [htb[kc // 4][0], wu], [bank])
                        dve(lambda e: e.tensor_add(X[0:nt, sl * 512:(sl + 1) * 512], X[0:nt, sl * 512:(sl + 1) * 512], bank[0:nt, :]),
                            [X, bank], [X])
            for st in range(nsub):
                X.set(XS[st])
                layer_norm(nt, 1, layer, load=(st == 0))

        def rms_gate_out(src, nh, hd, nt, NGB, gate, wout):
            sv = src[0:nt, :].rearrange("p (h d) -> p h d", h=nh)
            t3 = T3[0:nt, :].rearrange("p (h d) -> p h d", h=nh)
            dve(lambda e: e.tensor_mul(T3[0:nt, :], src[0:nt, :], src[0:nt, :]), [src], [T3])
            dve(lambda e: e.tensor_reduce(sm[0:nt, 16:16 + nh], t3, AX.X, ALU.add), [T3], [sm])
            dve(lambda e: e.tensor_scalar(sm[0:nt, 16:16 + nh], sm[0:nt, 16:16 + nh], 1.0 / hd, NORM_EPS,
                                          ALU.mult, ALU.add), [sm], [sm])
            act(lambda e: e.activation(sm[0:nt, 16:16 + nh], sm[0:nt, 16:16 + nh], AF.Sqrt), [sm], [sm])
            dve(lambda e: e.reciprocal(sm[0:nt, 32:32 + nh], sm[0:nt, 16:16 + nh]), [sm], [sm])
            dve(lambda e: e.tensor_mul(t3, sv, sm[0:nt, 32:32 + nh].unsqueeze(2).broadcast_to([nt, nh, hd])),
                [src, sm], [T3])
            S.dma('sp', T2[:], NGB.partition_broadcast(128), writes=[T2])
            dve(lambda e: e.tensor_mul(T3[0:nt, :], T3[0:nt, :], T2[0:nt, :]), [T3, T2], [T3])
            dve(lambda e: e.tensor_mul(HB[0:nt, :], T3[0:nt, :], gate[0:nt, :]), [T3, gate], [HB])
            to_fm(HB, nt, bf=True)
            linear_tm(XT, 8, wout, D, nt, resid_consumer(nt))

        def load_rope(row0, nt):
            S.dma('sp', ROPE[0:nt, :], ropet[row0:row0 + nt, :] if nt > 1 else ropet[row0:row0 + 1, :],
                  writes=[ROPE])

        def rope_apply(dst, ap, nh, nt, bank):
            xv = ap.rearrange("p (h d) -> p h d", h=nh)
            xh = ap.rearrange("p (h a d) -> p h a d", h=nh, a=2)
            t4 = T4[0:nt, 0:nh * 64].rearrange("p (h a d) -> p h a d", h=nh, a=2)
            t4f = T4[0:nt, 0:nh * 64].rearrange("p (h d) -> p h d", h=nh)
            cc = ROPE[0:nt, 0:64].unsqueeze(1).broadcast_to([nt, nh, 64])
            s1 = ROPE[0:nt, 64:96].unsqueeze(1).broadcast_to([nt, nh, 32])
            s2 = ROPE[0:nt, 96:128].unsqueeze(1).broadcast_to([nt, nh, 32])
            dve(lambda e: e.tensor_mul(t4[:, :, 0, :], xh[:, :, 1, :], s1), [bank, ROPE], [T4])
            dve(lambda e: e.tensor_mul(t4[:, :, 1, :], xh[:, :, 0, :], s2), [bank, ROPE], [T4])
            dve(lambda e: e.tensor_mul(dst, xv, cc), [bank, ROPE], [dst_buf[0]])
            dve(lambda e: e.tensor_add(dst, dst, t4f), [T4, dst_buf[0]], [dst_buf[0]])

        dst_buf = [None]

        C_AUG = sb("C_AUG", [128, 4, 257], F32); C_BF = sb("C_BF", [128, 4, 257], BF16)
        M_BC = sb("M_BC", [128, 4], F32)
        K_BF = sb("K_BF", [128, 512], BF16)
        V_AUG = sb("V_AUG", [128, 4, 257], BF16)
        SIGO = sb("SIGO", [128, D], F32)
        GATES = sb("GATES", [128, 8], F32)
        mls = sb("mls", [128, 64], F32)
        UT = sb("UT", [4, 128], F32); MROWS = sb("MROWS", [4, 128], F32); MR4 = sb("MR4", [4, 4, 128], F32)
        WT = sb("WT", [128, 4, 128], F32); EE = sb("EE", [128, 4, 128], F32)
        dve(lambda e: e.memset(V_AUG[:], 1.0), [], [V_AUG])
        dve(lambda e: e.memset(C_AUG[:], 0.0), [], [C_AUG])
        dve(lambda e: e.memset(M_BC[:], 0.0), [], [M_BC])

        def mlstm_proj(nt):
            def cons(si, c0, w, bank, ap):
                if si == 0:
                    act(lambda e: e.copy(Q_TM[0:nt, :], ap), [bank], [Q_TM])
                elif si == 1:
                    act(lambda e: e.mul(K_TM[0:nt, :], ap, 128.0 ** -0.5), [bank], [K_TM])
                    dve(lambda e: e.tensor_copy(K_BF[0:nt, :], K_TM[0:nt, :]), [K_TM], [K_BF])
                elif si in (2, 3):
                    hh = 2 * (si - 2)
                    act(lambda e: e.copy(V_AUG[0:nt, hh:hh + 2, 0:256], ap.rearrange("p (h d) -> p h d", h=2)),
                        [bank], [V_AUG])
                elif si in (4, 5):
                    cc = (si - 4) * 512
                    act(lambda e: e.activation(SIGO[0:nt, cc:cc + 512], ap, AF.Sigmoid), [bank], [SIGO])
                else:
                    dve(lambda e: e.tensor_add(GATES[0:nt, :], ap, MLBG[0:nt, :]), [bank, MLBG], [GATES])
            linear_tm(XT, 8, "ml_win", 3080, nt, cons)
            act(lambda e: e.activation(mls[0:nt, 0:4], GATES[0:nt, 4:8], AF.Exp, scale=-1.0), [GATES], [mls])
            dve(lambda e: e.tensor_scalar_add(mls[0:nt, 0:4], mls[0:nt, 0:4], 1.0), [mls], [mls])
            act(lambda e: e.activation(mls[0:nt, 0:4], mls[0:nt, 0:4], AF.Ln), [mls], [mls])
            dve(lambda e: e.tensor_scalar_mul(mls[0:nt, 0:4], mls[0:nt, 0:4], -1.0), [mls], [mls])

        def mlstm_prompt(t):
            nt = 128
            to_fm(X, nt)
            mlstm_proj(nt)
            for src, dstT in ((Q_TM, QT), (K_BF, KT)):
                bank = PB[4]
                pv = bank[:].bitcast(BF16).rearrange("p (a b) -> p a b", a=8)
                for h in range(4):
                    pe(lambda e: e.transpose(pv[:, h, :], src[:, h * 128:(h + 1) * 128], cb(M_ID)), [src, CB], [bank])
                dve(lambda e: e.tensor_copy(dstT[:], pv[:, 0:4, :]), [bank], [dstT])
            b5 = PB[5]
            pe(lambda e: e.matmul(b5[:, 0:4], cs(C_U), mls[:, 0:4], start=True, stop=True), [CST, mls], [b5])
            pe(lambda e: e.matmul(b5[:, 8:12], cs(C_ONES), mls[:, 0:4], start=True, stop=True), [CST, mls], [b5])
            dve(lambda e: e.tensor_copy(mls[:, 4:8], b5[:, 0:4]), [b5], [mls])
            dve(lambda e: e.tensor_copy(mls[:, 8:12], b5[:, 8:12]), [b5], [mls])
            dve(lambda e: e.tensor_sub(mls[:, 12:16], GATES[:, 0:4], mls[:, 4:8]), [GATES, mls], [mls])
            pe(lambda e: e.transpose(b5[0:4, 128:256], mls[:, 12:16], cs(C_ID)), [mls, CST], [b5])
            act(lambda e: e.copy(UT[:], b5[0:4, 128:256]), [b5], [UT])
            dve(lambda e: e.tensor_mul(mls[0:4, 16:20], M_BC[0:4, :], cs(C_ID)[0:4, 0:4]), [M_BC, CST], [mls])
            dve(lambda e: e.tensor_reduce(mls[0:4, 20:21], mls[0:4, 16:20], AX.X, ALU.add), [mls], [mls])
            dve(lambda e: e.tensor_tensor_scan(MROWS[:], UT[:], UT[:], mls[0:4, 20:21], ALU.max, ALU.max),
                [UT, mls], [MROWS])
            mb = PB[6]
            mbv = mb[:].rearrange("p (h t) -> p h t", h=4)
            dve(lambda e: e.tensor_mul(MR4[:], MROWS[:].unsqueeze(1).broadcast_to([4, 4, 128]),
                                       cs(C_ID)[0:4, 0:4].unsqueeze(2).broadcast_to([4, 4, 128])), [MROWS, CST], [MR4])
            for h in range(4):
                pe(lambda e: e.matmul(mbv[:, h, :], cs(C_ONES)[0:4, :], MR4[:, h, :],
                                      start=True, stop=True), [CST, MR4], [mb])
            pe(lambda e: e.transpose(b5[:, 16:20], MROWS[:], cs(C_ID)[0:4, 0:4]), [MROWS, CST], [b5])
            dve(lambda e: e.tensor_add(mls[:, 24:28], mls[:, 4:8], b5[:, 16:20]), [mls, b5], [mls])
            act(lambda e: e.activation(mls[:, 28:32], mls[:, 24:28], AF.Exp, scale=-1.0), [mls], [mls])
            for h in range(4):
                act(lambda e: e.activation(WT[:, h, :], mbv[:, h, :], AF.Exp, scale=-1.0, bias=mls[:, 12 + h:13 + h]),
                    [mb, mls], [WT])
                act(lambda e: e.activation(EE[:, h, :], mbv[:, h, :], AF.Exp, scale=-1.0, bias=M_BC[:, h:h + 1]),
                    [mb, M_BC], [EE])
            dve(lambda e: e.tensor_mul(KH[:], K_TM[:].rearrange("p (h d) -> p h d", h=4),
                                       WT[:, :, 127:128].broadcast_to([128, 4, 128])), [K_TM, WT], [KH])
            dve(lambda e: e.tensor_copy(mls[:, 32:36], EE[:, :, 127:128].rearrange("p h o -> p (h o)")), [EE], [mls])
            dve(lambda e: e.tensor_add(mls[:, 36:40], mls[:, 8:12], mbv[:, :, 127:128].rearrange("p h o -> p (h o)")),
                [mls, mb], [mls])
            dve(lambda e: e.tensor_mul(WT[:], WT[:], cb(M_U).unsqueeze(1).broadcast_to([128, 4, 128])), [WT, CB], [WT])
            sc = PB[7]
            scv = sc[:].rearrange("p (h t) -> p h t", h=4)
            for h in range(4):
                pe(lambda e: e.matmul(scv[:, h, :], KT[:, h, :], QT[:, h, :], start=True, stop=True), [KT, QT], [sc])
            dve(lambda e: e.tensor_mul(ATW[:], WT[:], scv), [WT, sc], [ATW])
            dve(lambda e: e.tensor_mul(QTS[:], QT[:], EE[:]), [QT, EE], [QTS])
            act(lambda e: e.copy(C_BF[:], C_AUG[:]), [C_AUG], [C_BF])
            nb = [PB[4], PB[5], PB[2], PB[3]]
            for h in range(4):
                o = nb[h][:, 0:257]
                pe(lambda e: e.matmul(o, ATW[:, h, :], V_AUG[:, h, :], start=True, stop=False), [ATW, V_AUG], [nb[h]])
                pe(lambda e: e.matmul(o, QTS[:, h, :], C_BF[:, h, :], start=False, stop=True), [QTS, C_BF], [nb[h]])
            for h in range(4):
                o = nb[h][:, 0:257]
                act(lambda e: e.activation(mls[:, 40 + h:41 + h], o[:, 256:257], AF.Abs), [nb[h]], [mls])
                dve(lambda e: e.tensor_max(mls[:, 40 + h:41 + h], mls[:, 40 + h:41 + h], mls[:, 28 + h:29 + h]), [mls], [mls])
            dve(lambda e: e.reciprocal(mls[:, 44:48], mls[:, 40:44]), [mls], [mls])
            for h in range(4):
                o = nb[h][:, 0:256]
                act(lambda e: e.activation(T1[:, h * 256:(h + 1) * 256], o, AF.Copy, scale=mls[:, 44 + h:45 + h]),
                    [nb[h], mls], [T1])
            ub = [PB[6], PB[7], PB[2], PB[3]]
            for h in range(4):
                o = ub[h][:, 0:257]
                pe(lambda e: e.matmul(o, KH[:, h, :], V_AUG[:, h, :], start=True, stop=True), [KH, V_AUG], [ub[h]])
            for h in range(4):
                o = ub[h][:, 0:257]
                dve(lambda e: e.scalar_tensor_tensor(C_AUG[:, h, :], C_AUG[:, h, :], mls[:, 32 + h:33 + h], o,
                                                     ALU.mult, ALU.add), [C_AUG, mls, ub[h]], [C_AUG])
            dve(lambda e: e.tensor_copy(M_BC[:], mls[:, 36:40]), [mls], [M_BC])
            rms_gate_out(T1, 4, 256, nt, ML_NG, SIGO, "ml_wout")
            if t == NT - 1:
                S.dma('sp', oC_p.rearrange("h d v -> d h v"), C_AUG[:, :, 0:256], reads=[C_AUG])
                S.dma('sp', on_p.rearrange("h d -> d h"), C_AUG[:, :, 256:257].rearrange("p h o -> p (h o)"),
                      reads=[C_AUG], allow_slow_non_contiguous=True)
                S.dma('sp', om_p, M_BC[0:1, :], reads=[M_BC])

        RING = [2, 5, 17]
        KTR = [[sb("ktr%d_%d" % (g, i), [128, 4, 128], BF16) for i in range(RING[g])] for g in range(3)]
        VR = [[sb("vr%d_%d" % (g, i), [128, 8, 65], BF16) for i in range(RING[g])] for g in range(3)]
        for g in range(3):
            for v in VR[g]:
                dve(lambda e: e.memset(v[:], 1.0), [], [v])
        QTG = [sb("qtg%d" % g, [128, 4, 128], BF16) for g in range(3)]
        QR = sb("QR", [128, 1024], F32)
        QRB = sb("QRB", [128, 1024], BF16)
        KR = sb("KR", [128, 512], F32); KRB = sb("KRB", [128, 512], BF16)
        VF = sb("VF", [128, 512], F32)
        PT = [sb("PT%d" % i, [128, 4, 128], BF16) for i in range(4)]
        OALL = sb("OALL", [128, 16, 65], F32)
        YB = sb("YB", [128, 1024], BF16)
        Q_TM = KRB; K_TM = KR; QT, KT, KH = QTG; ATW, QTS = PT[0], PT[1]
        MASKS = {1: (M_U, None, M_L), 4: (M_U4, M_R4, M_L4), 16: (M_U16, M_R16, M_L16)}
        att = {'i': 0}

        def attn_run(heads):
            chunks = []
            for hi, (units, h) in enumerate(heads):
                n = len(units)
                for c0 in range(0, n, 4):
                    chunks.append((units[c0:c0 + 4], c0 == 0, c0 + 4 >= n, PB[6 + hi % 2], h))
            sbanks = [PB[0], PB[1], PB[4], PB[5]]
            N = len(chunks)

            def st_score(i):
                bank = sbanks[i % 4]
                pv = bank[:].rearrange("p (a b) -> p a b", a=4)
                for j, u in enumerate(chunks[i][0]):
                    pe(lambda e: e.matmul(pv[:, j, :], u[0], u[1], start=True, stop=True), [u[4], u[5]], [bank])

            def st_soft(i):
                bank = sbanks[i % 4]
                pt = PT[i % 4]
                pv = bank[:].rearrange("p (a b) -> p a b", a=4)
                us = chunks[i][0]
                m = len(us)
                act(lambda e: e.activation(pt[:, 0:m, :], pv[:, 0:m, :], AF.Exp, scale=0.125), [bank], [pt])
                j0 = 0
                while j0 < m:
                    j1 = j0
                    while j1 < m and us[j1][3] == us[j0][3]:
                        j1 += 1
                    mk = us[j0][3]
                    if mk is not None:
                        dve(lambda e: e.tensor_mul(pt[:, j0:j1, :], pt[:, j0:j1, :],
                                                   cb(mk).unsqueeze(1).broadcast_to([128, j1 - j0, 128])), [pt, CB], [pt])
                    j0 = j1

            def st_pv(i):
                us, first, last, accb, h = chunks[i]
                pt = PT[i % 4]
                m = len(us)
                for j, u in enumerate(us):
                    pe(lambda e: e.matmul(accb[:, 0:65], pt[:, j, :], u[2], start=(first and j == 0), stop=(last and j == m - 1)),
                       [pt, u[6]], [accb])
                if last:
                    dve(lambda e: e.tensor_copy(OALL[:, h, :], accb[:, 0:65]), [accb], [OALL])

            for i in range(N + 2):
                if i < N:
                    st_score(i)
                if 0 <= i - 1 < N:
                    st_soft(i - 1)
                if 0 <= i - 2 < N:
                    st_pv(i - 2)

        def dil_prompt(t):
            nt = 128
            to_fm(X, nt)
            load_rope(t * 128, nt)

            def cons(si, c0, w, bank, ap):
                g, j = si // 3, si % 3
                slot = t % RING[g]
                if j == 0:
                    dst_buf[0] = QR
                    rope_apply(QR[0:nt, 0:512].rearrange("p (h d) -> p h d", h=8), ap, 8, nt, bank)
                    dve(lambda e: e.tensor_copy(QRB[:, 0:512], QR[:, 0:512]), [QR], [QRB])
                    tb = PB[2]
                    pv = tb[:].bitcast(BF16).rearrange("p (a b) -> p a b", a=8)
                    for pr in range(4):
                        pe(lambda e: e.transpose(pv[:, pr, :], QRB[:, pr * 128:(pr + 1) * 128], cb(M_ID)), [QRB, CB], [tb])
                    act(lambda e: e.copy(QTG[g][:], pv[:, 0:4, :]), [tb], [QTG[g]])
                elif j == 1:
                    dst_buf[0] = KR
                    rope_apply(KR[0:nt, :].rearrange("p (h d) -> p h d", h=8), ap, 8, nt, bank)
                    dve(lambda e: e.tensor_copy(KRB[:], KR[:]), [KR], [KRB])
                    tb = PB[3]
                    pv = tb[:].bitcast(BF16).rearrange("p (a b) -> p a b", a=8)
                    for pr in range(4):
                        pe(lambda e: e.transpose(pv[:, pr, :], KRB[:, pr * 128:(pr + 1) * 128], cb(M_ID)), [KRB, CB], [tb])
                    act(lambda e: e.copy(KTR[g][slot][:], pv[:, 0:4, :]), [tb], [KTR[g][slot]])
                    win = min(DIL[g][0], SEQ)
                    if t * 128 >= SEQ - win:
                        r0 = t * 128 - (SEQ - win)
                        S.dma('sp', okv_p[g][r0:r0 + 128, 0].rearrange("l h d -> l (h d)"), KR[:], reads=[KR])
                else:
                    act(lambda e: e.copy(VF[:], ap), [bank], [VF])
                    dve(lambda e: e.tensor_copy(VR[g][slot][:, :, 0:64], VF[:].rearrange("p (h d) -> p h d", h=8)),
                        [VF], [VR[g][slot]])
                    win = min(DIL[g][0], SEQ)
                    if t * 128 >= SEQ - win:
                        r0 = t * 128 - (SEQ - win)
                        S.dma('sp', okv_p[g][r0:r0 + 128, 1].rearrange("l h d -> l (h d)"), VF[:], reads=[VF])
            linear_tm(XT, 8, "dl_win", 4608, nt, cons)
            heads = []
            for h in range(8):
                pr, e0 = h // 2, (h % 2) * 64
                units = []
                for g in (2, 1, 0):
                    win, dil = DIL[g]
                    nd = win // 128
                    mu, mm, ml = MASKS[dil]
                    for dlt in range(0, nd + 1):
                        if t - dlt < 0:
                            continue
                        slot = (t - dlt) % RING[g]
                        mk = mu if dlt == 0 else (ml if dlt == nd else mm)
                        units.append((KTR[g][slot][e0:e0 + 64, pr, :], QTG[g][e0:e0 + 64, pr, :], VR[g][slot][:, h, :],
                                      mk, KTR[g][slot], QTG[g], VR[g][slot]))
                mids = [u for u in units if u[3] in (M_R16, M_R4)]
                mids.sort(key=lambda u: 0 if u[3] == M_R16 else 1)
                units = mids + [u for u in units if u[3] not in (M_R16, M_R4)]
                heads.append((units, h))
            attn_run(heads)
            dve(lambda e: e.reciprocal(sm[:, 48:56], OALL[:, 0:8, 64:65].rearrange("p h o -> p (h o)")), [OALL], [sm])
            dve(lambda e: e.tensor_mul(YB[:, 0:512].rearrange("p (h d) -> p h d", h=8), OALL[:, 0:8, 0:64],
                                       sm[:, 48:56].unsqueeze(2).broadcast_to([128, 8, 64])), [OALL, sm], [YB])
            to_fm(YB, nt, KC=4, bf=True)
            linear_tm(XT, 4, "dl_wout", D, nt, resid_consumer(nt))

        SKT = [sb("skt%d" % i, [128, 2, 128], BF16) for i in range(2)]
        SV = [sb("sv%d" % i, [128, 2, 65], BF16) for i in range(2)]
        for v in SV:
            dve(lambda e: e.memset(v[:], 1.0), [], [v])
        SQT = sb("SQT", [128, 8, 128], BF16)
        KDUP = sb("KDUP", [128, 2, 2, 64], BF16)

        def swa_proj(nt, pos_row):
            load_rope(pos_row, nt)

            def cons(si, c0, w, bank, ap):
                if si < 2:
                    dst_buf[0] = QR
                    rope_apply(QR[0:nt, si * 512:(si + 1) * 512].rearrange("p (h d) -> p h d", h=8), ap, 8, nt, bank)
                else:
                    dst_buf[0] = KR
                    rope_apply(KR[0:nt, 0:128].rearrange("p (h d) -> p h d", h=2), ap[:, 0:128], 2, nt, bank)
                    act(lambda e: e.copy(VF[0:nt, 0:128], ap[:, 128:256]), [bank], [VF])
            linear_tm(XT, 8, "sw_win", 1280, nt, cons)

        def swa_prompt(t):
            nt = 128
            to_fm(X, nt)
            swa_proj(nt, t * 128)
            slot = t % 2
            dve(lambda e: e.tensor_copy(QRB[:], QR[:]), [QR], [QRB])
            for half in range(2):
                tb = PB[2 + half]
                pv = tb[:].bitcast(BF16).rearrange("p (a b) -> p a b", a=8)
                for pr in range(4):
                    c = (half * 4 + pr) * 128
                    pe(lambda e: e.transpose(pv[:, pr, :], QRB[:, c:c + 128], cb(M_ID)), [QRB, CB], [tb])
                act(lambda e: e.copy(SQT[:, half * 4:half * 4 + 4, :], pv[:, 0:4, :]), [tb], [SQT])
            kr = KR[:, 0:128].rearrange("p (h d) -> p h d", h=2)
            for dup in range(2):
                dve(lambda e: e.tensor_copy(KDUP[:, :, dup, :], kr), [KR], [KDUP])
            tb = PB[2]
            pv = tb[:].bitcast(BF16).rearrange("p (a b) -> p a b", a=8)
            for kvh in range(2):
                pe(lambda e: e.transpose(pv[:, kvh, :], KDUP[:, kvh, :, :].rearrange("p a d -> p (a d)"), cb(M_ID)),
                   [KDUP, CB], [tb])
            act(lambda e: e.copy(SKT[slot][:], pv[:, 0:2, :]), [tb], [SKT[slot]])
            dve(lambda e: e.tensor_copy(SV[slot][:, :, 0:64], VF[:, 0:128].rearrange("p (h d) -> p h d", h=2)),
                [VF], [SV[slot]])
            if t == NT - 1:
                S.dma('sp', osw_p[:, 0].rearrange("l h d -> l (h d)"), KR[:, 0:128], reads=[KR])
                S.dma('sp', osw_p[:, 1].rearrange("l h d -> l (h d)"), VF[:, 0:128], reads=[VF])
            heads = []
            for h in range(16):
                kvh, pr, e0 = h // 8, h // 2, (h % 2) * 64
                units = []
                for dlt in (0, 1):
                    if t - dlt < 0:
                        continue
                    sl = (t - dlt) % 2
                    units.append((SKT[sl][e0:e0 + 64, kvh, :], SQT[e0:e0 + 64, pr, :], SV[sl][:, kvh, :],
                                  M_U if dlt == 0 else M_L, SKT[sl], SQT, SV[sl]))
                heads.append((units, h))
            attn_run(heads)
            swa_finish(nt)

        def swa_finish(nt):
            dve(lambda e: e.tensor_add(sm[0:nt, 48:64], OALL[0:nt, :, 64:65].rearrange("p h o -> p (h o)"), ESINK[0:nt, :]),
                [OALL, ESINK], [sm])
            dve(lambda e: e.reciprocal(sm[0:nt, 48:64], sm[0:nt, 48:64]), [sm], [sm])
            dve(lambda e: e.tensor_mul(YB[0:nt, :].rearrange("p (h d) -> p h d", h=16), OALL[0:nt, :, 0:64],
                                       sm[0:nt, 48:64].unsqueeze(2).broadcast_to([nt, 16, 64])), [OALL, sm], [YB])
            to_fm(YB, nt, bf=True)
            linear_tm(XT, 8, "sw_wout", D, nt, resid_consumer(nt))

        HS = sb("HS", [128, 8, 128], F32)
        HSB = [sb("HSB%d" % c, [128, 8, 128], BF16) for c in range(4)]
        QTM = [sb("QTM%d" % c, [128, 8, 128], BF16) for c in range(4)]
        HQ = QR; HLF = SIGO; HK = sb("HK", [128, D], F32)
        HV = HB; HG = sb("HG", [128, D], F32)
        HQB = QRB; HKB = YB; HKH = T4
        HKM2 = [sb("HKM%d" % c, [128, D], BF16) for c in range(2)]
        HKM = [HKM2[0], HKM2[1], HKM2[0], HKM2[1]]
        HQT = SQT; HKT = sb("HKT", [128, 8, 128], BF16)
        HDEC = sb("HDEC", [128, 8, 4], F32)
        HAT = sb("HAT", [128, 8, 128], BF16)
        dve(lambda e: e.memset(HS[:], 0.0), [], [HS])
        for c in range(4):
            dve(lambda e: e.memset(QTM[c][:], 0.0), [], [QTM[c]])

        def hgrn_proj(nt):
            S.dma('sp', T2[:], hg_bf.partition_broadcast(128), writes=[T2])
            def cons(si, c0, w, bank, ap):
                cc = (si % 2) * 512
                if si < 2:
                    act(lambda e: e.activation(HQ[0:nt, cc:cc + 512], ap, AF.Silu), [bank], [HQ])
                elif si < 4:
                    dve(lambda e: e.tensor_add(T1[0:nt, cc:cc + 512], ap, BFB[0:nt, cc:cc + 512]), [bank, BFB], [T1])
                    act(lambda e: e.activation(T1[0:nt, cc:cc + 512], T1[0:nt, cc:cc + 512], AF.Sigmoid), [T1], [T1])
                    dve(lambda e: e.tensor_scalar(T1[0:nt, cc:cc + 512], T1[0:nt, cc:cc + 512], -1.0, 1.0, ALU.mult, ALU.add),
                        [T1], [T1])
                    dve(lambda e: e.tensor_mul(T1[0:nt, cc:cc + 512], T1[0:nt, cc:cc + 512], OML[0:nt, cc:cc + 512]),
                        [T1, OML], [T1])
                    dve(lambda e: e.tensor_scalar(T1[0:nt, cc:cc + 512], T1[0:nt, cc:cc + 512], -1.0, 1.0, ALU.mult, ALU.add),
                        [T1], [T1])
                    act(lambda e: e.activation(HLF[0:nt, cc:cc + 512], T1[0:nt, cc:cc + 512], AF.Ln), [T1], [HLF])
                    dve(lambda e: e.tensor_scalar(HK[0:nt, cc:cc + 512], T1[0:nt, cc:cc + 512], -1.0, 1.0,
                                                  ALU.mult, ALU.add), [T1], [HK])
                elif si < 6:
                    act(lambda e: e.copy(HV[0:nt, cc:cc + 512], ap), [bank], [HV])
                else:
                    act(lambda e: e.activation(HG[0:nt, cc:cc + 512], ap, AF.Sigmoid), [bank], [HG])
            linear_tm(XT, 8, "hg_win", 4096, nt, cons)

        def hgrn_prompt(t):
            nt = 128
            to_fm(X, nt)
            hgrn_proj(nt)
            for hf in range(2):
                cc = hf * 512
                bb, be = PB[4 + hf], PB[6 + hf]
                pe(lambda e: e.matmul(bb[:, :], cs(C_BU), HLF[:, cc:cc + 512], start=True, stop=True), [CST, HLF], [bb])
                pe(lambda e: e.matmul(be[:, :], cs(C_BO), HLF[:, cc:cc + 512], start=True, stop=True), [CST, HLF], [be])
                act(lambda e: e.copy(T1[:, cc:cc + 512], bb[:, :]), [bb], [T1])
                dve(lambda e: e.tensor_sub(T2[:, cc:cc + 512], be[:, :], T1[:, cc:cc + 512]), [be, T1], [T2])
            act(lambda e: e.activation(T3[:], T1[:], AF.Exp), [T1], [T3])
            dve(lambda e: e.tensor_mul(HQB[:], HQ[:], T3[:]), [HQ, T3], [HQB])
            act(lambda e: e.activation(T3[:], T1[:], AF.Exp, scale=-1.0), [T1], [T3])
            dve(lambda e: e.tensor_mul(HKB[:], HK[:], T3[:]), [HK, T3], [HKB])
            act(lambda e: e.activation(T3[:], T2[:], AF.Exp), [T2], [T3])
            dve(lambda e: e.tensor_mul(HKH[:], HK[:], T3[:]), [HK, T3], [HKH])
            for src, dstT in ((HQB, HQT), (HKB, HKT)):
                to_fm(src, nt, bf=True, dst=dstT)
            for c in range(4):
                dve(lambda e: e.tensor_copy(QTM[c][:, :, 32 * c:32 * c + 32], HQT[:, :, 32 * c:32 * c + 32]), [HQT], [QTM[c]])
            db = PB[4]
            dbv = db[:, 0:32].rearrange("p (h c) -> p h c", h=8)
            for h in range(8):
                pe(lambda e: e.matmul(dbv[:, h, :], HLF[:, h * 128:(h + 1) * 128], CST[:, C_CI:C_CI + 4],
                                      start=True, stop=True), [HLF, CST], [db])
            act(lambda e: e.activation(HDEC[:], dbv, AF.Exp), [db], [HDEC])
            for c in range(4):
                dve(lambda e: e.tensor_scalar_mul(HKM[c][:], HKH[:], CST[:, C_CI + c:C_CI + c + 1]), [HKH, CST], [HKM[c]])
                act(lambda e: e.copy(HSB[c][:], HS[:]), [HS], [HSB[c]])
                ub = [PB[6], PB[7]]
                for h in range(8):
                    o = ub[h // 4][:, (h % 4) * 128:(h % 4) * 128 + 128]
                    pe(lambda e: e.matmul(o, HKM[c][:, h * 128:(h + 1) * 128], HV[:, h * 128:(h + 1) * 128],
                                          start=True, stop=True), [HKM[c], HV], [ub[h // 4]])
                dve(lambda e: e.tensor_mul(HS[:], HS[:], HDEC[:, :, c:c + 1].broadcast_to([128, 8, 128])), [HS, HDEC], [HS])
                for hf in range(2):
                    dve(lambda e: e.tensor_add(HS[:, 4 * hf:4 * hf + 4, :], HS[:, 4 * hf:4 * hf + 4, :],
                                               ub[hf][:, :].rearrange("p (h v) -> p h v", h=4)), [HS, ub[hf]], [HS])
            for hf in range(2):
                sbk = PB[4 + hf]
                sv = sbk[:].rearrange("p (h t) -> p h t", h=4)
                for j in range(4):
                    h = 4 * hf + j
                    pe(lambda e: e.matmul(sv[:, j, :], HKT[:, h, :], HQT[:, h, :], start=True, stop=True), [HKT, HQT], [sbk])
                dve(lambda e: e.tensor_mul(HAT[:, 4 * hf:4 * hf + 4, :], sv, cs(C_BU).unsqueeze(1).broadcast_to([128, 4, 128])),
                    [sbk, CST], [HAT])
            for h in range(8):
                bank = PB[6 + h % 2]
                o = bank[:, 0:128]
                pe(lambda e: e.matmul(o, HAT[:, h, :], HV[:, h * 128:(h + 1) * 128], start=True, stop=False), [HAT, HV], [bank])
                for c in range(4):
                    pe(lambda e: e.matmul(o, QTM[c][:, h, :], HSB[c][:, h, :], start=False, stop=(c == 3)),
                       [QTM[c], HSB[c]], [bank])
                act(lambda e: e.copy(T1[:, h * 128:(h + 1) * 128], o), [bank], [T1])
            rms_gate_out(T1, 8, 128, nt, HG_NG, HG, "hg_wout")
            if t == NT - 1:
                S.dma('sp', oS_p.rearrange("h d v -> d h v"), HS[:], reads=[HS])

        mixers = [mlstm_prompt, dil_prompt, hgrn_prompt, swa_prompt]
        for mt in range(NT // NSUB):
            for st in range(NSUB):
                t = mt * NSUB + st
                S.dma('sp', XS[st][:], xp[t * 128:(t + 1) * 128, :], writes=[XS[st]])
            for layer in range(4):
                for st in range(NSUB):
                    X.set(XS[st])
                    lin['i'] = 0
                    mixers[layer](mt * NSUB + st)
                    layer_norm(128, 0, layer)
                mlp(NSUB, 128, layer)
            for st in range(NSUB):
                t = mt * NSUB + st
                S.dma('sp', yp[t * 128:(t + 1) * 128, :], XS[st][:], reads=[XS[st]])
            issue_shift(-(-4 * NS // (NT // NSUB)))

        nt = NS
        SSA = sb("SSA", [128, 128], F32); SSB = sb("SSB", [128, 128], F32); SSC = sb("SSC", [128, 128], F32)
        EP = lambda s: CST[:, C_EP + s * 16:C_EP + s * 16 + NS]
        idNS = cs(C_ID)[0:NS, 0:NS]

        def shift_copy(src, dst, L, rowlen):
            for s in range(NS):
                a = src[s, 1:L].rearrange("l a h d -> (l a h d)").rearrange("(o i) -> o i", o=128)
                b = dst[s, 0:L - 1].rearrange("l a h d -> (l a h d)").rearrange("(o i) -> o i", o=128)
                S.dma('sp', b, a)

        def mlstm_sample():
            to_fm(X, nt)
            mlstm_proj(nt)
            S.dma('sp', mls[0:nt, 4:8], st_m, writes=[mls])
            dve(lambda e: e.tensor_add(mls[0:nt, 8:12], mls[0:nt, 0:4], mls[0:nt, 4:8]), [mls], [mls])
            dve(lambda e: e.tensor_max(mls[0:nt, 12:16], mls[0:nt, 8:12], GATES[0:nt, 0:4]), [mls, GATES], [mls])
            S.dma('sp', om_s, mls[0:nt, 12:16], reads=[mls])
            dve(lambda e: e.tensor_sub(mls[0:nt, 16:20], mls[0:nt, 8:12], mls[0:nt, 12:16]), [mls], [mls])
            act(lambda e: e.activation(mls[0:nt, 16:20], mls[0:nt, 16:20], AF.Exp), [mls], [mls])
            dve(lambda e: e.tensor_sub(mls[0:nt, 20:24], GATES[0:nt, 0:4], mls[0:nt, 12:16]), [mls, GATES], [mls])
            act(lambda e: e.activation(mls[0:nt, 20:24], mls[0:nt, 20:24], AF.Exp), [mls], [mls])
            act(lambda e: e.activation(mls[0:nt, 28:32], mls[0:nt, 12:16], AF.Exp, scale=-1.0), [mls], [mls])
            dve(lambda e: e.tensor_mul(T2[0:nt, 0:512].rearrange("p (h d) -> p h d", h=4),
                                       K_TM[0:nt, :].rearrange("p (h d) -> p h d", h=4),
                                       mls[0:nt, 20:24].unsqueeze(2).broadcast_to([nt, 4, 128])), [K_TM, mls], [T2])
            VA = T3[0:nt, 0:1024].rearrange("p (h d) -> p h d", h=4)
            dve(lambda e: e.tensor_copy(VA, V_AUG[0:nt, :, 0:256]), [V_AUG], [T3])
            dsel = T4[0:nt, 0:NS * 4].rearrange("p (s h) -> p s h", h=4)
            dve(lambda e: e.tensor_mul(dsel, idNS.unsqueeze(2).broadcast_to([nt, NS, 4]),
                                       mls[0:nt, 16:20].unsqueeze(1).broadcast_to([nt, NS, 4])), [CST, mls], [T4])
            pb = PB[4]
            pe(lambda e: e.matmul(pb[:, 0:NS * 4], cs(C_ONES)[0:NS, :], T4[0:nt, 0:NS * 4], start=True, stop=True),
               [CST, T4], [pb])
            WPB = SSA
            dve(lambda e: e.tensor_copy(WPB[:, 0:NS * 4], pb[:, 0:NS * 4]), [pb], [WPB])
            dve(lambda e: e.tensor_copy(T1[0:nt, 0:512], Q_TM[0:nt, :]), [Q_TM], [T1])
            tb = PB[2]
            tv = tb[:].rearrange("p (a b) -> p a b", a=4)
            for h in range(4):
                pe(lambda e: e.transpose(tv[:, h, 0:nt], T1[0:nt, h * 128:(h + 1) * 128], idNS), [T1, CST], [tb])
            QF = SSB
            qf = QF[:, 0:4 * NS].rearrange("p (h s) -> p h s", h=4)
            act(lambda e: e.copy(qf, tv[:, :, 0:nt]), [tb], [QF])
            acc = [PB[0], PB[1], PB[6], PB[7]]
            CS_ = HK
            CSN = HG
            for s in range(NS):
                KM = VF
                dve(lambda e: e.tensor_scalar_mul(KM[0:nt, 0:512], T2[0:nt, 0:512], cs(C_ID)[0:nt, s:s + 1]), [T2, CST], [KM])
                QM = SSC
                qm = QM[:, 0:4 * NS].rearrange("p (h s) -> p h s", h=4)
                dve(lambda e: e.tensor_mul(qm, qf, CST[:, C_EP + s * 16:C_EP + s * 16 + NS].unsqueeze(1).broadcast_to([128, 4, NS])),
                    [QF, CST], [QM])
                for h in range(4):
                    S.dma('sp', CS_[:, 0:256], st_C[s, h], writes=[CS_])
                    S.dma('sp', CS_[:, 256:257], st_n[s, h].rearrange("(d o) -> d o", o=1), writes=[CS_])
                    ub = PB[5]
                    pe(lambda e: e.matmul(ub[:, 0:256], KM[0:nt, h * 128:(h + 1) * 128], VA[:, h, :], start=True, stop=True),
                       [KM, T3], [ub])
                    pe(lambda e: e.matmul(ub[:, 256:257], KM[0:nt, h * 128:(h + 1) * 128], cs(C_ONES)[0:nt, 0:1], start=True, stop=True),
                       [KM, CST], [ub])
                    dve(lambda e: e.scalar_tensor_tensor(CSN[:, 0:257], CS_[:, 0:257], WPB[:, s * 4 + h:s * 4 + h + 1],
                                                         ub[:, 0:257], ALU.mult, ALU.add), [CS_, WPB, ub], [CSN])
                    S.dma('sp', oC_s[s, h], CSN[:, 0:256], reads=[CSN])
                    S.dma('sp', on_s[s, h].rearrange("(d o) -> d o", o=1), CSN[:, 256:257], reads=[CSN])
                    pe(lambda e: e.matmul(acc[h][0:nt, 0:257], qm[:, h, :], CSN[:, 0:257], start=(s == 0), stop=(s == NS - 1)),
                       [QM, CSN], [acc[h]])
            for h in range(4):
                act(lambda e: e.activation(mls[0:nt, 40 + h:41 + h], acc[h][0:nt, 256:257], AF.Abs), [acc[h]], [mls])
                dve(lambda e: e.tensor_max(mls[0:nt, 40 + h:41 + h], mls[0:nt, 40 + h:41 + h], mls[0:nt, 28 + h:29 + h]), [mls], [mls])
            dve(lambda e: e.reciprocal(mls[0:nt, 44:48], mls[0:nt, 40:44]), [mls], [mls])
            for h in range(4):
                act(lambda e: e.activation(T1[0:nt, h * 256:(h + 1) * 256], acc[h][0:nt, 0:256], AF.Copy,
                                           scale=mls[0:nt, 44 + h:45 + h]), [acc[h], mls], [T1])
            rms_gate_out(T1, 4, 256, nt, ML_NG, SIGO, "ml_wout")

        def hgrn_sample():
            to_fm(X, nt)
            hgrn_proj(nt)
            act(lambda e: e.activation(T2[0:nt, :], HLF[0:nt, :], AF.Exp), [HLF], [T2])
            FF, QF32 = SSA, SSB
            for src, dstb, dt_is_bf in ((T2, FF, False), (HQ, QF32, False)):
                for half in range(2):
                    tb = PB[2 + half]
                    tv = tb[:].rearrange("p (a b) -> p a b", a=4)
                    for j in range(4):
                        h = half * 4 + j
                        pe(lambda e: e.transpose(tv[:, j, 0:nt], src[0:nt, h * 128:(h + 1) * 128], idNS), [src, CST], [tb])
                    act(lambda e: e.copy(dstb[:, half * 4 * NS:(half + 1) * 4 * NS].rearrange("p (h s) -> p h s", h=4),
                                         tv[:, :, 0:nt]), [tb], [dstb])
            ffv = FF[:, 0:8 * NS].rearrange("p (h s) -> p h s", h=8)
            qfv = QF32[:, 0:8 * NS].rearrange("p (h s) -> p h s", h=8)
            dve(lambda e: e.tensor_copy(T4[0:nt, :], HV[0:nt, :]), [HV], [T4])
            acc = [PB[0], PB[1]]
            for s in range(NS):
                KM = HG if False else T1
                dve(lambda e: e.tensor_scalar_mul(KM[0:nt, :], HK[0:nt, :], cs(C_ID)[0:nt, s:s + 1]), [HK, CST], [KM])
                QM = SSC
                qm = QM[:, 0:8 * NS].rearrange("p (h s) -> p h s", h=8)
                dve(lambda e: e.tensor_mul(qm, qfv, CST[:, C_EP + s * 16:C_EP + s * 16 + NS].unsqueeze(1).broadcast_to([128, 8, NS])),
                    [QF32, CST], [QM])
                for h in range(8):
                    SS_, SN_ = HS, HAT
                    SSf = HS[:, 0, :]
                    S.dma('sp', SSf, st_S[s, h], writes=[HS])
                    ub = PB[4 + h % 2]
                    pe(lambda e: e.matmul(ub[:, 0:128], KM[0:nt, h * 128:(h + 1) * 128], T4[0:nt, h * 128:(h + 1) * 128],
                                          start=True, stop=True), [KM, T4], [ub])
                    SNf = HS[:, 1 + h % 2, :]
                    dve(lambda e: e.scalar_tensor_tensor(SNf, SSf, ffv[:, h, s:s + 1], ub[:, 0:128], ALU.mult, ALU.add),
                        [HS, FF, ub], [HS])
                    S.dma('sp', oS_s[s, h], SNf, reads=[HS])
                    a = acc[h // 4][0:nt, (h % 4) * 128:(h % 4) * 128 + 128]
                    pe(lambda e: e.matmul(a, qm[:, h, :], SNf, start=(s == 0 and h % 4 == 0), stop=(s == NS - 1 and h % 4 == 3),
                                          skip_group_check=True), [QM, HS], [acc[h // 4]])
            for hf in range(2):
                act(lambda e: e.copy(T1[0:nt, hf * 512:(hf + 1) * 512], acc[hf][0:nt, :]), [acc[hf]], [T1])
            rms_gate_out(T1, 8, 128, nt, HG_NG, HG, "hg_wout")

        KS = HS

        KSV = HS[:].rearrange("p h v -> p (h v)").rearrange("p (a c) -> p a c", a=2)
        DQb, DQo = [QR, QR, SIGO], [0, 512, 0]
        DKb, DKo = [SIGO, KR, VF], [512, 0, 0]
        DVb, DVo = [HK, HK, HG], [0, 512, 0]

        def dil_sample():
            to_fm(X, nt)
            S.dma('sp', ROPE[0:nt, :], ropet[SEQ].partition_broadcast(nt), writes=[ROPE])

            def cons(si, c0, w, bank, ap):
                g, j = si // 3, si % 3
                if j == 0:
                    dst_buf[0] = DQb[g]
                    rope_apply(DQb[g][0:nt, DQo[g]:DQo[g] + 512].rearrange("p (h d) -> p h d", h=8), ap, 8, nt, bank)
                elif j == 1:
                    dst_buf[0] = DKb[g]
                    rope_apply(DKb[g][0:nt, DKo[g]:DKo[g] + 512].rearrange("p (h d) -> p h d", h=8), ap, 8, nt, bank)
                else:
                    act(lambda e: e.copy(DVb[g][0:nt, DVo[g]:DVo[g] + 512], ap), [bank], [DVb[g]])
            linear_tm(XT, 8, "dl_win", 4608, nt, cons)
            for g in range(3):
                L = DIL[g][0]
                S.dma('sp', okv_s[g][:, L - 1, 0].rearrange("s h d -> s (h d)"), DKb[g][0:nt, DKo[g]:DKo[g] + 512], reads=[DKb[g]])
                S.dma('sp', okv_s[g][:, L - 1, 1].rearrange("s h d -> s (h d)"), DVb[g][0:nt, DVo[g]:DVo[g] + 512], reads=[DVb[g]])
            acc_o, acc_d = [PB[6]], PB[7]
            n = 0
            for s in range(NS):
                for g in range(3):
                    L, dil = DIL[g]
                    QR_save = None
                    cache_attn_g(s, g, L, dil, acc_o, acc_d, n == 0, n == NS * 3 - 1)
                    n += 1
            dve(lambda e: e.tensor_copy(OALL[0:nt, 0:8, 0:64], acc_o[0][0:nt, 0:512].rearrange("p (h d) -> p h d", h=8)),
                [acc_o[0]], [OALL])
            dve(lambda e: e.tensor_copy(OALL[0:nt, 0:8, 64:65], acc_d[0:nt, 0:8].unsqueeze(2)), [acc_d], [OALL])
            for g in range(3):
                qv = DQb[g][0:nt, DQo[g]:DQo[g] + 512].rearrange("p (h d) -> p h d", h=8)
                kv = DKb[g][0:nt, DKo[g]:DKo[g] + 512].rearrange("p (h d) -> p h d", h=8)
                vv = DVb[g][0:nt, DVo[g]:DVo[g] + 512].rearrange("p (h d) -> p h d", h=8)
                t1v = T1[0:nt, 0:512].rearrange("p (h d) -> p h d", h=8)
                dve(lambda e: e.tensor_mul(t1v, qv, kv), [DQb[g], DKb[g]], [T1])
                dve(lambda e: e.tensor_reduce(sm[0:nt, 32:40], t1v, AX.X, ALU.add), [T1], [sm])
                act(lambda e: e.activation(sm[0:nt, 32:40], sm[0:nt, 32:40], AF.Exp, scale=0.125), [sm], [sm])
                dve(lambda e: e.tensor_mul(t1v, vv, sm[0:nt, 32:40].unsqueeze(2).broadcast_to([nt, 8, 64])), [DVb[g], sm], [T1])
                dve(lambda e: e.tensor_add(OALL[0:nt, 0:8, 0:64], OALL[0:nt, 0:8, 0:64], t1v), [OALL, T1], [OALL])
                dve(lambda e: e.tensor_add(OALL[0:nt, 0:8, 64:65], OALL[0:nt, 0:8, 64:65], sm[0:nt, 32:40].unsqueeze(2)),
                    [OALL, sm], [OALL])
            dve(lambda e: e.reciprocal(sm[0:nt, 48:56], OALL[0:nt, 0:8, 64:65].rearrange("p h o -> p (h o)")), [OALL], [sm])
            dve(lambda e: e.tensor_mul(YB[0:nt, 0:512].rearrange("p (h d) -> p h d", h=8), OALL[0:nt, 0:8, 0:64],
                                       sm[0:nt, 48:56].unsqueeze(2).broadcast_to([nt, 8, 64])), [OALL, sm], [YB])
            to_fm(YB, nt, KC=4, bf=True)
            linear_tm(XT, 4, "dl_wout", D, nt, resid_consumer(nt))

        def cache_attn_g(s, g, L, dil, acc_o, acc_d, first, last):
            S.dma('sp', KSV[:, :, :], kvin[g][s, 0:L:dil].rearrange("l a h d -> l a (h d)"), writes=[KS])
            qb = PB[4]
            dve(lambda e: e.tensor_scalar_mul(T4[0:nt, 512:1024], DQb[g][0:nt, DQo[g]:DQo[g] + 512], cs(C_ID)[0:nt, s:s + 1]), [DQb[g], CST], [T4])
            pe(lambda e: e.matmul(qb[:, 0:512], cs(C_ONES)[0:nt, :], T4[0:nt, 512:1024], start=True, stop=True), [CST, T4], [qb])
            dve(lambda e: e.tensor_mul(T1[:, 0:512], qb[:, 0:512], KSV[:, 0, :]), [qb, KS], [T1])
            dve(lambda e: e.tensor_reduce(sm[:, 0:8], T1[:, 0:512].rearrange("p (h d) -> p h d", h=8), AX.X, ALU.add), [T1], [sm])
            act(lambda e: e.activation(sm[:, 16:24], sm[:, 0:8], AF.Exp, scale=0.125), [sm], [sm])
            dve(lambda e: e.tensor_mul(T2[:, 0:512].rearrange("p (h d) -> p h d", h=8),
                                       KSV[:, 1, :].rearrange("p (h d) -> p h d", h=8),
                                       sm[:, 16:24].unsqueeze(2).broadcast_to([128, 8, 64])), [KS, sm], [T2])
            pe(lambda e: e.matmul(acc_o[0][0:nt, 0:512], EP(s), T2[:, 0:512], start=first, stop=last), [CST, T2], [acc_o[0]])
            pe(lambda e: e.matmul(acc_d[0:nt, 0:8], EP(s), sm[:, 16:24], start=first, stop=last), [CST, sm], [acc_d])

        def swa_sample():
            to_fm(X, nt)
            S.dma('sp', ROPE[0:nt, :], ropet[SEQ].partition_broadcast(nt), writes=[ROPE])

            def cons(si, c0, w, bank, ap):
                if si < 2:
                    dst_buf[0] = QR
                    rope_apply(QR[0:nt, si * 512:(si + 1) * 512].rearrange("p (h d) -> p h d", h=8), ap, 8, nt, bank)
                else:
                    dst_buf[0] = KR
                    rope_apply(KR[0:nt, 0:128].rearrange("p (h d) -> p h d", h=2), ap[:, 0:128], 2, nt, bank)
                    act(lambda e: e.copy(VF[0:nt, 0:128], ap[:, 128:256]), [bank], [VF])
            linear_tm(XT, 8, "sw_win", 1280, nt, cons)
            S.dma('sp', osw_s[:, 127, 0].rearrange("s h d -> s (h d)"), KR[0:nt, 0:128], reads=[KR])
            S.dma('sp', osw_s[:, 127, 1].rearrange("s h d -> s (h d)"), VF[0:nt, 0:128], reads=[VF])
            acc_o, acc_d = [PB[6], PB[5]], PB[7]
            for s in range(NS):
                S.dma('sp', KSV[:, :, 0:128], swain[s].rearrange("l a h d -> l a (h d)"), writes=[KS])
                for half in range(2):
                    qb = PB[4]
                    dve(lambda e: e.tensor_scalar_mul(T4[0:nt, 512:1024], QR[0:nt, half * 512:(half + 1) * 512], cs(C_ID)[0:nt, s:s + 1]), [QR, CST], [T4])
                    pe(lambda e: e.matmul(qb[:, 0:512], cs(C_ONES)[0:nt, :], T4[0:nt, 512:1024], start=True, stop=True), [CST, T4], [qb])
                    kb = KSV[:, 0, half * 64:(half + 1) * 64].unsqueeze(1).broadcast_to([128, 8, 64])
                    dve(lambda e: e.tensor_mul(T1[:, 0:512].rearrange("p (h d) -> p h d", h=8),
                                               qb[:, 0:512].rearrange("p (h d) -> p h d", h=8), kb), [qb, KS], [T1])
                    dve(lambda e: e.tensor_reduce(sm[:, 0:8], T1[:, 0:512].rearrange("p (h d) -> p h d", h=8), AX.X, ALU.add),
                        [T1], [sm])
                    act(lambda e: e.activation(sm[:, 16 + 8 * half:24 + 8 * half], sm[:, 0:8], AF.Exp, scale=0.125), [sm], [sm])
                    vb = KSV[:, 1, half * 64:(half + 1) * 64].unsqueeze(1).broadcast_to([128, 8, 64])
                    dve(lambda e: e.tensor_mul(T2[:, 0:512].rearrange("p (h d) -> p h d", h=8), vb,
                                               sm[:, 16 + 8 * half:24 + 8 * half].unsqueeze(2).broadcast_to([128, 8, 64])),
                        [KS, sm], [T2])
                    pe(lambda e: e.matmul(acc_o[half][0:nt, 0:512], EP(s), T2[:, 0:512], start=(s == 0), stop=(s == NS - 1)),
                       [CST, T2], [acc_o[half]])
                pe(lambda e: e.matmul(acc_d[0:nt, 0:16], EP(s), sm[:, 16:32], start=(s == 0), stop=(s == NS - 1)), [CST, sm], [acc_d])
            for half in range(2):
                dve(lambda e: e.tensor_copy(OALL[0:nt, 8 * half:8 * half + 8, 0:64],
                                            acc_o[half][0:nt, 0:512].rearrange("p (h d) -> p h d", h=8)), [acc_o[half]], [OALL])
            dve(lambda e: e.tensor_copy(OALL[0:nt, :, 64:65], acc_d[0:nt, 0:16].unsqueeze(2)), [acc_d], [OALL])
            qv = QR[0:nt, :].rearrange("p (k g d) -> p k g d", k=2, g=8)
            kv = KR[0:nt, 0:128].rearrange("p (k d) -> p k d", k=2).unsqueeze(2).broadcast_to([nt, 2, 8, 64])
            vv = VF[0:nt, 0:128].rearrange("p (k d) -> p k d", k=2).unsqueeze(2).broadcast_to([nt, 2, 8, 64])
            t1v = T1[0:nt, :].rearrange("p (k g d) -> p k g d", k=2, g=8)
            dve(lambda e: e.tensor_mul(t1v, qv, kv), [QR, KR], [T1])
            dve(lambda e: e.tensor_reduce(sm[0:nt, 32:48], T1[0:nt, :].rearrange("p (h d) -> p h d", h=16), AX.X, ALU.add), [T1], [sm])
            act(lambda e: e.activation(sm[0:nt, 32:48], sm[0:nt, 32:48], AF.Exp, scale=0.125), [sm], [sm])
            dve(lambda e: e.tensor_mul(t1v, vv, sm[0:nt, 32:48].rearrange("p (k g) -> p k g", k=2).unsqueeze(3).broadcast_to([nt, 2, 8, 64])),
                [VF, sm], [T1])
            dve(lambda e: e.tensor_add(OALL[0:nt, :, 0:64], OALL[0:nt, :, 0:64], T1[0:nt, :].rearrange("p (h d) -> p h d", h=16)),
                [OALL, T1], [OALL])
            dve(lambda e: e.tensor_add(OALL[0:nt, :, 64:65], OALL[0:nt, :, 64:65], sm[0:nt, 32:48].unsqueeze(2)), [OALL, sm], [OALL])
            swa_finish(nt)

        issue_shift(len(shift_jobs))
        X.set(XS[0])
        S.dma('sp', X[0:nt, :], xs, writes=[X])
        smix = [mlstm_sample, dil_sample, hgrn_sample, swa_sample]
        for layer in range(4):
            lin['i'] = 0
            smix[layer]()
            layer_norm(nt, 0, layer)
            mlp(1, nt, layer)
        S.dma('sp', ys, X[0:nt, :], reads=[X])
        S.finish()
    return nc


_cache = {}


def kernel(**inp):
    SEQ = inp["x_prompt"].shape[1]
    NB = inp["x_sample"].shape[0]
    NS = NB // 8
    key = (SEQ, NS)
    if key not in _cache:
        _cache[key] = build(SEQ, NS)
    nc = _cache[key]
    f = lambda a: np.ascontiguousarray(np.asarray(a, dtype=np.float32))
    cst, cstm = host_consts()
    rope = host_rope(SEQ)
    shared = dict(
        ml_win=f(inp["mlstm_w_in"][0]), ml_bg=f(inp["mlstm_b_gates"][0]), ml_ng=f(inp["mlstm_norm_g"][0]), ml_wout=f(inp["mlstm_w_out"][0]),
        dl_win=f(inp["dil_w_in"][0]), dl_wout=f(inp["dil_w_out"][0]),
        hg_win=f(inp["hgrn_w_in"][0]), hg_bf=f(inp["hgrn_b_f"][0]), hg_lb=f(inp["hgrn_lb_logits"]), hg_ng=f(inp["hgrn_norm_g"][0]),
        hg_wout=f(inp["hgrn_w_out"][0]),
        sw_win=f(inp["swa_w_in"][0]), sw_sink=f(inp["swa_sinks"][0]), sw_wout=f(inp["swa_w_out"][0]),
        ln1_g=f(inp["ln1_g"]), ln1_b=f(inp["ln1_b"]), ln2_g=f(inp["ln2_g"]), ln2_b=f(inp["ln2_b"]),
        w1=f(inp["mlp_w1"]), w2=f(inp["mlp_w2"]), cst=cst, cstm=cstm, rope=rope)
    in_maps = []
    for c in range(8):
        sl = slice(c * NS, (c + 1) * NS)
        m = dict(shared)
        m.update(
            xp=f(inp["x_prompt"][c % 2]), xs=f(inp["x_sample"][sl, 0]),
            st_C=f(inp["state_mlstm_C"][0, sl]), st_n=f(inp["state_mlstm_n"][0, sl]), st_m=f(inp["state_mlstm_m"][0, sl]),
            kv0=f(inp["cache_dil_kv0"][0, sl]), kv1=f(inp["cache_dil_kv1"][0, sl]), kv2=f(inp["cache_dil_kv2"][0, sl]),
            st_S=f(inp["state_hgrn_S"][0, sl]), swakv=f(inp["cache_swa_kv"][0, sl]))
        in_maps.append(m)
    res = run_bass_kernel_spmd(nc, in_maps, core_ids=list(range(8))).results
    cat = lambda k: np.concatenate([res[c][k] for c in range(8)], axis=0)
    two = lambda k: np.stack([res[0][k], res[1][k]], axis=0)
    outs = (
        two("yp"), cat("ys")[:, None, :],
        two("oC_p")[None], cat("oC_s")[None], two("on_p")[None], cat("on_s")[None],
        np.stack([res[0]["om_p"][0], res[1]["om_p"][0]])[None], cat("om_s")[None],
        two("okv0_p")[None], cat("okv0_s")[None], two("okv1_p")[None], cat("okv1_s")[None],
        two("okv2_p")[None], cat("okv2_s")[None],
        two("oS_p")[None], cat("oS_s")[None], two("osw_p")[None], cat("osw_s")[None])
    return tuple(np.ascontiguousarray(o.astype(np.float32)) for o in outs)
```

```python
import contextlib
import numpy as np
import concourse.bass as bass
import concourse.mybir as mybir
from concourse.bass_utils import run_bass_kernel_spmd

F32 = mybir.dt.float32
BF16 = mybir.dt.bfloat16
AF = mybir.ActivationFunctionType
ALU = mybir.AluOpType
AX = mybir.AxisListType

D = 1024
ALPHA = 8.0 ** 0.25
LN_EPS = 1e-5
NORM_EPS = 1e-6
PAST_LEN = 8192
DIL = ((128, 1), (512, 4), (2048, 16))

C_ID, C_U, C_BU, C_BO, C_ONES = [i * 128 for i in range(5)]
C_CI = 5 * 128
C_EP = C_CI + 4
CW = C_EP + 256
M_ID, M_U, M_L, M_U4, M_R4, M_L4, M_U16, M_R16, M_L16 = [i * 128 for i in range(9)]
MW = 9 * 128


def host_consts():
    c = np.zeros((128, CW), np.float32)
    m = np.zeros((128, MW), np.float32)
    s = np.arange(128)[:, None]
    t = np.arange(128)[None, :]
    U = (s <= t)
    L = (s >= t)
    c[:, C_ID:C_ID + 128] = (s == t)
    c[:, C_U:C_U + 128] = U
    same = (s // 32 == t // 32)
    c[:, C_BU:C_BU + 128] = same & U
    c[:, C_BO:C_BO + 128] = same
    c[:, C_ONES:C_ONES + 128] = 1.0
    for k in range(4):
        c[32 * k:32 * k + 32, C_CI + k] = 1.0
    for k in range(16):
        c[:, C_EP + k * 16 + k] = 1.0
    m[:, M_ID:M_ID + 128] = (s == t)
    m[:, M_U:M_U + 128] = U
    m[:, M_L:M_L + 128] = L
    for dil, cu, cr, cl in ((4, M_U4, M_R4, M_L4), (16, M_U16, M_R16, M_L16)):
        R = ((t - s) % dil == 0)
        m[:, cu:cu + 128] = U & R
        m[:, cr:cr + 128] = R
        m[:, cl:cl + 128] = L & R
    return c, m


def host_rope(seq):
    pos = np.concatenate([np.arange(seq), [PAST_LEN]]).astype(np.float32)
    inv = np.power(np.float32(10000.0), -np.arange(32, dtype=np.float32) / np.float32(32)).astype(np.float32)
    ang = (pos[:, None] * inv[None, :]).astype(np.float32)
    cs, sn = np.cos(ang).astype(np.float32), np.sin(ang).astype(np.float32)
    return np.concatenate([cs, cs, -sn, sn], axis=1).astype(np.float32)


class Buf:
    def __init__(self, name, h):
        self.name = name
        self.h = h
        self.w = {}
        self.r = {}

    def __getitem__(self, idx):
        return self.h[idx]


class Cur:
    def __init__(self):
        object.__setattr__(self, 'b', None)

    def set(self, b):
        object.__setattr__(self, 'b', b)

    def __getitem__(self, i):
        return self.b[i]

    def __getattr__(self, n):
        return getattr(self.b, n)

    def __setattr__(self, n, v):
        setattr(self.b, n, v)


class Sched:
    NDMA = 40
    NSW = 6
    SAME_ENGINE_SYNC = True

    def __init__(self, nc):
        self.nc = nc
        self.eng = {'pe': nc.tensor, 'act': nc.scalar, 'dve': nc.vector, 'pool': nc.gpsimd, 'sp': nc.sync}
        self.sems, self.count = {}, {}
        self.known = {e: {} for e in self.eng}
        self.dma_rr = 0
        self.n_own = 0
        self.sw_rr = 0
        self.n_inst = 0

    def __enter__(self):
        self.stack = contextlib.ExitStack()
        self.stack.__enter__()
        for e in self.eng:
            self.sems[e] = self.stack.enter_context(self.nc.semaphore("prog_" + e))
            self.count[e] = 0
        for i in range(self.NDMA):
            k = "dma%d" % i
            self.sems[k] = self.stack.enter_context(self.nc.semaphore(k))
            self.count[k] = 0
        for i in range(self.NSW):
            k = "dmasw%d" % i
            self.sems[k] = self.stack.enter_context(self.nc.semaphore(k))
            self.count[k] = 0
        self.block = self.stack.enter_context(self.nc.Block())
        return self

    def __exit__(self, *a):
        return self.stack.__exit__(*a)

    def sb(self, name, shape, dtype):
        return Buf(name, self.stack.enter_context(self.nc.sbuf_tensor("sb_" + name, list(shape), dtype)))

    def ps(self, name, shape, dtype):
        return Buf(name, self.stack.enter_context(self.nc.psum_tensor("ps_" + name, list(shape), dtype)))

    def _wait(self, e, key, val):
        if key == e and (e == 'pe' or not self.SAME_ENGINE_SYNC):
            return
        if self.known[e].get(key, 0) >= val:
            return
        self.eng[e].wait_ge(self.sems[key], val)
        self.known[e][key] = val

    def _deps(self, e, reads, writes):
        for b in reads:
            for k, v in b.w.items():
                self._wait(e, k, v)
        for b in writes:
            for k, v in b.w.items():
                self._wait(e, k, v)
            for k, v in b.r.items():
                self._wait(e, k, v)

    def _record(self, key, val, reads, writes, acc_w=False):
        for b in reads:
            if b.r.get(key, 0) < val:
                b.r[key] = val
        for b in writes:
            if acc_w:
                b.w[key] = val
            else:
                b.w = {key: val}
                b.r = {}

    def op(self, e, fn, reads=(), writes=()):
        self._deps(e, reads, writes)
        ins = fn(self.eng[e])
        self.count[e] += 1
        ins.then_inc(self.sems[e], 1)
        self._record(e, self.count[e], reads, writes)
        self.n_inst += 1

    def dve(self, fn, reads=(), writes=()):
        self.op('dve', fn, reads, writes)

    def act(self, fn, reads=(), writes=()):
        self.op('act', fn, reads, writes)

    def pe(self, fn, reads=(), writes=()):
        self.op('pe', fn, reads, writes)

    def dma(self, q, out, in_, reads=(), writes=(), acc_w=False, own_sem=False, **kw):
        if own_sem:
            k = "dmax%d" % self.n_own
            self.n_own += 1
            self.sems[k] = self.stack.enter_context(self.nc.semaphore(k))
            self.count[k] = 0
        elif q == 'pool':
            k = "dmasw%d" % self.sw_rr
            self.sw_rr = (self.sw_rr + 1) % self.NSW
        else:
            k = "dma%d" % self.dma_rr
            self.dma_rr = (self.dma_rr + 1) % self.NDMA
        if self.count[k] > 0:
            self._wait(q, k, self.count[k])
        self._deps(q, reads, writes)
        ins = self.eng[q].dma_start(out=out, in_=in_, **kw)
        self.count[k] += 16
        ins.then_inc(self.sems[k], 16)
        self._record(k, self.count[k], reads, writes, acc_w)
        self.n_inst += 1

    def finish(self):
        for k in list(self.sems):
            if k.startswith("dma") and self.count[k] > 0:
                self._wait('sp', k, self.count[k])
        for e in ('pe', 'act', 'dve', 'pool'):
            if self.count[e] > 0:
                self.eng['sp'].wait_ge(self.sems[e], self.count[e])


def build(SEQ, NS):
    NT = SEQ // 128
    nc = bass.Bass("TRN2", target_bir_lowering=False)

    def din(name, shape):
        return nc.dram_tensor(name, list(shape), F32, kind="ExternalInput").ap()

    def dout(name, shape):
        return nc.dram_tensor(name, list(shape), F32, kind="ExternalOutput").ap()

    xp = din("xp", [SEQ, D]); xs = din("xs", [NS, D])
    st_C = din("st_C", [NS, 4, 128, 256]); st_n = din("st_n", [NS, 4, 128]); st_m = din("st_m", [NS, 4])
    kvin = [din("kv%d" % g, [NS, DIL[g][0], 2, 8, 64]) for g in range(3)]
    st_S = din("st_S", [NS, 8, 128, 128]); swain = din("swakv", [NS, 128, 2, 2, 64])
    ml_win = din("ml_win", [D, 3080]); ml_bg = din("ml_bg", [8]); ml_ng = din("ml_ng", [D]); ml_wout = din("ml_wout", [D, D])
    dl_win = din("dl_win", [D, 4608]); dl_wout = din("dl_wout", [512, D])
    hg_win = din("hg_win", [D, 4096]); hg_bf = din("hg_bf", [D]); hg_lb = din("hg_lb", [4, D]); hg_ng = din("hg_ng", [D]); hg_wout = din("hg_wout", [D, D])
    sw_win = din("sw_win", [D, 1280]); sw_sink = din("sw_sink", [16]); sw_wout = din("sw_wout", [D, D])
    ln_g = [din("ln1_g", [4, D]), din("ln2_g", [4, D])]
    ln_b = [din("ln1_b", [4, D]), din("ln2_b", [4, D])]
    w1 = din("w1", [4, D, 4096]); w2 = din("w2", [4, 4096, D])
    cst = din("cst", [128, CW]); cstm = din("cstm", [128, MW]); ropet = din("rope", [SEQ + 1, 128])

    def dscr(name, shape):
        return nc.dram_tensor(name, list(shape), BF16).ap()

    wb = {}
    for nm, ap_ in (("ml_win", ml_win), ("ml_wout", ml_wout), ("dl_win", dl_win), ("dl_wout", dl_wout), ("hg_win", hg_win),
                    ("hg_wout", hg_wout), ("sw_win", sw_win), ("sw_wout", sw_wout)):
        wb[nm] = ap_
    for l in range(4):
        wb["w1_%d" % l] = w1[l]; wb["w2_%d" % l] = w2[l]
    unit_ix = {}
    for nm, Wf in wb.items():
        K_, N_ = Wf.shape
        for c0 in range(0, N_, 512):
            for k0 in range(0, K_, 1024):
                unit_ix[(nm, k0, c0)] = len(unit_ix)
    wscr = dscr("wscr", [len(unit_ix), 128, 4096])
    yp = dout("yp", [SEQ, D]); ys = dout("ys", [NS, D])
    oC_p = dout("oC_p", [4, 128, 256]); oC_s = dout("oC_s", [NS, 4, 128, 256])
    on_p = dout("on_p", [4, 128]); on_s = dout("on_s", [NS, 4, 128])
    om_p = dout("om_p", [1, 4]); om_s = dout("om_s", [NS, 4])
    okv_p = [dout("okv%d_p" % g, [min(DIL[g][0], SEQ), 2, 8, 64]) for g in range(3)]
    okv_s = [dout("okv%d_s" % g, [NS, DIL[g][0], 2, 8, 64]) for g in range(3)]
    oS_p = dout("oS_p", [8, 128, 128]); oS_s = dout("oS_s", [NS, 8, 128, 128])
    osw_p = dout("osw_p", [128, 2, 2, 64]); osw_s = dout("osw_s", [NS, 128, 2, 2, 64])

    S = Sched(nc)
    with S:
        sb, dve, act, pe = S.sb, S.dve, S.act, S.pe
        PB = [S.ps("pb%d" % i, [128, 512], F32) for i in range(8)]
        CST = sb("cst", [128, CW], F32)
        CB = sb("cstb", [128, MW], BF16)
        NSUB = 4 if NT % 4 == 0 else (2 if NT % 2 == 0 else 1)
        XS = [sb("X%d" % i, [128, D], F32) for i in range(NSUB)]
        X = Cur()
        X.set(XS[0])
        XT = sb("XT", [128, 8, 128], BF16)
        HT = sb("HT", [128, 8, 128], BF16)
        NW = 3
        WR = [sb("wr%d" % i, [128, 8, 512], BF16) for i in range(NW)]
        T1 = sb("T1", [128, D], F32); T2 = sb("T2", [128, D], F32); T3 = sb("T3", [128, D], F32)
        T4 = sb("T4", [128, D], F32)
        HB = sb("HB", [128, D], BF16)
        ROPE = sb("ROPE", [128, 128], F32)
        sm = sb("sm", [128, 64], F32)
        ML_NG = ml_ng; HG_NG = hg_ng; BFB = T2
        OML = sb("OML", [128, D], F32)
        LBB = None
        GW = sb("GW", [128, 8, 8], BF16)
        MLBG = sb("MLBG", [128, 8], F32); ESINK = sb("ESINK", [128, 16], F32)
        wstate = {'i': 0}

        def cs(c0, n=128, p=128):
            return CST[0:p, c0:c0 + n]

        def cb(c0, n=128):
            return CB[:, c0:c0 + n]

        S.dma('sp', CST[:], cst, writes=[CST])
        S.dma('pool', CB[:], cstm, writes=[CB])
        S.dma('sp', MLBG[:], ml_bg.partition_broadcast(128), writes=[MLBG])
        S.dma('pool', GW[:], ml_win[:, 3072:3080].rearrange("(k p) n -> p k n", p=128), writes=[GW])
        S.dma('sp', ESINK[:], sw_sink.partition_broadcast(128), writes=[ESINK])
        act(lambda e: e.activation(ESINK[:], ESINK[:], AF.Exp), [ESINK], [ESINK])
        TMX = XS[0]
        for i, T in enumerate((T1, T2, T3, T4)):
            S.dma('sp', T[:], hg_lb[i].partition_broadcast(128), writes=[T])
        dve(lambda e: e.tensor_max(TMX[:], T1[:], T2[:]), [T1, T2], [TMX])
        dve(lambda e: e.tensor_max(OML[:], T3[:], T4[:]), [T3, T4], [OML])
        dve(lambda e: e.tensor_max(TMX[:], TMX[:], OML[:]), [TMX, OML], [TMX])
        for T in (T1, T2, T3, T4):
            dve(lambda e: e.tensor_sub(T[:], T[:], TMX[:]), [T, TMX], [T])
            act(lambda e: e.activation(T[:], T[:], AF.Exp), [T], [T])
        dve(lambda e: e.tensor_add(TMX[:], T2[:], T3[:]), [T2, T3], [TMX])
        dve(lambda e: e.tensor_add(OML[:], T1[:], T4[:]), [T1, T4], [OML])
        dve(lambda e: e.tensor_add(T1[:], TMX[:], OML[:]), [TMX, OML], [T1])
        dve(lambda e: e.reciprocal(T1[:], T1[:]), [T1], [T1])
        dve(lambda e: e.tensor_mul(OML[:], OML[:], T1[:]), [OML, T1], [OML])

        shift_jobs = []
        for src_, dst_, L_ in ((kvin[2], okv_s[2], 2048), (kvin[1], okv_s[1], 512), (kvin[0], okv_s[0], 128), (swain, osw_s, 128)):
            for s_ in range(NS):
                a_ = src_[s_, 1:L_].rearrange("l a h d -> (l a h d)").rearrange("(o i) -> o i", o=128)
                b_ = dst_[s_, 0:L_ - 1].rearrange("l a h d -> (l a h d)").rearrange("(o i) -> o i", o=128)
                shift_jobs.append((b_, a_))

        def issue_shift(n):
            for _ in range(n):
                if shift_jobs:
                    b_, a_ = shift_jobs.pop(0)
                    S.dma('sp', b_, a_)

        WBUF = Buf("wscratch", None)

        def precast():
            for nm, Wf in wb.items():
                K_, N_ = Wf.shape
                for c0 in range(0, N_, 512):
                    w = min(512, N_ - c0)
                    for k0 in range(0, K_, 1024):
                        kc = min(8, (K_ - k0) // 128)
                        wu = WR[wstate['i'] % NW]
                        wstate['i'] += 1
                        S.dma('pool', wu[:, 0:kc, 0:w], Wf[k0:k0 + kc * 128, c0:c0 + w].rearrange("(k p) n -> p k n", p=128),
                              writes=[wu])
                        S.dma('sp', wscr[unit_ix[(nm, k0, c0)]][:, 0:kc * w].rearrange("p (k n) -> p k n", k=kc), wu[:, 0:kc, 0:w],
                              reads=[wu], writes=[WBUF], acc_w=True)

        def load_unit(nm, k0, kc, c0, w):
            wu = WR[wstate['i'] % NW]
            wstate['i'] += 1
            src = wscr[unit_ix[(nm, k0, c0)]][:, 0:kc * w].rearrange("p (k n) -> p k n", k=kc)
            S.dma('pool', wu[:, 0:kc, 0:w], src, reads=[WBUF], writes=[wu])
            return wu

        precast()

        def to_fm(src, nt, KC=8, dst=None, bf=False, dsts=None, c0=0):
            dst = dst or XT
            for half in range((KC + 3) // 4):
                n = min(4, KC - 4 * half)
                bank = PB[2 + half % 2]
                if bf:
                    pv = bank[:].bitcast(BF16).rearrange("p (a b) -> p a b", a=8)
                    idn = cb(M_ID)[0:nt, 0:nt]
                else:
                    pv = bank[:].rearrange("p (a b) -> p a b", a=4)
                    idn = cs(C_ID)[0:nt, 0:nt]
                for j in range(n):
                    kc = 4 * half + j
                    pe(lambda e: e.transpose(pv[:, j, 0:nt], src[0:nt, kc * 128:(kc + 1) * 128], idn),
                       [src, CB if bf else CST], [bank])
                if dsts is not None:
                    dbuf = dsts[half][0]
                    o = dsts[half][1][:, 0:n, c0:c0 + nt]
                else:
                    dbuf = dst
                    o = dst[:, 4 * half:4 * half + n, 0:nt]
                if half % 2 == 0:
                    act(lambda e: e.copy(o, pv[:, 0:n, 0:nt]), [bank], [dbuf])
                else:
                    dve(lambda e: e.tensor_copy(o, pv[:, 0:n, 0:nt]), [bank], [dbuf])

        lin = {'i': 0, 'j': 0}

        def linear_tm(xT, KC, Wd, N, nt, consumer):
            for si, c0 in enumerate(range(0, N, 512)):
                w = min(512, N - c0)
                bank = PB[lin['i'] % 2]
                lin['i'] += 1
                wu = load_unit(Wd, 0, KC, c0, w)
                for kc in range(KC):
                    pe(lambda e: e.matmul(bank[0:nt, 0:w], xT[:, kc, 0:nt], wu[:, kc, 0:w],
                                          start=(kc == 0), stop=(kc == KC - 1)), [xT, wu], [bank])
                consumer(si, c0, w, bank, bank[0:nt, 0:w])

        def resid_consumer(nt):
            def f(si, c0, w, bank, ap):
                dve(lambda e: e.scalar_tensor_tensor(X[0:nt, c0:c0 + w], X[0:nt, c0:c0 + w], ALPHA, ap,
                                                     ALU.mult, ALU.add), [X, bank], [X])
            return f

        def layer_norm(nt, which, layer, load=True):
            G, B = T3, T4
            if load:
                S.dma('sp', G[:], ln_g[which][layer].partition_broadcast(128), writes=[G])
                S.dma('sp', B[:], ln_b[which][layer].partition_broadcast(128), writes=[B])
            st = sm[0:nt, 0:12].rearrange("p (a b) -> p a b", a=2)
            for hf in range(2):
                dve(lambda e: e.bn_stats(st[:, hf, :], X[0:nt, hf * 512:(hf + 1) * 512]), [X], [sm])
            dve(lambda e: e.bn_aggr(sm[0:nt, 12:14], st), [sm], [sm])
            dve(lambda e: e.tensor_scalar_add(sm[0:nt, 14:15], sm[0:nt, 13:14], LN_EPS), [sm], [sm])
            act(lambda e: e.activation(sm[0:nt, 14:15], sm[0:nt, 14:15], AF.Sqrt), [sm], [sm])
            dve(lambda e: e.reciprocal(sm[0:nt, 15:16], sm[0:nt, 14:15]), [sm], [sm])
            dve(lambda e: e.scalar_tensor_tensor(X[0:nt, :], X[0:nt, :], sm[0:nt, 12:13], G[0:nt, :],
                                                 ALU.subtract, ALU.mult), [X, sm, G], [X])
            dve(lambda e: e.scalar_tensor_tensor(X[0:nt, :], X[0:nt, :], sm[0:nt, 15:16], B[0:nt, :],
                                                 ALU.mult, ALU.add), [X, sm, B], [X])

        def mlp(nsub, nt, layer):
            NTOK = (nsub - 1) * 128 + nt
            xtb = [(QR, QR[:].bitcast(BF16).rearrange("p (k t) -> p k t", k=4)),
                   (SIGO, SIGO[:].bitcast(BF16).rearrange("p (k t) -> p k t", k=4))]
            htb = [(HK, HK[:].bitcast(BF16).rearrange("p (k t) -> p k t", k=4)),
                   (HG, HG[:].bitcast(BF16).rearrange("p (k t) -> p k t", k=4))]
            for st in range(nsub):
                X.set(XS[st])
                to_fm(X, nt, dsts=xtb, c0=st * 128)
                act(lambda e: e.mul(X[0:nt, :], X[0:nt, :], ALPHA), [X], [X])
            for grp in range(4):
                for u in range(2):
                    wu = load_unit("w1_%d" % layer, 0, 8, grp * 1024 + u * 512, 512)
                    for j in range(4):
                        bank = PB[4 + (lin['i'] % 4)]
                        lin['i'] += 1
                        for kc in range(8):
                            pe(lambda e: e.matmul(bank[:, 0:NTOK], wu[:, kc, j * 128:(j + 1) * 128], xtb[kc // 4][1][:, kc % 4, 0:NTOK],
                                                  start=(kc == 0), stop=(kc == 7)), [wu, xtb[kc // 4][0]], [bank])
                        tv = T1 if j % 2 == 0 else T2
                        act(lambda e: e.activation(tv[:, 0:NTOK], bank[:, 0:NTOK], AF.Relu), [bank], [tv])
                        dve(lambda e: e.tensor_mul(htb[u][1][:, j, 0:NTOK], tv[:, 0:NTOK], tv[:, 0:NTOK]), [tv], [htb[u][0]])
                for sl in range(2):
                    wu = load_unit("w2_%d" % layer, grp * 1024, 8, sl * 512, 512)
                    for st in range(nsub):
                        X.set(XS[st])
                        bank = PB[lin['j'] % 4]
                        lin['j'] += 1
                        for kc in range(8):
                            pe(lambda e: e.matmul(bank[0:nt, :], htb[kc // 4][1][:, kc % 4, st * 128:st * 128 + nt], wu[:, kc, :],
                                                  start=(kc == 0), stop=(kc == 7)), [htb[kc // 4][0], wu], [bank])
                        dve(lambda e: e.tensor_add(X[0:nt, sl * 512:(sl + 1) * 512], X[0:nt, sl * 512:(sl + 1) * 512], bank[0:nt, :]),
                            [X, bank], [X])
            for st in range(nsub):
                X.set(XS[st])
                layer_norm(nt, 1, layer, load=(st == 0))

        def rms_gate_out(src, nh, hd, nt, NGB, gate, wout):
            sv = src[0:nt, :].rearrange("p (h d) -> p h d", h=nh)
            t3 = T3[0:nt, :].rearrange("p (h d) -> p h d", h=nh)
            dve(lambda e: e.tensor_mul(T3[0:nt, :], src[0:nt, :], src[0:nt, :]), [src], [T3])
            dve(lambda e: e.tensor_reduce(sm[0:nt, 16:16 + nh], t3, AX.X, ALU.add), [T3], [sm])
            dve(lambda e: e.tensor_scalar(sm[0:nt, 16:16 + nh], sm[0:nt, 16:16 + nh], 1.0 / hd, NORM_EPS,
                                          ALU.mult, ALU.add), [sm], [sm])
            act(lambda e: e.activation(sm[0:nt, 16:16 + nh], sm[0:nt, 16:16 + nh], AF.Sqrt), [sm], [sm])
            dve(lambda e: e.reciprocal(sm[0:nt, 32:32 + nh], sm[0:nt, 16:16 + nh]), [sm], [sm])
            dve(lambda e: e.tensor_mul(t3, sv, sm[0:nt, 32:32 + nh].unsqueeze(2).broadcast_to([nt, nh, hd])),
                [src, sm], [T3])
            S.dma('sp', T2[:], NGB.partition_broadcast(128), writes=[T2])
            dve(lambda e: e.tensor_mul(T3[0:nt, :], T3[0:nt, :], T2[0:nt, :]), [T3, T2], [T3])
            dve(lambda e: e.tensor_mul(HB[0:nt, :], T3[0:nt, :], gate[0:nt, :]), [T3, gate], [HB])
            to_fm(HB, nt, bf=True)
            linear_tm(XT, 8, wout, D, nt, resid_consumer(nt))

        def load_rope(row0, nt):
            S.dma('sp', ROPE[0:nt, :], ropet[row0:row0 + nt, :] if nt > 1 else ropet[row0:row0 + 1, :],
                  writes=[ROPE])

        def rope_apply(dst, ap, nh, nt, bank):
            xv = ap.rearrange("p (h d) -> p h d", h=nh)
            xh = ap.rearrange("p (h a d) -> p h a d", h=nh, a=2)
            t4 = T4[0:nt, 0:nh * 64].rearrange("p (h a d) -> p h a d", h=nh, a=2)
            t4f = T4[0:nt, 0:nh * 64].rearrange("p (h d) -> p h d", h=nh)
            cc = ROPE[0:nt, 0:64].unsqueeze(1).broadcast_to([nt, nh, 64])
            s1 = ROPE[0:nt, 64:96].unsqueeze(1).broadcast_to([nt, nh, 32])
            s2 = ROPE[0:nt, 96:128].unsqueeze(1).broadcast_to([nt, nh, 32])
            dve(lambda e: e.tensor_mul(t4[:, :, 0, :], xh[:, :, 1, :], s1), [bank, ROPE], [T4])
            dve(lambda e: e.tensor_mul(t4[:, :, 1, :], xh[:, :, 0, :], s2), [bank, ROPE], [T4])
            dve(lambda e: e.tensor_mul(dst, xv, cc), [bank, ROPE], [dst_buf[0]])
            dve(lambda e: e.tensor_add(dst, dst, t4f), [T4, dst_buf[0]], [dst_buf[0]])

        dst_buf = [None]

        C_AUG = sb("C_AUG", [128, 4, 257], F32); C_BF = sb("C_BF", [128, 4, 257], BF16)
        M_BC = sb("M_BC", [128, 4], F32)
        K_BF = sb("K_BF", [128, 512], BF16)
        V_AUG = sb("V_AUG", [128, 4, 257], BF16)
        SIGO = sb("SIGO", [128, D], F32)
        GATES = sb("GATES", [128, 8], F32)
        mls = sb("mls", [128, 64], F32)
        UT = sb("UT", [4, 128], F32); MROWS = sb("MROWS", [4, 128], F32); MR4 = sb("MR4", [4, 4, 128], F32)
        WT = sb("WT", [128, 4, 128], F32); EE = sb("EE", [128, 4, 128], F32)
        dve(lambda e: e.memset(V_AUG[:], 1.0), [], [V_AUG])
        dve(lambda e: e.memset(C_AUG[:], 0.0), [], [C_AUG])
        dve(lambda e: e.memset(M_BC[:], 0.0), [], [M_BC])

        def mlstm_proj(nt):
            def cons(si, c0, w, bank, ap):
                if si == 0:
                    act(lambda e: e.copy(Q_TM[0:nt, :], ap), [bank], [Q_TM])
                elif si == 1:
                    act(lambda e: e.mul(K_TM[0:nt, :], ap, 128.0 ** -0.5), [bank], [K_TM])
                    dve(lambda e: e.tensor_copy(K_BF[0:nt, :], K_TM[0:nt, :]), [K_TM], [K_BF])
                elif si in (2, 3):
                    hh = 2 * (si - 2)
                    act(lambda e: e.copy(V_AUG[0:nt, hh:hh + 2, 0:256], ap.rearrange("p (h d) -> p h d", h=2)),
                        [bank], [V_AUG])
                elif si in (4, 5):
                    cc = (si - 4) * 512
                    act(lambda e: e.activation(SIGO[0:nt, cc:cc + 512], ap, AF.Sigmoid), [bank], [SIGO])
                else:
                    dve(lambda e: e.tensor_add(GATES[0:nt, :], ap, MLBG[0:nt, :]), [bank, MLBG], [GATES])
            linear_tm(XT, 8, "ml_win", 3072, nt, cons)
            gbank = PB[lin['i'] % 2]
            lin['i'] += 1
            for kc in range(8):
                pe(lambda e: e.matmul(gbank[0:nt, 0:8], XT[:, kc, 0:nt], GW[:, kc, :], start=(kc == 0), stop=(kc == 7)),
                   [XT, GW], [gbank])
            cons(6, 3072, 8, gbank, gbank[0:nt, 0:8])
            act(lambda e: e.activation(mls[0:nt, 0:4], GATES[0:nt, 4:8], AF.Exp, scale=-1.0), [GATES], [mls])
            dve(lambda e: e.tensor_scalar_add(mls[0:nt, 0:4], mls[0:nt, 0:4], 1.0), [mls], [mls])
            act(lambda e: e.activation(mls[0:nt, 0:4], mls[0:nt, 0:4], AF.Ln), [mls], [mls])
            dve(lambda e: e.tensor_scalar_mul(mls[0:nt, 0:4], mls[0:nt, 0:4], -1.0), [mls], [mls])

        def mlstm_prompt(t):
            nt = 128
            to_fm(X, nt)
            mlstm_proj(nt)
            for src, dstT in ((Q_TM, QT), (K_BF, KT)):
                bank = PB[4]
                pv = bank[:].bitcast(BF16).rearrange("p (a b) -> p a b", a=8)
                for h in range(4):
                    pe(lambda e: e.transpose(pv[:, h, :], src[:, h * 128:(h + 1) * 128], cb(M_ID)), [src, CB], [bank])
                dve(lambda e: e.tensor_copy(dstT[:], pv[:, 0:4, :]), [bank], [dstT])
            b5 = PB[5]
            pe(lambda e: e.matmul(b5[:, 0:4], cs(C_U), mls[:, 0:4], start=True, stop=True), [CST, mls], [b5])
            pe(lambda e: e.matmul(b5[:, 8:12], cs(C_ONES), mls[:, 0:4], start=True, stop=True), [CST, mls], [b5])
            dve(lambda e: e.tensor_copy(mls[:, 4:8], b5[:, 0:4]), [b5], [mls])
            dve(lambda e: e.tensor_copy(mls[:, 8:12], b5[:, 8:12]), [b5], [mls])
            dve(lambda e: e.tensor_sub(mls[:, 12:16], GATES[:, 0:4], mls[:, 4:8]), [GATES, mls], [mls])
            pe(lambda e: e.transpose(b5[0:4, 128:256], mls[:, 12:16], cs(C_ID)), [mls, CST], [b5])
            act(lambda e: e.copy(UT[:], b5[0:4, 128:256]), [b5], [UT])
            dve(lambda e: e.tensor_mul(mls[0:4, 16:20], M_BC[0:4, :], cs(C_ID)[0:4, 0:4]), [M_BC, CST], [mls])
            dve(lambda e: e.tensor_reduce(mls[0:4, 20:21], mls[0:4, 16:20], AX.X, ALU.add), [mls], [mls])
            dve(lambda e: e.tensor_tensor_scan(MROWS[:], UT[:], UT[:], mls[0:4, 20:21], ALU.max, ALU.max),
                [UT, mls], [MROWS])
            mb = PB[6]
            mbv = mb[:].rearrange("p (h t) -> p h t", h=4)
            dve(lambda e: e.tensor_mul(MR4[:], MROWS[:].unsqueeze(1).broadcast_to([4, 4, 128]),
                                       cs(C_ID)[0:4, 0:4].unsqueeze(2).broadcast_to([4, 4, 128])), [MROWS, CST], [MR4])
            for h in range(4):
                pe(lambda e: e.matmul(mbv[:, h, :], cs(C_ONES)[0:4, :], MR4[:, h, :],
                                      start=True, stop=True), [CST, MR4], [mb])
            pe(lambda e: e.transpose(b5[:, 16:20], MROWS[:], cs(C_ID)[0:4, 0:4]), [MROWS, CST], [b5])
            dve(lambda e: e.tensor_add(mls[:, 24:28], mls[:, 4:8], b5[:, 16:20]), [mls, b5], [mls])
            act(lambda e: e.activation(mls[:, 28:32], mls[:, 24:28], AF.Exp, scale=-1.0), [mls], [mls])
            for h in range(4):
                act(lambda e: e.activation(WT[:, h, :], mbv[:, h, :], AF.Exp, scale=-1.0, bias=mls[:, 12 + h:13 + h]),
                    [mb, mls], [WT])
                act(lambda e: e.activation(EE[:, h, :], mbv[:, h, :], AF.Exp, scale=-1.0, bias=M_BC[:, h:h + 1]),
                    [mb, M_BC], [EE])
            dve(lambda e: e.tensor_mul(KH[:], K_TM[:].rearrange("p (h d) -> p h d", h=4),
                                       WT[:, :, 127:128].broadcast_to([128, 4, 128])), [K_TM, WT], [KH])
            dve(lambda e: e.tensor_copy(mls[:, 32:36], EE[:, :, 127:128].rearrange("p h o -> p (h o)")), [EE], [mls])
            dve(lambda e: e.tensor_add(mls[:, 36:40], mls[:, 8:12], mbv[:, :, 127:128].rearrange("p h o -> p (h o)")),
                [mls, mb], [mls])
            dve(lambda e: e.tensor_mul(WT[:], WT[:], cb(M_U).unsqueeze(1).broadcast_to([128, 4, 128])), [WT, CB], [WT])
            sc = PB[7]
            scv = sc[:].rearrange("p (h t) -> p h t", h=4)
            for h in range(4):
                pe(lambda e: e.matmul(scv[:, h, :], KT[:, h, :], QT[:, h, :], start=True, stop=True), [KT, QT], [sc])
            dve(lambda e: e.tensor_mul(ATW[:], WT[:], scv), [WT, sc], [ATW])
            dve(lambda e: e.tensor_mul(QTS[:], QT[:], EE[:]), [QT, EE], [QTS])
            act(lambda e: e.copy(C_BF[:], C_AUG[:]), [C_AUG], [C_BF])
            nb = [PB[4], PB[5], PB[2], PB[3]]
            for h in range(4):
                o = nb[h][:, 0:257]
                pe(lambda e: e.matmul(o, ATW[:, h, :], V_AUG[:, h, :], start=True, stop=False), [ATW, V_AUG], [nb[h]])
                pe(lambda e: e.matmul(o, QTS[:, h, :], C_BF[:, h, :], start=False, stop=True), [QTS, C_BF], [nb[h]])
            for h in range(4):
                o = nb[h][:, 0:257]
                act(lambda e: e.activation(mls[:, 40 + h:41 + h], o[:, 256:257], AF.Abs), [nb[h]], [mls])
                dve(lambda e: e.tensor_max(mls[:, 40 + h:41 + h], mls[:, 40 + h:41 + h], mls[:, 28 + h:29 + h]), [mls], [mls])
            dve(lambda e: e.reciprocal(mls[:, 44:48], mls[:, 40:44]), [mls], [mls])
            for h in range(4):
                o = nb[h][:, 0:256]
                act(lambda e: e.activation(T1[:, h * 256:(h + 1) * 256], o, AF.Copy, scale=mls[:, 44 + h:45 + h]),
                    [nb[h], mls], [T1])
            ub = [PB[6], PB[7], PB[2], PB[3]]
            for h in range(4):
                o = ub[h][:, 0:257]
                pe(lambda e: e.matmul(o, KH[:, h, :], V_AUG[:, h, :], start=True, stop=True), [KH, V_AUG], [ub[h]])
            for h in range(4):
                o = ub[h][:, 0:257]
                dve(lambda e: e.scalar_tensor_tensor(C_AUG[:, h, :], C_AUG[:, h, :], mls[:, 32 + h:33 + h], o,
                                                     ALU.mult, ALU.add), [C_AUG, mls, ub[h]], [C_AUG])
            dve(lambda e: e.tensor_copy(M_BC[:], mls[:, 36:40]), [mls], [M_BC])
            rms_gate_out(T1, 4, 256, nt, ML_NG, SIGO, "ml_wout")
            if t == NT - 1:
                S.dma('sp', oC_p.rearrange("h d v -> d h v"), C_AUG[:, :, 0:256], reads=[C_AUG])
                S.dma('sp', on_p.rearrange("h d -> d h"), C_AUG[:, :, 256:257].rearrange("p h o -> p (h o)"),
                      reads=[C_AUG], allow_slow_non_contiguous=True)
                S.dma('sp', om_p, M_BC[0:1, :], reads=[M_BC])

        RING = [2, 5, 17]
        KTR = [[sb("ktr%d_%d" % (g, i), [128, 4, 128], BF16) for i in range(RING[g])] for g in range(3)]
        VR = [[sb("vr%d_%d" % (g, i), [128, 8, 65], BF16) for i in range(RING[g])] for g in range(3)]
        for g in range(3):
            for v in VR[g]:
                dve(lambda e: e.memset(v[:], 1.0), [], [v])
        QTG = [sb("qtg%d" % g, [128, 4, 128], BF16) for g in range(3)]
        QR = sb("QR", [128, 1024], F32)
        QRB = sb("QRB", [128, 1024], BF16)
        KR = sb("KR", [128, 512], F32); KRB = sb("KRB", [128, 512], BF16)
        VF = sb("VF", [128, 512], F32)
        PT = [sb("PT%d" % i, [128, 4, 128], BF16) for i in range(4)]
        OALL = sb("OALL", [128, 16, 65], F32)
        YB = sb("YB", [128, 1024], BF16)
        Q_TM = KRB; K_TM = KR; QT, KT, KH = QTG; ATW, QTS = PT[0], PT[1]
        MASKS = {1: (M_U, None, M_L), 4: (M_U4, M_R4, M_L4), 16: (M_U16, M_R16, M_L16)}
        att = {'i': 0}

        def attn_run(heads):
            chunks = []
            for hi, (units, h) in enumerate(heads):
                n = len(units)
                for c0 in range(0, n, 4):
                    chunks.append((units[c0:c0 + 4], c0 == 0, c0 + 4 >= n, PB[6 + hi % 2], h))
            sbanks = [PB[0], PB[1], PB[4], PB[5]]
            N = len(chunks)

            def st_score(i):
                bank = sbanks[i % 4]
                pv = bank[:].rearrange("p (a b) -> p a b", a=4)
                for j, u in enumerate(chunks[i][0]):
                    pe(lambda e: e.matmul(pv[:, j, :], u[0], u[1], start=True, stop=True), [u[4], u[5]], [bank])

            def st_soft(i):
                bank = sbanks[i % 4]
                pt = PT[i % 4]
                pv = bank[:].rearrange("p (a b) -> p a b", a=4)
                us = chunks[i][0]
                m = len(us)
                act(lambda e: e.activation(pt[:, 0:m, :], pv[:, 0:m, :], AF.Exp, scale=0.125), [bank], [pt])
                j0 = 0
                while j0 < m:
                    j1 = j0
                    while j1 < m and us[j1][3] == us[j0][3]:
                        j1 += 1
                    mk = us[j0][3]
                    if mk is not None:
                        dve(lambda e: e.tensor_mul(pt[:, j0:j1, :], pt[:, j0:j1, :],
                                                   cb(mk).unsqueeze(1).broadcast_to([128, j1 - j0, 128])), [pt, CB], [pt])
                    j0 = j1

            def st_pv(i):
                us, first, last, accb, h = chunks[i]
                pt = PT[i % 4]
                m = len(us)
                for j, u in enumerate(us):
                    pe(lambda e: e.matmul(accb[:, 0:65], pt[:, j, :], u[2], start=(first and j == 0), stop=(last and j == m - 1)),
                       [pt, u[6]], [accb])
                if last:
                    dve(lambda e: e.tensor_copy(OALL[:, h, :], accb[:, 0:65]), [accb], [OALL])

            for i in range(N + 2):
                if i < N:
                    st_score(i)
                if 0 <= i - 1 < N:
                    st_soft(i - 1)
                if 0 <= i - 2 < N:
                    st_pv(i - 2)

        def dil_prompt(t):
            nt = 128
            to_fm(X, nt)
            load_rope(t * 128, nt)

            def cons(si, c0, w, bank, ap):
                g, j = si // 3, si % 3
                slot = t % RING[g]
                if j == 0:
                    dst_buf[0] = QR
                    rope_apply(QR[0:nt, 0:512].rearrange("p (h d) -> p h d", h=8), ap, 8, nt, bank)
                    dve(lambda e: e.tensor_copy(QRB[:, 0:512], QR[:, 0:512]), [QR], [QRB])
                    tb = PB[2]
                    pv = tb[:].bitcast(BF16).rearrange("p (a b) -> p a b", a=8)
                    for pr in range(4):
                        pe(lambda e: e.transpose(pv[:, pr, :], QRB[:, pr * 128:(pr + 1) * 128], cb(M_ID)), [QRB, CB], [tb])
                    act(lambda e: e.copy(QTG[g][:], pv[:, 0:4, :]), [tb], [QTG[g]])
                elif j == 1:
                    dst_buf[0] = KR
                    rope_apply(KR[0:nt, :].rearrange("p (h d) -> p h d", h=8), ap, 8, nt, bank)
                    dve(lambda e: e.tensor_copy(KRB[:], KR[:]), [KR], [KRB])
                    tb = PB[3]
                    pv = tb[:].bitcast(BF16).rearrange("p (a b) -> p a b", a=8)
                    for pr in range(4):
                        pe(lambda e: e.transpose(pv[:, pr, :], KRB[:, pr * 128:(pr + 1) * 128], cb(M_ID)), [KRB, CB], [tb])
                    act(lambda e: e.copy(KTR[g][slot][:], pv[:, 0:4, :]), [tb], [KTR[g][slot]])
                    win = min(DIL[g][0], SEQ)
                    if t * 128 >= SEQ - win:
                        r0 = t * 128 - (SEQ - win)
                        S.dma('sp', okv_p[g][r0:r0 + 128, 0].rearrange("l h d -> l (h d)"), KR[:], reads=[KR])
                else:
                    act(lambda e: e.copy(VF[:], ap), [bank], [VF])
                    dve(lambda e: e.tensor_copy(VR[g][slot][:, :, 0:64], VF[:].rearrange("p (h d) -> p h d", h=8)),
                        [VF], [VR[g][slot]])
                    win = min(DIL[g][0], SEQ)
                    if t * 128 >= SEQ - win:
                        r0 = t * 128 - (SEQ - win)
                        S.dma('sp', okv_p[g][r0:r0 + 128, 1].rearrange("l h d -> l (h d)"), VF[:], reads=[VF])
            linear_tm(XT, 8, "dl_win", 4608, nt, cons)
            heads = []
            for h in range(8):
                pr, e0 = h // 2, (h % 2) * 64
                units = []
                for g in (2, 1, 0):
                    win, dil = DIL[g]
                    nd = win // 128
                    mu, mm, ml = MASKS[dil]
                    for dlt in range(0, nd + 1):
                        if t - dlt < 0:
                            continue
                        slot = (t - dlt) % RING[g]
                        mk = mu if dlt == 0 else (ml if dlt == nd else mm)
                        units.append((KTR[g][slot][e0:e0 + 64, pr, :], QTG[g][e0:e0 + 64, pr, :], VR[g][slot][:, h, :],
                                      mk, KTR[g][slot], QTG[g], VR[g][slot]))
                mids = [u for u in units if u[3] in (M_R16, M_R4)]
                mids.sort(key=lambda u: 0 if u[3] == M_R16 else 1)
                units = mids + [u for u in units if u[3] not in (M_R16, M_R4)]
                heads.append((units, h))
            attn_run(heads)
            dve(lambda e: e.reciprocal(sm[:, 48:56], OALL[:, 0:8, 64:65].rearrange("p h o -> p (h o)")), [OALL], [sm])
            dve(lambda e: e.tensor_mul(YB[:, 0:512].rearrange("p (h d) -> p h d", h=8), OALL[:, 0:8, 0:64],
                                       sm[:, 48:56].unsqueeze(2).broadcast_to([128, 8, 64])), [OALL, sm], [YB])
            to_fm(YB, nt, KC=4, bf=True)
            linear_tm(XT, 4, "dl_wout", D, nt, resid_consumer(nt))

        SKT = [sb("skt%d" % i, [128, 2, 128], BF16) for i in range(2)]
        SV = [sb("sv%d" % i, [128, 2, 65], BF16) for i in range(2)]
        for v in SV:
            dve(lambda e: e.memset(v[:], 1.0), [], [v])
        SQT = sb("SQT", [128, 8, 128], BF16)
        KDUP = sb("KDUP", [128, 2, 2, 64], BF16)

        def swa_proj(nt, pos_row):
            load_rope(pos_row, nt)

            def cons(si, c0, w, bank, ap):
                if si < 2:
                    dst_buf[0] = QR
                    rope_apply(QR[0:nt, si * 512:(si + 1) * 512].rearrange("p (h d) -> p h d", h=8), ap, 8, nt, bank)
                else:
                    dst_buf[0] = KR
                    rope_apply(KR[0:nt, 0:128].rearrange("p (h d) -> p h d", h=2), ap[:, 0:128], 2, nt, bank)
                    act(lambda e: e.copy(VF[0:nt, 0:128], ap[:, 128:256]), [bank], [VF])
            linear_tm(XT, 8, "sw_win", 1280, nt, cons)

        def swa_prompt(t):
            nt = 128
            to_fm(X, nt)
            swa_proj(nt, t * 128)
            slot = t % 2
            dve(lambda e: e.tensor_copy(QRB[:], QR[:]), [QR], [QRB])
            for half in range(2):
                tb = PB[2 + half]
                pv = tb[:].bitcast(BF16).rearrange("p (a b) -> p a b", a=8)
                for pr in range(4):
                    c = (half * 4 + pr) * 128
                    pe(lambda e: e.transpose(pv[:, pr, :], QRB[:, c:c + 128], cb(M_ID)), [QRB, CB], [tb])
                act(lambda e: e.copy(SQT[:, half * 4:half * 4 + 4, :], pv[:, 0:4, :]), [tb], [SQT])
            kr = KR[:, 0:128].rearrange("p (h d) -> p h d", h=2)
            for dup in range(2):
                dve(lambda e: e.tensor_copy(KDUP[:, :, dup, :], kr), [KR], [KDUP])
            tb = PB[2]
            pv = tb[:].bitcast(BF16).rearrange("p (a b) -> p a b", a=8)
            for kvh in range(2):
                pe(lambda e: e.transpose(pv[:, kvh, :], KDUP[:, kvh, :, :].rearrange("p a d -> p (a d)"), cb(M_ID)),
                   [KDUP, CB], [tb])
            act(lambda e: e.copy(SKT[slot][:], pv[:, 0:2, :]), [tb], [SKT[slot]])
            dve(lambda e: e.tensor_copy(SV[slot][:, :, 0:64], VF[:, 0:128].rearrange("p (h d) -> p h d", h=2)),
                [VF], [SV[slot]])
            if t == NT - 1:
                S.dma('sp', osw_p[:, 0].rearrange("l h d -> l (h d)"), KR[:, 0:128], reads=[KR])
                S.dma('sp', osw_p[:, 1].rearrange("l h d -> l (h d)"), VF[:, 0:128], reads=[VF])
            heads = []
            for h in range(16):
                kvh, pr, e0 = h // 8, h // 2, (h % 2) * 64
                units = []
                for dlt in (0, 1):
                    if t - dlt < 0:
                        continue
                    sl = (t - dlt) % 2
                    units.append((SKT[sl][e0:e0 + 64, kvh, :], SQT[e0:e0 + 64, pr, :], SV[sl][:, kvh, :],
                                  M_U if dlt == 0 else M_L, SKT[sl], SQT, SV[sl]))
                heads.append((units, h))
            attn_run(heads)
            swa_finish(nt)

        def swa_finish(nt):
            dve(lambda e: e.tensor_add(sm[0:nt, 48:64], OALL[0:nt, :, 64:65].rearrange("p h o -> p (h o)"), ESINK[0:nt, :]),
                [OALL, ESINK], [sm])
            dve(lambda e: e.reciprocal(sm[0:nt, 48:64], sm[0:nt, 48:64]), [sm], [sm])
            dve(lambda e: e.tensor_mul(YB[0:nt, :].rearrange("p (h d) -> p h d", h=16), OALL[0:nt, :, 0:64],
                                       sm[0:nt, 48:64].unsqueeze(2).broadcast_to([nt, 16, 64])), [OALL, sm], [YB])
            to_fm(YB, nt, bf=True)
            linear_tm(XT, 8, "sw_wout", D, nt, resid_consumer(nt))

        HS = sb("HS", [128, 8, 128], F32)
        HSB = [sb("HSB%d" % c, [128, 8, 128], BF16) for c in range(4)]
        QTM = [sb("QTM%d" % c, [128, 8, 128], BF16) for c in range(4)]
        HQ = QR; HLF = SIGO; HK = sb("HK", [128, D], F32)
        HV = HB; HG = sb("HG", [128, D], F32)
        HQB = QRB; HKB = YB; HKH = T4
        HKM2 = [sb("HKM%d" % c, [128, D], BF16) for c in range(2)]
        HKM = [HKM2[0], HKM2[1], HKM2[0], HKM2[1]]
        HQT = SQT; HKT = sb("HKT", [128, 8, 128], BF16)
        HDEC = sb("HDEC", [128, 8, 4], F32)
        HAT = sb("HAT", [128, 8, 128], BF16)
        dve(lambda e: e.memset(HS[:], 0.0), [], [HS])
        for c in range(4):
            dve(lambda e: e.memset(QTM[c][:], 0.0), [], [QTM[c]])

        def hgrn_proj(nt):
            S.dma('sp', T2[:], hg_bf.partition_broadcast(128), writes=[T2])
            def cons(si, c0, w, bank, ap):
                cc = (si % 2) * 512
                if si < 2:
                    act(lambda e: e.activation(HQ[0:nt, cc:cc + 512], ap, AF.Silu), [bank], [HQ])
                elif si < 4:
                    dve(lambda e: e.tensor_add(T1[0:nt, cc:cc + 512], ap, BFB[0:nt, cc:cc + 512]), [bank, BFB], [T1])
                    act(lambda e: e.activation(T1[0:nt, cc:cc + 512], T1[0:nt, cc:cc + 512], AF.Sigmoid), [T1], [T1])
                    dve(lambda e: e.tensor_scalar(T1[0:nt, cc:cc + 512], T1[0:nt, cc:cc + 512], -1.0, 1.0, ALU.mult, ALU.add),
                        [T1], [T1])
                    dve(lambda e: e.tensor_mul(T1[0:nt, cc:cc + 512], T1[0:nt, cc:cc + 512], OML[0:nt, cc:cc + 512]),
                        [T1, OML], [T1])
                    dve(lambda e: e.tensor_scalar(T1[0:nt, cc:cc + 512], T1[0:nt, cc:cc + 512], -1.0, 1.0, ALU.mult, ALU.add),
                        [T1], [T1])
                    act(lambda e: e.activation(HLF[0:nt, cc:cc + 512], T1[0:nt, cc:cc + 512], AF.Ln), [T1], [HLF])
                    dve(lambda e: e.tensor_scalar(HK[0:nt, cc:cc + 512], T1[0:nt, cc:cc + 512], -1.0, 1.0,
                                                  ALU.mult, ALU.add), [T1], [HK])
                elif si < 6:
                    act(lambda e: e.copy(HV[0:nt, cc:cc + 512], ap), [bank], [HV])
                else:
                    act(lambda e: e.activation(HG[0:nt, cc:cc + 512], ap, AF.Sigmoid), [bank], [HG])
            linear_tm(XT, 8, "hg_win", 4096, nt, cons)

        def hgrn_prompt(t):
            nt = 128
            to_fm(X, nt)
            hgrn_proj(nt)
            for hf in range(2):
                cc = hf * 512
                bb, be = PB[4 + hf], PB[6 + hf]
                pe(lambda e: e.matmul(bb[:, :], cs(C_BU), HLF[:, cc:cc + 512], start=True, stop=True), [CST, HLF], [bb])
                pe(lambda e: e.matmul(be[:, :], cs(C_BO), HLF[:, cc:cc + 512], start=True, stop=True), [CST, HLF], [be])
                act(lambda e: e.copy(T1[:, cc:cc + 512], bb[:, :]), [bb], [T1])
                dve(lambda e: e.tensor_sub(T2[:, cc:cc + 512], be[:, :], T1[:, cc:cc + 512]), [be, T1], [T2])
            act(lambda e: e.activation(T3[:], T1[:], AF.Exp), [T1], [T3])
            dve(lambda e: e.tensor_mul(HQB[:], HQ[:], T3[:]), [HQ, T3], [HQB])
            act(lambda e: e.activation(T3[:], T1[:], AF.Exp, scale=-1.0), [T1], [T3])
            dve(lambda e: e.tensor_mul(HKB[:], HK[:], T3[:]), [HK, T3], [HKB])
            act(lambda e: e.activation(T3[:], T2[:], AF.Exp), [T2], [T3])
            dve(lambda e: e.tensor_mul(HKH[:], HK[:], T3[:]), [HK, T3], [HKH])
            for src, dstT in ((HQB, HQT), (HKB, HKT)):
                to_fm(src, nt, bf=True, dst=dstT)
            for c in range(4):
                dve(lambda e: e.tensor_copy(QTM[c][:, :, 32 * c:32 * c + 32], HQT[:, :, 32 * c:32 * c + 32]), [HQT], [QTM[c]])
            db = PB[4]
            dbv = db[:, 0:32].rearrange("p (h c) -> p h c", h=8)
            for h in range(8):
                pe(lambda e: e.matmul(dbv[:, h, :], HLF[:, h * 128:(h + 1) * 128], CST[:, C_CI:C_CI + 4],
                                      start=True, stop=True), [HLF, CST], [db])
            act(lambda e: e.activation(HDEC[:], dbv, AF.Exp), [db], [HDEC])
            for c in range(4):
                dve(lambda e: e.tensor_scalar_mul(HKM[c][:], HKH[:], CST[:, C_CI + c:C_CI + c + 1]), [HKH, CST], [HKM[c]])
                act(lambda e: e.copy(HSB[c][:], HS[:]), [HS], [HSB[c]])
                ub = [PB[6], PB[7]]
                for h in range(8):
                    o = ub[h // 4][:, (h % 4) * 128:(h % 4) * 128 + 128]
                    pe(lambda e: e.matmul(o, HKM[c][:, h * 128:(h + 1) * 128], HV[:, h * 128:(h + 1) * 128],
                                          start=True, stop=True), [HKM[c], HV], [ub[h // 4]])
                dve(lambda e: e.tensor_mul(HS[:], HS[:], HDEC[:, :, c:c + 1].broadcast_to([128, 8, 128])), [HS, HDEC], [HS])
                for hf in range(2):
                    dve(lambda e: e.tensor_add(HS[:, 4 * hf:4 * hf + 4, :], HS[:, 4 * hf:4 * hf + 4, :],
                                               ub[hf][:, :].rearrange("p (h v) -> p h v", h=4)), [HS, ub[hf]], [HS])
            for hf in range(2):
                sbk = PB[4 + hf]
                sv = sbk[:].rearrange("p (h t) -> p h t", h=4)
                for j in range(4):
                    h = 4 * hf + j
                    pe(lambda e: e.matmul(sv[:, j, :], HKT[:, h, :], HQT[:, h, :], start=True, stop=True), [HKT, HQT], [sbk])
                dve(lambda e: e.tensor_mul(HAT[:, 4 * hf:4 * hf + 4, :], sv, cs(C_BU).unsqueeze(1).broadcast_to([128, 4, 128])),
                    [sbk, CST], [HAT])
            for h in range(8):
                bank = PB[6 + h % 2]
                o = bank[:, 0:128]
                pe(lambda e: e.matmul(o, HAT[:, h, :], HV[:, h * 128:(h + 1) * 128], start=True, stop=False), [HAT, HV], [bank])
                for c in range(4):
                    pe(lambda e: e.matmul(o, QTM[c][:, h, :], HSB[c][:, h, :], start=False, stop=(c == 3)),
                       [QTM[c], HSB[c]], [bank])
                act(lambda e: e.copy(T1[:, h * 128:(h + 1) * 128], o), [bank], [T1])
            rms_gate_out(T1, 8, 128, nt, HG_NG, HG, "hg_wout")
            if t == NT - 1:
                S.dma('sp', oS_p.rearrange("h d v -> d h v"), HS[:], reads=[HS])

        mixers = [mlstm_prompt, dil_prompt, hgrn_prompt, swa_prompt]
        for mt in range(NT // NSUB):
            for st in range(NSUB):
                t = mt * NSUB + st
                S.dma('sp', XS[st][:], xp[t * 128:(t + 1) * 128, :], writes=[XS[st]])
            for layer in range(4):
                for st in range(NSUB):
                    X.set(XS[st])
                    lin['i'] = 0
                    mixers[layer](mt * NSUB + st)
                    layer_norm(128, 0, layer)
                mlp(NSUB, 128, layer)
            for st in range(NSUB):
                t = mt * NSUB + st
                S.dma('sp', yp[t * 128:(t + 1) * 128, :], XS[st][:], reads=[XS[st]])
            issue_shift(-(-4 * NS // (NT // NSUB)))

        nt = NS
        SSA = sb("SSA", [128, 128], F32); SSB = sb("SSB", [128, 128], F32); SSC = sb("SSC", [128, 128], F32)
        EP = lambda s: CST[:, C_EP + s * 16:C_EP + s * 16 + NS]
        idNS = cs(C_ID)[0:NS, 0:NS]

        def shift_copy(src, dst, L, rowlen):
            for s in range(NS):
                a = src[s, 1:L].rearrange("l a h d -> (l a h d)").rearrange("(o i) -> o i", o=128)
                b = dst[s, 0:L - 1].rearrange("l a h d -> (l a h d)").rearrange("(o i) -> o i", o=128)
                S.dma('sp', b, a)

        def mlstm_sample():
            to_fm(X, nt)
            mlstm_proj(nt)
            S.dma('sp', mls[0:nt, 4:8], st_m, writes=[mls])
            dve(lambda e: e.tensor_add(mls[0:nt, 8:12], mls[0:nt, 0:4], mls[0:nt, 4:8]), [mls], [mls])
            dve(lambda e: e.tensor_max(mls[0:nt, 12:16], mls[0:nt, 8:12], GATES[0:nt, 0:4]), [mls, GATES], [mls])
            S.dma('sp', om_s, mls[0:nt, 12:16], reads=[mls])
            dve(lambda e: e.tensor_sub(mls[0:nt, 16:20], mls[0:nt, 8:12], mls[0:nt, 12:16]), [mls], [mls])
            act(lambda e: e.activation(mls[0:nt, 16:20], mls[0:nt, 16:20], AF.Exp), [mls], [mls])
            dve(lambda e: e.tensor_sub(mls[0:nt, 20:24], GATES[0:nt, 0:4], mls[0:nt, 12:16]), [mls, GATES], [mls])
            act(lambda e: e.activation(mls[0:nt, 20:24], mls[0:nt, 20:24], AF.Exp), [mls], [mls])
            act(lambda e: e.activation(mls[0:nt, 28:32], mls[0:nt, 12:16], AF.Exp, scale=-1.0), [mls], [mls])
            dve(lambda e: e.tensor_mul(T2[0:nt, 0:512].rearrange("p (h d) -> p h d", h=4),
                                       K_TM[0:nt, :].rearrange("p (h d) -> p h d", h=4),
                                       mls[0:nt, 20:24].unsqueeze(2).broadcast_to([nt, 4, 128])), [K_TM, mls], [T2])
            VA = T3[0:nt, 0:1024].rearrange("p (h d) -> p h d", h=4)
            dve(lambda e: e.tensor_copy(VA, V_AUG[0:nt, :, 0:256]), [V_AUG], [T3])
            dsel = T4[0:nt, 0:NS * 4].rearrange("p (s h) -> p s h", h=4)
            dve(lambda e: e.tensor_mul(dsel, idNS.unsqueeze(2).broadcast_to([nt, NS, 4]),
                                       mls[0:nt, 16:20].unsqueeze(1).broadcast_to([nt, NS, 4])), [CST, mls], [T4])
            pb = PB[4]
            pe(lambda e: e.matmul(pb[:, 0:NS * 4], cs(C_ONES)[0:NS, :], T4[0:nt, 0:NS * 4], start=True, stop=True),
               [CST, T4], [pb])
            WPB = SSA
            dve(lambda e: e.tensor_copy(WPB[:, 0:NS * 4], pb[:, 0:NS * 4]), [pb], [WPB])
            dve(lambda e: e.tensor_copy(T1[0:nt, 0:512], Q_TM[0:nt, :]), [Q_TM], [T1])
            tb = PB[2]
            tv = tb[:].rearrange("p (a b) -> p a b", a=4)
            for h in range(4):
                pe(lambda e: e.transpose(tv[:, h, 0:nt], T1[0:nt, h * 128:(h + 1) * 128], idNS), [T1, CST], [tb])
            QF = SSB
            qf = QF[:, 0:4 * NS].rearrange("p (h s) -> p h s", h=4)
            act(lambda e: e.copy(qf, tv[:, :, 0:nt]), [tb], [QF])
            acc = [PB[0], PB[1], PB[6], PB[7]]
            CS_ = HK
            CSN = HG
            for s in range(NS):
                KM = VF
                dve(lambda e: e.tensor_scalar_mul(KM[0:nt, 0:512], T2[0:nt, 0:512], cs(C_ID)[0:nt, s:s + 1]), [T2, CST], [KM])
                QM = SSC
                qm = QM[:, 0:4 * NS].rearrange("p (h s) -> p h s", h=4)
                dve(lambda e: e.tensor_mul(qm, qf, CST[:, C_EP + s * 16:C_EP + s * 16 + NS].unsqueeze(1).broadcast_to([128, 4, NS])),
                    [QF, CST], [QM])
                for h in range(4):
                    S.dma('sp', CS_[:, 0:256], st_C[s, h], writes=[CS_])
                    S.dma('sp', CS_[:, 256:257], st_n[s, h].rearrange("(d o) -> d o", o=1), writes=[CS_])
                    ub = PB[5]
                    pe(lambda e: e.matmul(ub[:, 0:256], KM[0:nt, h * 128:(h + 1) * 128], VA[:, h, :], start=True, stop=True),
                       [KM, T3], [ub])
                    pe(lambda e: e.matmul(ub[:, 256:257], KM[0:nt, h * 128:(h + 1) * 128], cs(C_ONES)[0:nt, 0:1], start=True, stop=True),
                       [KM, CST], [ub])
                    dve(lambda e: e.scalar_tensor_tensor(CSN[:, 0:257], CS_[:, 0:257], WPB[:, s * 4 + h:s * 4 + h + 1],
                                                         ub[:, 0:257], ALU.mult, ALU.add), [CS_, WPB, ub], [CSN])
                    S.dma('sp', oC_s[s, h], CSN[:, 0:256], reads=[CSN])
                    S.dma('sp', on_s[s, h].rearrange("(d o) -> d o", o=1), CSN[:, 256:257], reads=[CSN])
                    pe(lambda e: e.matmul(acc[h][0:nt, 0:257], qm[:, h, :], CSN[:, 0:257], start=(s == 0), stop=(s == NS - 1)),
                       [QM, CSN], [acc[h]])
            for h in range(4):
                act(lambda e: e.activation(mls[0:nt, 40 + h:41 + h], acc[h][0:nt, 256:257], AF.Abs), [acc[h]], [mls])
                dve(lambda e: e.tensor_max(mls[0:nt, 40 + h:41 + h], mls[0:nt, 40 + h:41 + h], mls[0:nt, 28 + h:29 + h]), [mls], [mls])
            dve(lambda e: e.reciprocal(mls[0:nt, 44:48], mls[0:nt, 40:44]), [mls], [mls])
            for h in range(4):
                act(lambda e: e.activation(T1[0:nt, h * 256:(h + 1) * 256], acc[h][0:nt, 0:256], AF.Copy,
                                           scale=mls[0:nt, 44 + h:45 + h]), [acc[h], mls], [T1])
            rms_gate_out(T1, 4, 256, nt, ML_NG, SIGO, "ml_wout")

        def hgrn_sample():
            to_fm(X, nt)
            hgrn_proj(nt)
            act(lambda e: e.activation(T2[0:nt, :], HLF[0:nt, :], AF.Exp), [HLF], [T2])
            FF, QF32 = SSA, SSB
            for src, dstb, dt_is_bf in ((T2, FF, False), (HQ, QF32, False)):
                for half in range(2):
                    tb = PB[2 + half]
                    tv = tb[:].rearrange("p (a b) -> p a b", a=4)
                    for j in range(4):
                        h = half * 4 + j
                        pe(lambda e: e.transpose(tv[:, j, 0:nt], src[0:nt, h * 128:(h + 1) * 128], idNS), [src, CST], [tb])
                    act(lambda e: e.copy(dstb[:, half * 4 * NS:(half + 1) * 4 * NS].rearrange("p (h s) -> p h s", h=4),
                                         tv[:, :, 0:nt]), [tb], [dstb])
            ffv = FF[:, 0:8 * NS].rearrange("p (h s) -> p h s", h=8)
            qfv = QF32[:, 0:8 * NS].rearrange("p (h s) -> p h s", h=8)
            dve(lambda e: e.tensor_copy(T4[0:nt, :], HV[0:nt, :]), [HV], [T4])
            acc = [PB[0], PB[1]]
            for s in range(NS):
                KM = HG if False else T1
                dve(lambda e: e.tensor_scalar_mul(KM[0:nt, :], HK[0:nt, :], cs(C_ID)[0:nt, s:s + 1]), [HK, CST], [KM])
                QM = SSC
                qm = QM[:, 0:8 * NS].rearrange("p (h s) -> p h s", h=8)
                dve(lambda e: e.tensor_mul(qm, qfv, CST[:, C_EP + s * 16:C_EP + s * 16 + NS].unsqueeze(1).broadcast_to([128, 8, NS])),
                    [QF32, CST], [QM])
                for h in range(8):
                    SS_, SN_ = HS, HAT
                    SSf = HS[:, 0, :]
                    S.dma('sp', SSf, st_S[s, h], writes=[HS])
                    ub = PB[4 + h % 2]
                    pe(lambda e: e.matmul(ub[:, 0:128], KM[0:nt, h * 128:(h + 1) * 128], T4[0:nt, h * 128:(h + 1) * 128],
                                          start=True, stop=True), [KM, T4], [ub])
                    SNf = HS[:, 1 + h % 2, :]
                    dve(lambda e: e.scalar_tensor_tensor(SNf, SSf, ffv[:, h, s:s + 1], ub[:, 0:128], ALU.mult, ALU.add),
                        [HS, FF, ub], [HS])
                    S.dma('sp', oS_s[s, h], SNf, reads=[HS])
                    a = acc[h // 4][0:nt, (h % 4) * 128:(h % 4) * 128 + 128]
                    pe(lambda e: e.matmul(a, qm[:, h, :], SNf, start=(s == 0 and h % 4 == 0), stop=(s == NS - 1 and h % 4 == 3),
                                          skip_group_check=True), [QM, HS], [acc[h // 4]])
            for hf in range(2):
                act(lambda e: e.copy(T1[0:nt, hf * 512:(hf + 1) * 512], acc[hf][0:nt, :]), [acc[hf]], [T1])
            rms_gate_out(T1, 8, 128, nt, HG_NG, HG, "hg_wout")

        KS = HS

        KSV = HS[:].rearrange("p h v -> p (h v)").rearrange("p (a c) -> p a c", a=2)
        DQb, DQo = [QR, QR, SIGO], [0, 512, 0]
        DKb, DKo = [SIGO, KR, VF], [512, 0, 0]
        DVb, DVo = [HK, HK, HG], [0, 512, 0]

        def dil_sample():
            to_fm(X, nt)
            S.dma('sp', ROPE[0:nt, :], ropet[SEQ].partition_broadcast(nt), writes=[ROPE])

            def cons(si, c0, w, bank, ap):
                g, j = si // 3, si % 3
                if j == 0:
                    dst_buf[0] = DQb[g]
                    rope_apply(DQb[g][0:nt, DQo[g]:DQo[g] + 512].rearrange("p (h d) -> p h d", h=8), ap, 8, nt, bank)
                elif j == 1:
                    dst_buf[0] = DKb[g]
                    rope_apply(DKb[g][0:nt, DKo[g]:DKo[g] + 512].rearrange("p (h d) -> p h d", h=8), ap, 8, nt, bank)
                else:
                    act(lambda e: e.copy(DVb[g][0:nt, DVo[g]:DVo[g] + 512], ap), [bank], [DVb[g]])
            linear_tm(XT, 8, "dl_win", 4608, nt, cons)
            for g in range(3):
                L = DIL[g][0]
                S.dma('sp', okv_s[g][:, L - 1, 0].rearrange("s h d -> s (h d)"), DKb[g][0:nt, DKo[g]:DKo[g] + 512], reads=[DKb[g]])
                S.dma('sp', okv_s[g][:, L - 1, 1].rearrange("s h d -> s (h d)"), DVb[g][0:nt, DVo[g]:DVo[g] + 512], reads=[DVb[g]])
            acc_o, acc_d = [PB[6]], PB[7]
            n = 0
            for s in range(NS):
                for g in range(3):
                    L, dil = DIL[g]
                    QR_save = None
                    cache_attn_g(s, g, L, dil, acc_o, acc_d, n == 0, n == NS * 3 - 1)
                    n += 1
            dve(lambda e: e.tensor_copy(OALL[0:nt, 0:8, 0:64], acc_o[0][0:nt, 0:512].rearrange("p (h d) -> p h d", h=8)),
                [acc_o[0]], [OALL])
            dve(lambda e: e.tensor_copy(OALL[0:nt, 0:8, 64:65], acc_d[0:nt, 0:8].unsqueeze(2)), [acc_d], [OALL])
            for g in range(3):
                qv = DQb[g][0:nt, DQo[g]:DQo[g] + 512].rearrange("p (h d) -> p h d", h=8)
                kv = DKb[g][0:nt, DKo[g]:DKo[g] + 512].rearrange("p (h d) -> p h d", h=8)
                vv = DVb[g][0:nt, DVo[g]:DVo[g] + 512].rearrange("p (h d) -> p h d", h=8)
                t1v = T1[0:nt, 0:512].rearrange("p (h d) -> p h d", h=8)
                dve(lambda e: e.tensor_mul(t1v, qv, kv), [DQb[g], DKb[g]], [T1])
                dve(lambda e: e.tensor_reduce(sm[0:nt, 32:40], t1v, AX.X, ALU.add), [T1], [sm])
                act(lambda e: e.activation(sm[0:nt, 32:40], sm[0:nt, 32:40], AF.Exp, scale=0.125), [sm], [sm])
                dve(lambda e: e.tensor_mul(t1v, vv, sm[0:nt, 32:40].unsqueeze(2).broadcast_to([nt, 8, 64])), [DVb[g], sm], [T1])
                dve(lambda e: e.tensor_add(OALL[0:nt, 0:8, 0:64], OALL[0:nt, 0:8, 0:64], t1v), [OALL, T1], [OALL])
                dve(lambda e: e.tensor_add(OALL[0:nt, 0:8, 64:65], OALL[0:nt, 0:8, 64:65], sm[0:nt, 32:40].unsqueeze(2)),
                    [OALL, sm], [OALL])
            dve(lambda e: e.reciprocal(sm[0:nt, 48:56], OALL[0:nt, 0:8, 64:65].rearrange("p h o -> p (h o)")), [OALL], [sm])
            dve(lambda e: e.tensor_mul(YB[0:nt, 0:512].rearrange("p (h d) -> p h d", h=8), OALL[0:nt, 0:8, 0:64],
                                       sm[0:nt, 48:56].unsqueeze(2).broadcast_to([nt, 8, 64])), [OALL, sm], [YB])
            to_fm(YB, nt, KC=4, bf=True)
            linear_tm(XT, 4, "dl_wout", D, nt, resid_consumer(nt))

        def cache_attn_g(s, g, L, dil, acc_o, acc_d, first, last):
            S.dma('sp', KSV[:, :, :], kvin[g][s, 0:L:dil].rearrange("l a h d -> l a (h d)"), writes=[KS])
            qb = PB[4]
            dve(lambda e: e.tensor_scalar_mul(T4[0:nt, 512:1024], DQb[g][0:nt, DQo[g]:DQo[g] + 512], cs(C_ID)[0:nt, s:s + 1]), [DQb[g], CST], [T4])
            pe(lambda e: e.matmul(qb[:, 0:512], cs(C_ONES)[0:nt, :], T4[0:nt, 512:1024], start=True, stop=True), [CST, T4], [qb])
            dve(lambda e: e.tensor_mul(T1[:, 0:512], qb[:, 0:512], KSV[:, 0, :]), [qb, KS], [T1])
            dve(lambda e: e.tensor_reduce(sm[:, 0:8], T1[:, 0:512].rearrange("p (h d) -> p h d", h=8), AX.X, ALU.add), [T1], [sm])
            act(lambda e: e.activation(sm[:, 16:24], sm[:, 0:8], AF.Exp, scale=0.125), [sm], [sm])
            dve(lambda e: e.tensor_mul(T2[:, 0:512].rearrange("p (h d) -> p h d", h=8),
                                       KSV[:, 1, :].rearrange("p (h d) -> p h d", h=8),
                                       sm[:, 16:24].unsqueeze(2).broadcast_to([128, 8, 64])), [KS, sm], [T2])
            pe(lambda e: e.matmul(acc_o[0][0:nt, 0:512], EP(s), T2[:, 0:512], start=first, stop=last), [CST, T2], [acc_o[0]])
            pe(lambda e: e.matmul(acc_d[0:nt, 0:8], EP(s), sm[:, 16:24], start=first, stop=last), [CST, sm], [acc_d])

        def swa_sample():
            to_fm(X, nt)
            S.dma('sp', ROPE[0:nt, :], ropet[SEQ].partition_broadcast(nt), writes=[ROPE])

            def cons(si, c0, w, bank, ap):
                if si < 2:
                    dst_buf[0] = QR
                    rope_apply(QR[0:nt, si * 512:(si + 1) * 512].rearrange("p (h d) -> p h d", h=8), ap, 8, nt, bank)
                else:
                    dst_buf[0] = KR
                    rope_apply(KR[0:nt, 0:128].rearrange("p (h d) -> p h d", h=2), ap[:, 0:128], 2, nt, bank)
                    act(lambda e: e.copy(VF[0:nt, 0:128], ap[:, 128:256]), [bank], [VF])
            linear_tm(XT, 8, "sw_win", 1280, nt, cons)
            S.dma('sp', osw_s[:, 127, 0].rearrange("s h d -> s (h d)"), KR[0:nt, 0:128], reads=[KR])
            S.dma('sp', osw_s[:, 127, 1].rearrange("s h d -> s (h d)"), VF[0:nt, 0:128], reads=[VF])
            acc_o, acc_d = [PB[6], PB[5]], PB[7]
            for s in range(NS):
                S.dma('sp', KSV[:, :, 0:128], swain[s].rearrange("l a h d -> l a (h d)"), writes=[KS])
                for half in range(2):
                    qb = PB[4]
                    dve(lambda e: e.tensor_scalar_mul(T4[0:nt, 512:1024], QR[0:nt, half * 512:(half + 1) * 512], cs(C_ID)[0:nt, s:s + 1]), [QR, CST], [T4])
                    pe(lambda e: e.matmul(qb[:, 0:512], cs(C_ONES)[0:nt, :], T4[0:nt, 512:1024], start=True, stop=True), [CST, T4], [qb])
                    kb = KSV[:, 0, half * 64:(half + 1) * 64].unsqueeze(1).broadcast_to([128, 8, 64])
                    dve(lambda e: e.tensor_mul(T1[:, 0:512].rearrange("p (h d) -> p h d", h=8),
                                               qb[:, 0:512].rearrange("p (h d) -> p h d", h=8), kb), [qb, KS], [T1])
                    dve(lambda e: e.tensor_reduce(sm[:, 0:8], T1[:, 0:512].rearrange("p (h d) -> p h d", h=8), AX.X, ALU.add),
                        [T1], [sm])
                    act(lambda e: e.activation(sm[:, 16 + 8 * half:24 + 8 * half], sm[:, 0:8], AF.Exp, scale=0.125), [sm], [sm])
                    vb = KSV[:, 1, half * 64:(half + 1) * 64].unsqueeze(1).broadcast_to([128, 8, 64])
                    dve(lambda e: e.tensor_mul(T2[:, 0:512].rearrange("p (h d) -> p h d", h=8), vb,
                                               sm[:, 16 + 8 * half:24 + 8 * half].unsqueeze(2).broadcast_to([128, 8, 64])),
                        [KS, sm], [T2])
                    pe(lambda e: e.matmul(acc_o[half][0:nt, 0:512], EP(s), T2[:, 0:512], start=(s == 0), stop=(s == NS - 1)),
                       [CST, T2], [acc_o[half]])
                pe(lambda e: e.matmul(acc_d[0:nt, 0:16], EP(s), sm[:, 16:32], start=(s == 0), stop=(s == NS - 1)), [CST, sm], [acc_d])
            for half in range(2):
                dve(lambda e: e.tensor_copy(OALL[0:nt, 8 * half:8 * half + 8, 0:64],
                                            acc_o[half][0:nt, 0:512].rearrange("p (h d) -> p h d", h=8)), [acc_o[half]], [OALL])
            dve(lambda e: e.tensor_copy(OALL[0:nt, :, 64:65], acc_d[0:nt, 0:16].unsqueeze(2)), [acc_d], [OALL])
            qv = QR[0:nt, :].rearrange("p (k g d) -> p k g d", k=2, g=8)
            kv = KR[0:nt, 0:128].rearrange("p (k d) -> p k d", k=2).unsqueeze(2).broadcast_to([nt, 2, 8, 64])
            vv = VF[0:nt, 0:128].rearrange("p (k d) -> p k d", k=2).unsqueeze(2).broadcast_to([nt, 2, 8, 64])
            t1v = T1[0:nt, :].rearrange("p (k g d) -> p k g d", k=2, g=8)
            dve(lambda e: e.tensor_mul(t1v, qv, kv), [QR, KR], [T1])
            dve(lambda e: e.tensor_reduce(sm[0:nt, 32:48], T1[0:nt, :].rearrange("p (h d) -> p h d", h=16), AX.X, ALU.add), [T1], [sm])
            act(lambda e: e.activation(sm[0:nt, 32:48], sm[0:nt, 32:48], AF.Exp, scale=0.125), [sm], [sm])
            dve(lambda e: e.tensor_mul(t1v, vv, sm[0:nt, 32:48].rearrange("p (k g) -> p k g", k=2).unsqueeze(3).broadcast_to([nt, 2, 8, 64])),
                [VF, sm], [T1])
            dve(lambda e: e.tensor_add(OALL[0:nt, :, 0:64], OALL[0:nt, :, 0:64], T1[0:nt, :].rearrange("p (h d) -> p h d", h=16)),
                [OALL, T1], [OALL])
            dve(lambda e: e.tensor_add(OALL[0:nt, :, 64:65], OALL[0:nt, :, 64:65], sm[0:nt, 32:48].unsqueeze(2)), [OALL, sm], [OALL])
            swa_finish(nt)

        issue_shift(len(shift_jobs))
        X.set(XS[0])
        S.dma('sp', X[0:nt, :], xs, writes=[X])
        smix = [mlstm_sample, dil_sample, hgrn_sample, swa_sample]
        for layer in range(4):
            lin['i'] = 0
            smix[layer]()
            layer_norm(nt, 0, layer)
            mlp(1, nt, layer)
        S.dma('sp', ys, X[0:nt, :], reads=[X])
        S.finish()
    return nc


_cache = {}


def kernel(**inp):
    SEQ = inp["x_prompt"].shape[1]
    NB = inp["x_sample"].shape[0]
    NS = NB // 8
    key = (SEQ, NS)
    if key not in _cache:
        _cache[key] = build(SEQ, NS)
    nc = _cache[key]
    f = lambda a: np.ascontiguousarray(np.asarray(a, dtype=np.float32))
    cst, cstm = host_consts()
    rope = host_rope(SEQ)
    shared = dict(
        ml_win=f(inp["mlstm_w_in"][0]), ml_bg=f(inp["mlstm_b_gates"][0]), ml_ng=f(inp["mlstm_norm_g"][0]), ml_wout=f(inp["mlstm_w_out"][0]),
        dl_win=f(inp["dil_w_in"][0]), dl_wout=f(inp["dil_w_out"][0]),
        hg_win=f(inp["hgrn_w_in"][0]), hg_bf=f(inp["hgrn_b_f"][0]), hg_lb=f(inp["hgrn_lb_logits"]), hg_ng=f(inp["hgrn_norm_g"][0]),
        hg_wout=f(inp["hgrn_w_out"][0]),
        sw_win=f(inp["swa_w_in"][0]), sw_sink=f(inp["swa_sinks"][0]), sw_wout=f(inp["swa_w_out"][0]),
        ln1_g=f(inp["ln1_g"]), ln1_b=f(inp["ln1_b"]), ln2_g=f(inp["ln2_g"]), ln2_b=f(inp["ln2_b"]),
        w1=f(inp["mlp_w1"]), w2=f(inp["mlp_w2"]), cst=cst, cstm=cstm, rope=rope)
    in_maps = []
    for c in range(8):
        sl = slice(c * NS, (c + 1) * NS)
        m = dict(shared)
        m.update(
            xp=f(inp["x_prompt"][c % 2]), xs=f(inp["x_sample"][sl, 0]),
            st_C=f(inp["state_mlstm_C"][0, sl]), st_n=f(inp["state_mlstm_n"][0, sl]), st_m=f(inp["state_mlstm_m"][0, sl]),
            kv0=f(inp["cache_dil_kv0"][0, sl]), kv1=f(inp["cache_dil_kv1"][0, sl]), kv2=f(inp["cache_dil_kv2"][0, sl]),
            st_S=f(inp["state_hgrn_S"][0, sl]), swakv=f(inp["cache_swa_kv"][0, sl]))
        in_maps.append(m)
    res = run_bass_kernel_spmd(nc, in_maps, core_ids=list(range(8))).results
    cat = lambda k: np.concatenate([res[c][k] for c in range(8)], axis=0)
    two = lambda k: np.stack([res[0][k], res[1][k]], axis=0)
    outs = (
        two("yp"), cat("ys")[:, None, :],
        two("oC_p")[None], cat("oC_s")[None], two("on_p")[None], cat("on_s")[None],
        np.stack([res[0]["om_p"][0], res[1]["om_p"][0]])[None], cat("om_s")[None],
        two("okv0_p")[None], cat("okv0_s")[None], two("okv1_p")[None], cat("okv1_s")[None],
        two("okv2_p")[None], cat("okv2_s")[None],
        two("oS_p")[None], cat("oS_s")[None], two("osw_p")[None], cat("osw_s")[None])
    return tuple(np.ascontiguousarray(o.astype(np.float32)) for o in outs)
```

```python
import contextlib
import numpy as np
import concourse.bass as bass
import concourse.mybir as mybir
from concourse.bass_utils import run_bass_kernel_spmd

F32 = mybir.dt.float32
BF16 = mybir.dt.bfloat16
AF = mybir.ActivationFunctionType
ALU = mybir.AluOpType
AX = mybir.AxisListType

D = 1024
ALPHA = 8.0 ** 0.25
LN_EPS = 1e-5
NORM_EPS = 1e-6
PAST_LEN = 8192
DIL = ((128, 1), (512, 4), (2048, 16))

C_ID, C_U, C_BU, C_BO, C_ONES = [i * 128 for i in range(5)]
C_CI = 5 * 128
C_EP = C_CI + 4
CW = C_EP + 256
M_ID, M_U, M_L, M_U4, M_R4, M_L4, M_U16, M_R16, M_L16 = [i * 128 for i in range(9)]
MW = 9 * 128


def host_consts():
    c = np.zeros((128, CW), np.float32)
    m = np.zeros((128, MW), np.float32)
    s = np.arange(128)[:, None]
    t = np.arange(128)[None, :]
    U = (s <= t)
    L = (s >= t)
    c[:, C_ID:C_ID + 128] = (s == t)
    c[:, C_U:C_U + 128] = U
    same = (s // 32 == t // 32)
    c[:, C_BU:C_BU + 128] = same & U
    c[:, C_BO:C_BO + 128] = same
    c[:, C_ONES:C_ONES + 128] = 1.0
    for k in range(4):
        c[32 * k:32 * k + 32, C_CI + k] = 1.0
    for k in range(16):
        c[:, C_EP + k * 16 + k] = 1.0
    m[:, M_ID:M_ID + 128] = (s == t)
    m[:, M_U:M_U + 128] = U
    m[:, M_L:M_L + 128] = L
    for dil, cu, cr, cl in ((4, M_U4, M_R4, M_L4), (16, M_U16, M_R16, M_L16)):
        R = ((t - s) % dil == 0)
        m[:, cu:cu + 128] = U & R
        m[:, cr:cr + 128] = R
        m[:, cl:cl + 128] = L & R
    return c, m


def host_rope(seq):
    pos = np.concatenate([np.arange(seq), [PAST_LEN]]).astype(np.float32)
    inv = np.power(np.float32(10000.0), -np.arange(32, dtype=np.float32) / np.float32(32)).astype(np.float32)
    ang = (pos[:, None] * inv[None, :]).astype(np.float32)
    cs, sn = np.cos(ang).astype(np.float32), np.sin(ang).astype(np.float32)
    return np.concatenate([cs, cs, -sn, sn], axis=1).astype(np.float32)


class Buf:
    def __init__(self, name, h):
        self.name = name
        self.h = h
        self.w = {}
        self.r = {}

    def __getitem__(self, idx):
        return self.h[idx]


class Cur:
    def __init__(self):
        object.__setattr__(self, 'b', None)

    def set(self, b):
        object.__setattr__(self, 'b', b)

    def __getitem__(self, i):
        return self.b[i]

    def __getattr__(self, n):
        return getattr(self.b, n)

    def __setattr__(self, n, v):
        setattr(self.b, n, v)


class Sched:
    NDMA = 40
    NSW = 6
    SAME_ENGINE_SYNC = True

    def __init__(self, nc):
        self.nc = nc
        self.eng = {'pe': nc.tensor, 'act': nc.scalar, 'dve': nc.vector, 'pool': nc.gpsimd, 'sp': nc.sync}
        self.sems, self.count = {}, {}
        self.known = {e: {} for e in self.eng}
        self.dma_rr = 0
        self.n_own = 0
        self.sw_rr = 0
        self.n_inst = 0

    def __enter__(self):
        self.stack = contextlib.ExitStack()
        self.stack.__enter__()
        for e in self.eng:
            self.sems[e] = self.stack.enter_context(self.nc.semaphore("prog_" + e))
            self.count[e] = 0
        for i in range(self.NDMA):
            k = "dma%d" % i
            self.sems[k] = self.stack.enter_context(self.nc.semaphore(k))
            self.count[k] = 0
        for i in range(self.NSW):
            k = "dmasw%d" % i
            self.sems[k] = self.stack.enter_context(self.nc.semaphore(k))
            self.count[k] = 0
        self.block = self.stack.enter_context(self.nc.Block())
        return self

    def __exit__(self, *a):
        return self.stack.__exit__(*a)

    def sb(self, name, shape, dtype):
        return Buf(name, self.stack.enter_context(self.nc.sbuf_tensor("sb_" + name, list(shape), dtype)))

    def ps(self, name, shape, dtype):
        return Buf(name, self.stack.enter_context(self.nc.psum_tensor("ps_" + name, list(shape), dtype)))

    def _wait(self, e, key, val):
        if key == e and (e == 'pe' or not self.SAME_ENGINE_SYNC):
            return
        if self.known[e].get(key, 0) >= val:
            return
        self.eng[e].wait_ge(self.sems[key], val)
        self.known[e][key] = val

    def _deps(self, e, reads, writes):
        for b in reads:
            for k, v in b.w.items():
                self._wait(e, k, v)
        for b in writes:
            for k, v in b.w.items():
                self._wait(e, k, v)
            for k, v in b.r.items():
                self._wait(e, k, v)

    def _record(self, key, val, reads, writes, acc_w=False):
        for b in reads:
            if b.r.get(key, 0) < val:
                b.r[key] = val
        for b in writes:
            if acc_w:
                b.w[key] = val
            else:
                b.w = {key: val}
                b.r = {}

    def op(self, e, fn, reads=(), writes=()):
        self._deps(e, reads, writes)
        ins = fn(self.eng[e])
        self.count[e] += 1
        ins.then_inc(self.sems[e], 1)
        self._record(e, self.count[e], reads, writes)
        self.n_inst += 1

    def dve(self, fn, reads=(), writes=()):
        self.op('dve', fn, reads, writes)

    def act(self, fn, reads=(), writes=()):
        self.op('act', fn, reads, writes)

    def pe(self, fn, reads=(), writes=()):
        self.op('pe', fn, reads, writes)

    def dma(self, q, out, in_, reads=(), writes=(), acc_w=False, own_sem=False, **kw):
        if own_sem:
            k = "dmax%d" % self.n_own
            self.n_own += 1
            self.sems[k] = self.stack.enter_context(self.nc.semaphore(k))
            self.count[k] = 0
        elif q == 'pool':
            k = "dmasw%d" % self.sw_rr
            self.sw_rr = (self.sw_rr + 1) % self.NSW
        else:
            k = "dma%d" % self.dma_rr
            self.dma_rr = (self.dma_rr + 1) % self.NDMA
        if self.count[k] > 0:
            self._wait(q, k, self.count[k])
        self._deps(q, reads, writes)
        ins = self.eng[q].dma_start(out=out, in_=in_, **kw)
        self.count[k] += 16
        ins.then_inc(self.sems[k], 16)
        self._record(k, self.count[k], reads, writes, acc_w)
        self.n_inst += 1

    def finish(self):
        for k in list(self.sems):
            if k.startswith("dma") and self.count[k] > 0:
                self._wait('sp', k, self.count[k])
        for e in ('pe', 'act', 'dve', 'pool'):
            if self.count[e] > 0:
                self.eng['sp'].wait_ge(self.sems[e], self.count[e])


def build(SEQ, NS):
    NT = SEQ // 128
    nc = bass.Bass("TRN2", target_bir_lowering=False)

    def din(name, shape):
        return nc.dram_tensor(name, list(shape), F32, kind="ExternalInput").ap()

    def dout(name, shape):
        return nc.dram_tensor(name, list(shape), F32, kind="ExternalOutput").ap()

    xp = din("xp", [SEQ, D]); xs = din("xs", [NS, D])
    st_C = din("st_C", [NS, 4, 128, 256]); st_n = din("st_n", [NS, 4, 128]); st_m = din("st_m", [NS, 4])
    kvin = [din("kv%d" % g, [NS, DIL[g][0], 2, 8, 64]) for g in range(3)]
    st_S = din("st_S", [NS, 8, 128, 128]); swain = din("swakv", [NS, 128, 2, 2, 64])
    ml_win = din("ml_win", [D, 3080]); ml_bg = din("ml_bg", [8]); ml_ng = din("ml_ng", [D]); ml_wout = din("ml_wout", [D, D])
    dl_win = din("dl_win", [D, 4608]); dl_wout = din("dl_wout", [512, D])
    hg_win = din("hg_win", [D, 4096]); hg_bf = din("hg_bf", [D]); hg_lb = din("hg_lb", [4, D]); hg_ng = din("hg_ng", [D]); hg_wout = din("hg_wout", [D, D])
    sw_win = din("sw_win", [D, 1280]); sw_sink = din("sw_sink", [16]); sw_wout = din("sw_wout", [D, D])
    ln_g = [din("ln1_g", [4, D]), din("ln2_g", [4, D])]
    ln_b = [din("ln1_b", [4, D]), din("ln2_b", [4, D])]
    w1 = din("w1", [4, D, 4096]); w2 = din("w2", [4, 4096, D])
    cst = din("cst", [128, CW]); cstm = din("cstm", [128, MW]); ropet = din("rope", [SEQ + 1, 128])

    def dscr(name, shape):
        return nc.dram_tensor(name, list(shape), BF16).ap()

    wb = {}
    for nm, ap_ in (("ml_win", ml_win), ("ml_wout", ml_wout), ("dl_win", dl_win), ("dl_wout", dl_wout), ("hg_win", hg_win),
                    ("hg_wout", hg_wout), ("sw_win", sw_win), ("sw_wout", sw_wout)):
        wb[nm] = ap_
    for l in range(4):
        wb["w1_%d" % l] = w1[l]; wb["w2_%d" % l] = w2[l]
    unit_ix = {}
    for nm, Wf in wb.items():
        K_, N_ = Wf.shape
        for c0 in range(0, N_, 512):
            for k0 in range(0, K_, 1024):
                unit_ix[(nm, k0, c0)] = len(unit_ix)
    wscr = dscr("wscr", [len(unit_ix), 128, 4096])
    yp = dout("yp", [SEQ, D]); ys = dout("ys", [NS, D])
    oC_p = dout("oC_p", [4, 128, 256]); oC_s = dout("oC_s", [NS, 4, 128, 256])
    on_p = dout("on_p", [4, 128]); on_s = dout("on_s", [NS, 4, 128])
    om_p = dout("om_p", [1, 4]); om_s = dout("om_s", [NS, 4])
    okv_p = [dout("okv%d_p" % g, [min(DIL[g][0], SEQ), 2, 8, 64]) for g in range(3)]
    okv_s = [dout("okv%d_s" % g, [NS, DIL[g][0], 2, 8, 64]) for g in range(3)]
    oS_p = dout("oS_p", [8, 128, 128]); oS_s = dout("oS_s", [NS, 8, 128, 128])
    osw_p = dout("osw_p", [128, 2, 2, 64]); osw_s = dout("osw_s", [NS, 128, 2, 2, 64])

    S = Sched(nc)
    with S:
        sb, dve, act, pe = S.sb, S.dve, S.act, S.pe
        PB = [S.ps("pb%d" % i, [128, 512], F32) for i in range(8)]
        CST = sb("cst", [128, CW], F32)
        CB = sb("cstb", [128, MW], BF16)
        NSUB = 4 if NT % 4 == 0 else (2 if NT % 2 == 0 else 1)
        XS = [sb("X%d" % i, [128, D], F32) for i in range(NSUB)]
        X = Cur()
        X.set(XS[0])
        XT = sb("XT", [128, 8, 128], BF16)
        HT = sb("HT", [128, 8, 128], BF16)
        NW = 3
        WR = [sb("wr%d" % i, [128, 8, 512], BF16) for i in range(NW)]
        T1 = sb("T1", [128, D], F32); T2 = sb("T2", [128, D], F32); T3 = sb("T3", [128, D], F32)
        T4 = sb("T4", [128, D], F32)
        HB = sb("HB", [128, D], BF16)
        ROPE = sb("ROPE", [128, 128], F32)
        sm = sb("sm", [128, 64], F32)
        ML_NG = ml_ng; HG_NG = hg_ng; BFB = T2
        OML = sb("OML", [128, D], F32)
        LBB = None
        GW = sb("GW", [128, 8, 8], BF16)
        MLBG = sb("MLBG", [128, 8], F32); ESINK = sb("ESINK", [128, 16], F32)
        wstate = {'i': 0}

        def cs(c0, n=128, p=128):
            return CST[0:p, c0:c0 + n]

        def cb(c0, n=128):
            return CB[:, c0:c0 + n]

        S.dma('sp', CST[:], cst, writes=[CST])
        S.dma('pool', CB[:], cstm, writes=[CB])
        S.dma('sp', MLBG[:], ml_bg.partition_broadcast(128), writes=[MLBG])
        S.dma('pool', GW[:], ml_win[:, 3072:3080].rearrange("(k p) n -> p k n", p=128), writes=[GW])
        S.dma('sp', ESINK[:], sw_sink.partition_broadcast(128), writes=[ESINK])
        act(lambda e: e.activation(ESINK[:], ESINK[:], AF.Exp), [ESINK], [ESINK])
        TMX = XS[0]
        for i, T in enumerate((T1, T2, T3, T4)):
            S.dma('sp', T[:], hg_lb[i].partition_broadcast(128), writes=[T])
        dve(lambda e: e.tensor_max(TMX[:], T1[:], T2[:]), [T1, T2], [TMX])
        dve(lambda e: e.tensor_max(OML[:], T3[:], T4[:]), [T3, T4], [OML])
        dve(lambda e: e.tensor_max(TMX[:], TMX[:], OML[:]), [TMX, OML], [TMX])
        for T in (T1, T2, T3, T4):
            dve(lambda e: e.tensor_sub(T[:], T[:], TMX[:]), [T, TMX], [T])
            act(lambda e: e.activation(T[:], T[:], AF.Exp), [T], [T])
        dve(lambda e: e.tensor_add(TMX[:], T2[:], T3[:]), [T2, T3], [TMX])
        dve(lambda e: e.tensor_add(OML[:], T1[:], T4[:]), [T1, T4], [OML])
        dve(lambda e: e.tensor_add(T1[:], TMX[:], OML[:]), [TMX, OML], [T1])
        dve(lambda e: e.reciprocal(T1[:], T1[:]), [T1], [T1])
        dve(lambda e: e.tensor_mul(OML[:], OML[:], T1[:]), [OML, T1], [OML])

        shift_jobs = []
        for src_, dst_, L_ in ((kvin[2], okv_s[2], 2048), (kvin[1], okv_s[1], 512), (kvin[0], okv_s[0], 128), (swain, osw_s, 128)):
            for s_ in range(NS):
                a_ = src_[s_, 1:L_].rearrange("l a h d -> (l a h d)").rearrange("(o i) -> o i", o=128)
                b_ = dst_[s_, 0:L_ - 1].rearrange("l a h d -> (l a h d)").rearrange("(o i) -> o i", o=128)
                shift_jobs.append((b_, a_))

        def issue_shift(n):
            for _ in range(n):
                if shift_jobs:
                    b_, a_ = shift_jobs.pop(0)
                    S.dma('sp', b_, a_)

        WBUF = Buf("wscratch", None)

        def precast():
            for nm, Wf in wb.items():
                K_, N_ = Wf.shape
                for c0 in range(0, N_, 512):
                    w = min(512, N_ - c0)
                    for k0 in range(0, K_, 1024):
                        kc = min(8, (K_ - k0) // 128)
                        wu = WR[wstate['i'] % NW]
                        wstate['i'] += 1
                        S.dma('pool', wu[:, 0:kc, 0:w], Wf[k0:k0 + kc * 128, c0:c0 + w].rearrange("(k p) n -> p k n", p=128),
                              writes=[wu])
                        S.dma('sp', wscr[unit_ix[(nm, k0, c0)]][:, 0:kc * w].rearrange("p (k n) -> p k n", k=kc), wu[:, 0:kc, 0:w],
                              reads=[wu], writes=[WBUF], acc_w=True)

        def load_unit(nm, k0, kc, c0, w):
            wu = WR[wstate['i'] % NW]
            wstate['i'] += 1
            src = wscr[unit_ix[(nm, k0, c0)]][:, 0:kc * w].rearrange("p (k n) -> p k n", k=kc)
            S.dma('pool', wu[:, 0:kc, 0:w], src, reads=[WBUF], writes=[wu])
            return wu

        precast()

        def to_fm(src, nt, KC=8, dst=None, bf=False, dsts=None, c0=0):
            dst = dst or XT
            for half in range((KC + 3) // 4):
                n = min(4, KC - 4 * half)
                bank = PB[2 + half % 2]
                if bf:
                    pv = bank[:].bitcast(BF16).rearrange("p (a b) -> p a b", a=8)
                    idn = cb(M_ID)[0:nt, 0:nt]
                else:
                    pv = bank[:].rearrange("p (a b) -> p a b", a=4)
                    idn = cs(C_ID)[0:nt, 0:nt]
                for j in range(n):
                    kc = 4 * half + j
                    pe(lambda e: e.transpose(pv[:, j, 0:nt], src[0:nt, kc * 128:(kc + 1) * 128], idn),
                       [src, CB if bf else CST], [bank])
                if dsts is not None:
                    dbuf = dsts[half][0]
                    o = dsts[half][1][:, 0:n, c0:c0 + nt]
                else:
                    dbuf = dst
                    o = dst[:, 4 * half:4 * half + n, 0:nt]
                if half % 2 == 0:
                    act(lambda e: e.copy(o, pv[:, 0:n, 0:nt]), [bank], [dbuf])
                else:
                    dve(lambda e: e.tensor_copy(o, pv[:, 0:n, 0:nt]), [bank], [dbuf])

        lin = {'i': 0, 'j': 0}

        def linear_tm(xT, KC, Wd, N, nt, consumer):
            for si, c0 in enumerate(range(0, N, 512)):
                w = min(512, N - c0)
                bank = PB[lin['i'] % 2]
                lin['i'] += 1
                wu = load_unit(Wd, 0, KC, c0, w)
                for kc in range(KC):
                    pe(lambda e: e.matmul(bank[0:nt, 0:w], xT[:, kc, 0:nt], wu[:, kc, 0:w],
                                          start=(kc == 0), stop=(kc == KC - 1)), [xT, wu], [bank])
                consumer(si, c0, w, bank, bank[0:nt, 0:w])

        def resid_consumer(nt):
            def f(si, c0, w, bank, ap):
                dve(lambda e: e.scalar_tensor_tensor(X[0:nt, c0:c0 + w], X[0:nt, c0:c0 + w], ALPHA, ap,
                                                     ALU.mult, ALU.add), [X, bank], [X])
            return f

        lnpf = {'k': None}

        def prefetch_ln(which, layer):
            S.dma('sp', T3[:], ln_g[which][layer].partition_broadcast(128), writes=[T3])
            S.dma('sp', T4[:], ln_b[which][layer].partition_broadcast(128), writes=[T4])
            lnpf['k'] = (which, layer)

        def layer_norm(nt, which, layer, load=True):
            G, B = T3, T4
            if lnpf['k'] == (which, layer):
                load = False
            lnpf['k'] = None
            if load:
                S.dma('sp', G[:], ln_g[which][layer].partition_broadcast(128), writes=[G])
                S.dma('sp', B[:], ln_b[which][layer].partition_broadcast(128), writes=[B])
            st = sm[0:nt, 0:12].rearrange("p (a b) -> p a b", a=2)
            for hf in range(2):
                dve(lambda e: e.bn_stats(st[:, hf, :], X[0:nt, hf * 512:(hf + 1) * 512]), [X], [sm])
            dve(lambda e: e.bn_aggr(sm[0:nt, 12:14], st), [sm], [sm])
            dve(lambda e: e.tensor_scalar_add(sm[0:nt, 14:15], sm[0:nt, 13:14], LN_EPS), [sm], [sm])
            act(lambda e: e.activation(sm[0:nt, 14:15], sm[0:nt, 14:15], AF.Sqrt), [sm], [sm])
            dve(lambda e: e.reciprocal(sm[0:nt, 15:16], sm[0:nt, 14:15]), [sm], [sm])
            dve(lambda e: e.scalar_tensor_tensor(X[0:nt, :], X[0:nt, :], sm[0:nt, 12:13], G[0:nt, :],
                                                 ALU.subtract, ALU.mult), [X, sm, G], [X])
            dve(lambda e: e.scalar_tensor_tensor(X[0:nt, :], X[0:nt, :], sm[0:nt, 15:16], B[0:nt, :],
                                                 ALU.mult, ALU.add), [X, sm, B], [X])

        def mlp(nsub, nt, layer):
            NTOK = (nsub - 1) * 128 + nt
            xtb = [(QR, QR[:].bitcast(BF16).rearrange("p (k t) -> p k t", k=4)),
                   (SIGO, SIGO[:].bitcast(BF16).rearrange("p (k t) -> p k t", k=4))]
            htb = [(HK, HK[:].bitcast(BF16).rearrange("p (k t) -> p k t", k=4)),
                   (HG, HG[:].bitcast(BF16).rearrange("p (k t) -> p k t", k=4))]
            for st in range(nsub):
                X.set(XS[st])
                to_fm(X, nt, dsts=xtb, c0=st * 128)
                act(lambda e: e.mul(X[0:nt, :], X[0:nt, :], ALPHA), [X], [X])
            for grp in range(4):
                for u in range(2):
                    wu = load_unit("w1_%d" % layer, 0, 8, grp * 1024 + u * 512, 512)
                    for j in range(4):
                        bank = PB[4 + (lin['i'] % 4)]
                        lin['i'] += 1
                        for kc in range(8):
                            pe(lambda e: e.matmul(bank[:, 0:NTOK], wu[:, kc, j * 128:(j + 1) * 128], xtb[kc // 4][1][:, kc % 4, 0:NTOK],
                                                  start=(kc == 0), stop=(kc == 7)), [wu, xtb[kc // 4][0]], [bank])
                        tv = T1 if j % 2 == 0 else T2
                        act(lambda e: e.activation(tv[:, 0:NTOK], bank[:, 0:NTOK], AF.Relu), [bank], [tv])
                        dve(lambda e: e.tensor_mul(htb[u][1][:, j, 0:NTOK], tv[:, 0:NTOK], tv[:, 0:NTOK]), [tv], [htb[u][0]])
                for sl in range(2):
                    wu = load_unit("w2_%d" % layer, grp * 1024, 8, sl * 512, 512)
                    for st in range(nsub):
                        X.set(XS[st])
                        bank = PB[lin['j'] % 4]
                        lin['j'] += 1
                        for kc in range(8):
                            pe(lambda e: e.matmul(bank[0:nt, :], htb[kc // 4][1][:, kc % 4, st * 128:st * 128 + nt], wu[:, kc, :],
                                                  start=(kc == 0), stop=(kc == 7)), [htb[kc // 4][0], wu], [bank])
                        dve(lambda e: e.tensor_add(X[0:nt, sl * 512:(sl + 1) * 512], X[0:nt, sl * 512:(sl + 1) * 512], bank[0:nt, :]),
                            [X, bank], [X])
            for st in range(nsub):
                X.set(XS[st])
                layer_norm(nt, 1, layer, load=(st == 0))

        def rms_gate_out(src, nh, hd, nt, NGB, gate, wout):
            S.dma('sp', T2[:], NGB.partition_broadcast(128), writes=[T2])
            sv = src[0:nt, :].rearrange("p (h d) -> p h d", h=nh)
            t3 = T3[0:nt, :].rearrange("p (h d) -> p h d", h=nh)
            dve(lambda e: e.tensor_mul(T3[0:nt, :], src[0:nt, :], src[0:nt, :]), [src], [T3])
            dve(lambda e: e.tensor_reduce(sm[0:nt, 16:16 + nh], t3, AX.X, ALU.add), [T3], [sm])
            dve(lambda e: e.tensor_scalar(sm[0:nt, 16:16 + nh], sm[0:nt, 16:16 + nh], 1.0 / hd, NORM_EPS,
                                          ALU.mult, ALU.add), [sm], [sm])
            act(lambda e: e.activation(sm[0:nt, 16:16 + nh], sm[0:nt, 16:16 + nh], AF.Sqrt), [sm], [sm])
            dve(lambda e: e.reciprocal(sm[0:nt, 32:32 + nh], sm[0:nt, 16:16 + nh]), [sm], [sm])
            dve(lambda e: e.tensor_mul(t3, sv, sm[0:nt, 32:32 + nh].unsqueeze(2).broadcast_to([nt, nh, hd])),
                [src, sm], [T3])
            dve(lambda e: e.tensor_mul(T3[0:nt, :], T3[0:nt, :], T2[0:nt, :]), [T3, T2], [T3])
            dve(lambda e: e.tensor_mul(HB[0:nt, :], T3[0:nt, :], gate[0:nt, :]), [T3, gate], [HB])
            prefetch_ln(0, 0 if wout == "ml_wout" else 2)
            to_fm(HB, nt, bf=True)
            linear_tm(XT, 8, wout, D, nt, resid_consumer(nt))

        def load_rope(row0, nt):
            S.dma('sp', ROPE[0:nt, :], ropet[row0:row0 + nt, :] if nt > 1 else ropet[row0:row0 + 1, :],
                  writes=[ROPE])

        def rope_apply(dst, ap, nh, nt, bank):
            xv = ap.rearrange("p (h d) -> p h d", h=nh)
            xh = ap.rearrange("p (h a d) -> p h a d", h=nh, a=2)
            t4 = T4[0:nt, 0:nh * 64].rearrange("p (h a d) -> p h a d", h=nh, a=2)
            t4f = T4[0:nt, 0:nh * 64].rearrange("p (h d) -> p h d", h=nh)
            cc = ROPE[0:nt, 0:64].unsqueeze(1).broadcast_to([nt, nh, 64])
            s1 = ROPE[0:nt, 64:96].unsqueeze(1).broadcast_to([nt, nh, 32])
            s2 = ROPE[0:nt, 96:128].unsqueeze(1).broadcast_to([nt, nh, 32])
            dve(lambda e: e.tensor_mul(t4[:, :, 0, :], xh[:, :, 1, :], s1), [bank, ROPE], [T4])
            dve(lambda e: e.tensor_mul(t4[:, :, 1, :], xh[:, :, 0, :], s2), [bank, ROPE], [T4])
            dve(lambda e: e.tensor_mul(dst, xv, cc), [bank, ROPE], [dst_buf[0]])
            dve(lambda e: e.tensor_add(dst, dst, t4f), [T4, dst_buf[0]], [dst_buf[0]])

        dst_buf = [None]

        C_AUG = sb("C_AUG", [128, 4, 257], F32); C_BF = sb("C_BF", [128, 4, 257], BF16)
        M_BC = sb("M_BC", [128, 4], F32)
        K_BF = sb("K_BF", [128, 512], BF16)
        V_AUG = sb("V_AUG", [128, 4, 257], BF16)
        SIGO = sb("SIGO", [128, D], F32)
        GATES = sb("GATES", [128, 8], F32)
        mls = sb("mls", [128, 64], F32)
        UT = sb("UT", [4, 128], F32); MROWS = sb("MROWS", [4, 128], F32); MR4 = sb("MR4", [4, 4, 128], F32)
        WT = sb("WT", [128, 4, 128], F32); EE = sb("EE", [128, 4, 128], F32)
        dve(lambda e: e.memset(V_AUG[:], 1.0), [], [V_AUG])
        dve(lambda e: e.memset(C_AUG[:], 0.0), [], [C_AUG])
        dve(lambda e: e.memset(M_BC[:], 0.0), [], [M_BC])

        def mlstm_proj(nt):
            def cons(si, c0, w, bank, ap):
                if si == 0:
                    act(lambda e: e.copy(Q_TM[0:nt, :], ap), [bank], [Q_TM])
                elif si == 1:
                    act(lambda e: e.mul(K_TM[0:nt, :], ap, 128.0 ** -0.5), [bank], [K_TM])
                    dve(lambda e: e.tensor_copy(K_BF[0:nt, :], K_TM[0:nt, :]), [K_TM], [K_BF])
                elif si in (2, 3):
                    hh = 2 * (si - 2)
                    act(lambda e: e.copy(V_AUG[0:nt, hh:hh + 2, 0:256], ap.rearrange("p (h d) -> p h d", h=2)),
                        [bank], [V_AUG])
                elif si in (4, 5):
                    cc = (si - 4) * 512
                    act(lambda e: e.activation(SIGO[0:nt, cc:cc + 512], ap, AF.Sigmoid), [bank], [SIGO])
                else:
                    dve(lambda e: e.tensor_add(GATES[0:nt, :], ap, MLBG[0:nt, :]), [bank, MLBG], [GATES])
            linear_tm(XT, 8, "ml_win", 3072, nt, cons)
            gbank = PB[lin['i'] % 2]
            lin['i'] += 1
            for kc in range(8):
                pe(lambda e: e.matmul(gbank[0:nt, 0:8], XT[:, kc, 0:nt], GW[:, kc, :], start=(kc == 0), stop=(kc == 7)),
                   [XT, GW], [gbank])
            cons(6, 3072, 8, gbank, gbank[0:nt, 0:8])
            act(lambda e: e.activation(mls[0:nt, 0:4], GATES[0:nt, 4:8], AF.Exp, scale=-1.0), [GATES], [mls])
            dve(lambda e: e.tensor_scalar_add(mls[0:nt, 0:4], mls[0:nt, 0:4], 1.0), [mls], [mls])
            act(lambda e: e.activation(mls[0:nt, 0:4], mls[0:nt, 0:4], AF.Ln), [mls], [mls])
            dve(lambda e: e.tensor_scalar_mul(mls[0:nt, 0:4], mls[0:nt, 0:4], -1.0), [mls], [mls])

        def mlstm_prompt(t):
            nt = 128
            to_fm(X, nt)
            mlstm_proj(nt)
            for src, dstT in ((Q_TM, QT), (K_BF, KT)):
                bank = PB[4]
                pv = bank[:].bitcast(BF16).rearrange("p (a b) -> p a b", a=8)
                for h in range(4):
                    pe(lambda e: e.transpose(pv[:, h, :], src[:, h * 128:(h + 1) * 128], cb(M_ID)), [src, CB], [bank])
                dve(lambda e: e.tensor_copy(dstT[:], pv[:, 0:4, :]), [bank], [dstT])
            b5 = PB[5]
            pe(lambda e: e.matmul(b5[:, 0:4], cs(C_U), mls[:, 0:4], start=True, stop=True), [CST, mls], [b5])
            pe(lambda e: e.matmul(b5[:, 8:12], cs(C_ONES), mls[:, 0:4], start=True, stop=True), [CST, mls], [b5])
            dve(lambda e: e.tensor_copy(mls[:, 4:8], b5[:, 0:4]), [b5], [mls])
            dve(lambda e: e.tensor_copy(mls[:, 8:12], b5[:, 8:12]), [b5], [mls])
            dve(lambda e: e.tensor_sub(mls[:, 12:16], GATES[:, 0:4], mls[:, 4:8]), [GATES, mls], [mls])
            pe(lambda e: e.transpose(b5[0:4, 128:256], mls[:, 12:16], cs(C_ID)), [mls, CST], [b5])
            act(lambda e: e.copy(UT[:], b5[0:4, 128:256]), [b5], [UT])
            dve(lambda e: e.tensor_mul(mls[0:4, 16:20], M_BC[0:4, :], cs(C_ID)[0:4, 0:4]), [M_BC, CST], [mls])
            dve(lambda e: e.tensor_reduce(mls[0:4, 20:21], mls[0:4, 16:20], AX.X, ALU.add), [mls], [mls])
            dve(lambda e: e.tensor_tensor_scan(MROWS[:], UT[:], UT[:], mls[0:4, 20:21], ALU.max, ALU.max),
                [UT, mls], [MROWS])
            mb = PB[6]
            mbv = mb[:].rearrange("p (h t) -> p h t", h=4)
            dve(lambda e: e.tensor_mul(MR4[:], MROWS[:].unsqueeze(1).broadcast_to([4, 4, 128]),
                                       cs(C_ID)[0:4, 0:4].unsqueeze(2).broadcast_to([4, 4, 128])), [MROWS, CST], [MR4])
            for h in range(4):
                pe(lambda e: e.matmul(mbv[:, h, :], cs(C_ONES)[0:4, :], MR4[:, h, :],
                                      start=True, stop=True), [CST, MR4], [mb])
            pe(lambda e: e.transpose(b5[:, 16:20], MROWS[:], cs(C_ID)[0:4, 0:4]), [MROWS, CST], [b5])
            dve(lambda e: e.tensor_add(mls[:, 24:28], mls[:, 4:8], b5[:, 16:20]), [mls, b5], [mls])
            act(lambda e: e.activation(mls[:, 28:32], mls[:, 24:28], AF.Exp, scale=-1.0), [mls], [mls])
            for h in range(4):
                act(lambda e: e.activation(WT[:, h, :], mbv[:, h, :], AF.Exp, scale=-1.0, bias=mls[:, 12 + h:13 + h]),
                    [mb, mls], [WT])
                act(lambda e: e.activation(EE[:, h, :], mbv[:, h, :], AF.Exp, scale=-1.0, bias=M_BC[:, h:h + 1]),
                    [mb, M_BC], [EE])
            dve(lambda e: e.tensor_mul(KH[:], K_TM[:].rearrange("p (h d) -> p h d", h=4),
                                       WT[:, :, 127:128].broadcast_to([128, 4, 128])), [K_TM, WT], [KH])
            dve(lambda e: e.tensor_copy(mls[:, 32:36], EE[:, :, 127:128].rearrange("p h o -> p (h o)")), [EE], [mls])
            dve(lambda e: e.tensor_add(mls[:, 36:40], mls[:, 8:12], mbv[:, :, 127:128].rearrange("p h o -> p (h o)")),
                [mls, mb], [mls])
            dve(lambda e: e.tensor_mul(WT[:], WT[:], cb(M_U).unsqueeze(1).broadcast_to([128, 4, 128])), [WT, CB], [WT])
            sc = PB[7]
            scv = sc[:].rearrange("p (h t) -> p h t", h=4)
            for h in range(4):
                pe(lambda e: e.matmul(scv[:, h, :], KT[:, h, :], QT[:, h, :], start=True, stop=True), [KT, QT], [sc])
            dve(lambda e: e.tensor_mul(ATW[:], WT[:], scv), [WT, sc], [ATW])
            dve(lambda e: e.tensor_mul(QTS[:], QT[:], EE[:]), [QT, EE], [QTS])
            act(lambda e: e.copy(C_BF[:], C_AUG[:]), [C_AUG], [C_BF])
            nb = [PB[4], PB[5], PB[2], PB[3]]
            for h in range(4):
                o = nb[h][:, 0:257]
                pe(lambda e: e.matmul(o, ATW[:, h, :], V_AUG[:, h, :], start=True, stop=False), [ATW, V_AUG], [nb[h]])
                pe(lambda e: e.matmul(o, QTS[:, h, :], C_BF[:, h, :], start=False, stop=True), [QTS, C_BF], [nb[h]])
            for h in range(4):
                o = nb[h][:, 0:257]
                act(lambda e: e.activation(mls[:, 40 + h:41 + h], o[:, 256:257], AF.Abs), [nb[h]], [mls])
                dve(lambda e: e.tensor_max(mls[:, 40 + h:41 + h], mls[:, 40 + h:41 + h], mls[:, 28 + h:29 + h]), [mls], [mls])
            dve(lambda e: e.reciprocal(mls[:, 44:48], mls[:, 40:44]), [mls], [mls])
            for h in range(4):
                o = nb[h][:, 0:256]
                act(lambda e: e.activation(T1[:, h * 256:(h + 1) * 256], o, AF.Copy, scale=mls[:, 44 + h:45 + h]),
                    [nb[h], mls], [T1])
            ub = [PB[6], PB[7], PB[2], PB[3]]
            for h in range(4):
                o = ub[h][:, 0:257]
                pe(lambda e: e.matmul(o, KH[:, h, :], V_AUG[:, h, :], start=True, stop=True), [KH, V_AUG], [ub[h]])
            for h in range(4):
                o = ub[h][:, 0:257]
                dve(lambda e: e.scalar_tensor_tensor(C_AUG[:, h, :], C_AUG[:, h, :], mls[:, 32 + h:33 + h], o,
                                                     ALU.mult, ALU.add), [C_AUG, mls, ub[h]], [C_AUG])
            dve(lambda e: e.tensor_copy(M_BC[:], mls[:, 36:40]), [mls], [M_BC])
            rms_gate_out(T1, 4, 256, nt, ML_NG, SIGO, "ml_wout")
            if t == NT - 1:
                S.dma('sp', oC_p.rearrange("h d v -> d h v"), C_AUG[:, :, 0:256], reads=[C_AUG])
                S.dma('sp', on_p.rearrange("h d -> d h"), C_AUG[:, :, 256:257].rearrange("p h o -> p (h o)"),
                      reads=[C_AUG], allow_slow_non_contiguous=True)
                S.dma('sp', om_p, M_BC[0:1, :], reads=[M_BC])

        RING = [2, 5, 17]
        KTR = [[sb("ktr%d_%d" % (g, i), [128, 4, 128], BF16) for i in range(RING[g])] for g in range(3)]
        VR = [[sb("vr%d_%d" % (g, i), [128, 8, 65], BF16) for i in range(RING[g])] for g in range(3)]
        for g in range(3):
            for v in VR[g]:
                dve(lambda e: e.memset(v[:], 1.0), [], [v])
        QTG = [sb("qtg%d" % g, [128, 4, 128], BF16) for g in range(3)]
        QR = sb("QR", [128, 1024], F32)
        QRB = sb("QRB", [128, 1024], BF16)
        KR = sb("KR", [128, 512], F32); KRB = sb("KRB", [128, 512], BF16)
        VF = sb("VF", [128, 512], F32)
        PT = [sb("PT%d" % i, [128, 4, 128], BF16) for i in range(4)]
        OALL = sb("OALL", [128, 16, 65], F32)
        YB = sb("YB", [128, 1024], BF16)
        Q_TM = KRB; K_TM = KR; QT, KT, KH = QTG; ATW, QTS = PT[0], PT[1]
        MASKS = {1: (M_U, None, M_L), 4: (M_U4, M_R4, M_L4), 16: (M_U16, M_R16, M_L16)}
        att = {'i': 0}

        def attn_run(heads):
            chunks = []
            for hi, (units, h) in enumerate(heads):
                n = len(units)
                for c0 in range(0, n, 4):
                    chunks.append((units[c0:c0 + 4], c0 == 0, c0 + 4 >= n, PB[6 + hi % 2], h))
            sbanks = [PB[0], PB[1], PB[4], PB[5]]
            N = len(chunks)

            def st_score(i):
                bank = sbanks[i % 4]
                pv = bank[:].rearrange("p (a b) -> p a b", a=4)
                for j, u in enumerate(chunks[i][0]):
                    pe(lambda e: e.matmul(pv[:, j, :], u[0], u[1], start=True, stop=True), [u[4], u[5]], [bank])

            def st_soft(i):
                bank = sbanks[i % 4]
                pt = PT[i % 4]
                pv = bank[:].rearrange("p (a b) -> p a b", a=4)
                us = chunks[i][0]
                m = len(us)
                act(lambda e: e.activation(pt[:, 0:m, :], pv[:, 0:m, :], AF.Exp, scale=0.125), [bank], [pt])
                j0 = 0
                while j0 < m:
                    j1 = j0
                    while j1 < m and us[j1][3] == us[j0][3]:
                        j1 += 1
                    mk = us[j0][3]
                    if mk is not None:
                        dve(lambda e: e.tensor_mul(pt[:, j0:j1, :], pt[:, j0:j1, :],
                                                   cb(mk).unsqueeze(1).broadcast_to([128, j1 - j0, 128])), [pt, CB], [pt])
                    j0 = j1

            def st_pv(i):
                us, first, last, accb, h = chunks[i]
                pt = PT[i % 4]
                m = len(us)
                for j, u in enumerate(us):
                    pe(lambda e: e.matmul(accb[:, 0:65], pt[:, j, :], u[2], start=(first and j == 0), stop=(last and j == m - 1)),
                       [pt, u[6]], [accb])
                if last:
                    dve(lambda e: e.tensor_copy(OALL[:, h, :], accb[:, 0:65]), [accb], [OALL])

            for i in range(N + 2):
                if i < N:
                    st_score(i)
                if 0 <= i - 1 < N:
                    st_soft(i - 1)
                if 0 <= i - 2 < N:
                    st_pv(i - 2)

        def dil_prompt(t):
            nt = 128
            to_fm(X, nt)
            load_rope(t * 128, nt)

            def cons(si, c0, w, bank, ap):
                g, j = si // 3, si % 3
                slot = t % RING[g]
                if j == 0:
                    dst_buf[0] = QR
                    rope_apply(QR[0:nt, 0:512].rearrange("p (h d) -> p h d", h=8), ap, 8, nt, bank)
                    dve(lambda e: e.tensor_copy(QRB[:, 0:512], QR[:, 0:512]), [QR], [QRB])
                    tb = PB[2]
                    pv = tb[:].bitcast(BF16).rearrange("p (a b) -> p a b", a=8)
                    for pr in range(4):
                        pe(lambda e: e.transpose(pv[:, pr, :], QRB[:, pr * 128:(pr + 1) * 128], cb(M_ID)), [QRB, CB], [tb])
                    act(lambda e: e.copy(QTG[g][:], pv[:, 0:4, :]), [tb], [QTG[g]])
                elif j == 1:
                    dst_buf[0] = KR
                    rope_apply(KR[0:nt, :].rearrange("p (h d) -> p h d", h=8), ap, 8, nt, bank)
                    dve(lambda e: e.tensor_copy(KRB[:], KR[:]), [KR], [KRB])
                    tb = PB[3]
                    pv = tb[:].bitcast(BF16).rearrange("p (a b) -> p a b", a=8)
                    for pr in range(4):
                        pe(lambda e: e.transpose(pv[:, pr, :], KRB[:, pr * 128:(pr + 1) * 128], cb(M_ID)), [KRB, CB], [tb])
                    act(lambda e: e.copy(KTR[g][slot][:], pv[:, 0:4, :]), [tb], [KTR[g][slot]])
                    win = min(DIL[g][0], SEQ)
                    if t * 128 >= SEQ - win:
                        r0 = t * 128 - (SEQ - win)
                        S.dma('sp', okv_p[g][r0:r0 + 128, 0].rearrange("l h d -> l (h d)"), KR[:], reads=[KR])
                else:
                    act(lambda e: e.copy(VF[:], ap), [bank], [VF])
                    dve(lambda e: e.tensor_copy(VR[g][slot][:, :, 0:64], VF[:].rearrange("p (h d) -> p h d", h=8)),
                        [VF], [VR[g][slot]])
                    win = min(DIL[g][0], SEQ)
                    if t * 128 >= SEQ - win:
                        r0 = t * 128 - (SEQ - win)
                        S.dma('sp', okv_p[g][r0:r0 + 128, 1].rearrange("l h d -> l (h d)"), VF[:], reads=[VF])
            linear_tm(XT, 8, "dl_win", 4608, nt, cons)
            heads = []
            for h in range(8):
                pr, e0 = h // 2, (h % 2) * 64
                units = []
                for g in (2, 1, 0):
                    win, dil = DIL[g]
                    nd = win // 128
                    mu, mm, ml = MASKS[dil]
                    for dlt in range(0, nd + 1):
                        if t - dlt < 0:
                            continue
                        slot = (t - dlt) % RING[g]
                        mk = mu if dlt == 0 else (ml if dlt == nd else mm)
                        units.append((KTR[g][slot][e0:e0 + 64, pr, :], QTG[g][e0:e0 + 64, pr, :], VR[g][slot][:, h, :],
                                      mk, KTR[g][slot], QTG[g], VR[g][slot]))
                mids = [u for u in units if u[3] in (M_R16, M_R4)]
                mids.sort(key=lambda u: 0 if u[3] == M_R16 else 1)
                units = mids + [u for u in units if u[3] not in (M_R16, M_R4)]
                heads.append((units, h))
            attn_run(heads)
            dve(lambda e: e.reciprocal(sm[:, 48:56], OALL[:, 0:8, 64:65].rearrange("p h o -> p (h o)")), [OALL], [sm])
            dve(lambda e: e.tensor_mul(YB[:, 0:512].rearrange("p (h d) -> p h d", h=8), OALL[:, 0:8, 0:64],
                                       sm[:, 48:56].unsqueeze(2).broadcast_to([128, 8, 64])), [OALL, sm], [YB])
            prefetch_ln(0, 1)
            to_fm(YB, nt, KC=4, bf=True)
            linear_tm(XT, 4, "dl_wout", D, nt, resid_consumer(nt))

        SKT = [sb("skt%d" % i, [128, 2, 128], BF16) for i in range(2)]
        SV = [sb("sv%d" % i, [128, 2, 65], BF16) for i in range(2)]
        for v in SV:
            dve(lambda e: e.memset(v[:], 1.0), [], [v])
        SQT = sb("SQT", [128, 8, 128], BF16)
        KDUP = sb("KDUP", [128, 2, 2, 64], BF16)

        def swa_proj(nt, pos_row):
            load_rope(pos_row, nt)

            def cons(si, c0, w, bank, ap):
                if si < 2:
                    dst_buf[0] = QR
                    rope_apply(QR[0:nt, si * 512:(si + 1) * 512].rearrange("p (h d) -> p h d", h=8), ap, 8, nt, bank)
                else:
                    dst_buf[0] = KR
                    rope_apply(KR[0:nt, 0:128].rearrange("p (h d) -> p h d", h=2), ap[:, 0:128], 2, nt, bank)
                    act(lambda e: e.copy(VF[0:nt, 0:128], ap[:, 128:256]), [bank], [VF])
            linear_tm(XT, 8, "sw_win", 1280, nt, cons)

        def swa_prompt(t):
            nt = 128
            to_fm(X, nt)
            swa_proj(nt, t * 128)
            slot = t % 2
            dve(lambda e: e.tensor_copy(QRB[:], QR[:]), [QR], [QRB])
            for half in range(2):
                tb = PB[2 + half]
                pv = tb[:].bitcast(BF16).rearrange("p (a b) -> p a b", a=8)
                for pr in range(4):
                    c = (half * 4 + pr) * 128
                    pe(lambda e: e.transpose(pv[:, pr, :], QRB[:, c:c + 128], cb(M_ID)), [QRB, CB], [tb])
                act(lambda e: e.copy(SQT[:, half * 4:half * 4 + 4, :], pv[:, 0:4, :]), [tb], [SQT])
            kr = KR[:, 0:128].rearrange("p (h d) -> p h d", h=2)
            for dup in range(2):
                dve(lambda e: e.tensor_copy(KDUP[:, :, dup, :], kr), [KR], [KDUP])
            tb = PB[2]
            pv = tb[:].bitcast(BF16).rearrange("p (a b) -> p a b", a=8)
            for kvh in range(2):
                pe(lambda e: e.transpose(pv[:, kvh, :], KDUP[:, kvh, :, :].rearrange("p a d -> p (a d)"), cb(M_ID)),
                   [KDUP, CB], [tb])
            act(lambda e: e.copy(SKT[slot][:], pv[:, 0:2, :]), [tb], [SKT[slot]])
            dve(lambda e: e.tensor_copy(SV[slot][:, :, 0:64], VF[:, 0:128].rearrange("p (h d) -> p h d", h=2)),
                [VF], [SV[slot]])
            if t == NT - 1:
                S.dma('sp', osw_p[:, 0].rearrange("l h d -> l (h d)"), KR[:, 0:128], reads=[KR])
                S.dma('sp', osw_p[:, 1].rearrange("l h d -> l (h d)"), VF[:, 0:128], reads=[VF])
            heads = []
            for h in range(16):
                kvh, pr, e0 = h // 8, h // 2, (h % 2) * 64
                units = []
                for dlt in (0, 1):
                    if t - dlt < 0:
                        continue
                    sl = (t - dlt) % 2
                    units.append((SKT[sl][e0:e0 + 64, kvh, :], SQT[e0:e0 + 64, pr, :], SV[sl][:, kvh, :],
                                  M_U if dlt == 0 else M_L, SKT[sl], SQT, SV[sl]))
                heads.append((units, h))
            attn_run(heads)
            swa_finish(nt)

        def swa_finish(nt):
            dve(lambda e: e.tensor_add(sm[0:nt, 48:64], OALL[0:nt, :, 64:65].rearrange("p h o -> p (h o)"), ESINK[0:nt, :]),
                [OALL, ESINK], [sm])
            dve(lambda e: e.reciprocal(sm[0:nt, 48:64], sm[0:nt, 48:64]), [sm], [sm])
            dve(lambda e: e.tensor_mul(YB[0:nt, :].rearrange("p (h d) -> p h d", h=16), OALL[0:nt, :, 0:64],
                                       sm[0:nt, 48:64].unsqueeze(2).broadcast_to([nt, 16, 64])), [OALL, sm], [YB])
            prefetch_ln(0, 3)
            to_fm(YB, nt, bf=True)
            linear_tm(XT, 8, "sw_wout", D, nt, resid_consumer(nt))

        HS = sb("HS", [128, 8, 128], F32)
        HSB = [sb("HSB%d" % c, [128, 8, 128], BF16) for c in range(4)]
        QTM = [sb("QTM%d" % c, [128, 8, 128], BF16) for c in range(4)]
        HQ = QR; HLF = SIGO; HK = sb("HK", [128, D], F32)
        HV = HB; HG = sb("HG", [128, D], F32)
        HQB = QRB; HKB = YB; HKH = T4
        HKM2 = [sb("HKM%d" % c, [128, D], BF16) for c in range(2)]
        HKM = [HKM2[0], HKM2[1], HKM2[0], HKM2[1]]
        HQT = SQT; HKT = sb("HKT", [128, 8, 128], BF16)
        HDEC = sb("HDEC", [128, 8, 4], F32)
        HAT = sb("HAT", [128, 8, 128], BF16)
        dve(lambda e: e.memset(HS[:], 0.0), [], [HS])
        for c in range(4):
            dve(lambda e: e.memset(QTM[c][:], 0.0), [], [QTM[c]])

        def hgrn_proj(nt):
            S.dma('sp', T2[:], hg_bf.partition_broadcast(128), writes=[T2])
            def cons(si, c0, w, bank, ap):
                cc = (si % 2) * 512
                if si < 2:
                    act(lambda e: e.activation(HQ[0:nt, cc:cc + 512], ap, AF.Silu), [bank], [HQ])
                elif si < 4:
                    dve(lambda e: e.tensor_add(T1[0:nt, cc:cc + 512], ap, BFB[0:nt, cc:cc + 512]), [bank, BFB], [T1])
                    act(lambda e: e.activation(T1[0:nt, cc:cc + 512], T1[0:nt, cc:cc + 512], AF.Sigmoid), [T1], [T1])
                    dve(lambda e: e.tensor_scalar(T1[0:nt, cc:cc + 512], T1[0:nt, cc:cc + 512], -1.0, 1.0, ALU.mult, ALU.add),
                        [T1], [T1])
                    dve(lambda e: e.tensor_mul(T1[0:nt, cc:cc + 512], T1[0:nt, cc:cc + 512], OML[0:nt, cc:cc + 512]),
                        [T1, OML], [T1])
                    dve(lambda e: e.tensor_scalar(T1[0:nt, cc:cc + 512], T1[0:nt, cc:cc + 512], -1.0, 1.0, ALU.mult, ALU.add),
                        [T1], [T1])
                    act(lambda e: e.activation(HLF[0:nt, cc:cc + 512], T1[0:nt, cc:cc + 512], AF.Ln), [T1], [HLF])
                    dve(lambda e: e.tensor_scalar(HK[0:nt, cc:cc + 512], T1[0:nt, cc:cc + 512], -1.0, 1.0,
                                                  ALU.mult, ALU.add), [T1], [HK])
                elif si < 6:
                    act(lambda e: e.copy(HV[0:nt, cc:cc + 512], ap), [bank], [HV])
                else:
                    act(lambda e: e.activation(HG[0:nt, cc:cc + 512], ap, AF.Sigmoid), [bank], [HG])
            linear_tm(XT, 8, "hg_win", 4096, nt, cons)

        def hgrn_prompt(t):
            nt = 128
            to_fm(X, nt)
            hgrn_proj(nt)
            for hf in range(2):
                cc = hf * 512
                bb, be = PB[4 + hf], PB[6 + hf]
                pe(lambda e: e.matmul(bb[:, :], cs(C_BU), HLF[:, cc:cc + 512], start=True, stop=True), [CST, HLF], [bb])
                pe(lambda e: e.matmul(be[:, :], cs(C_BO), HLF[:, cc:cc + 512], start=True, stop=True), [CST, HLF], [be])
                act(lambda e: e.copy(T1[:, cc:cc + 512], bb[:, :]), [bb], [T1])
                dve(lambda e: e.tensor_sub(T2[:, cc:cc + 512], be[:, :], T1[:, cc:cc + 512]), [be, T1], [T2])
            act(lambda e: e.activation(T3[:], T1[:], AF.Exp), [T1], [T3])
            dve(lambda e: e.tensor_mul(HQB[:], HQ[:], T3[:]), [HQ, T3], [HQB])
            act(lambda e: e.activation(T3[:], T1[:], AF.Exp, scale=-1.0), [T1], [T3])
            dve(lambda e: e.tensor_mul(HKB[:], HK[:], T3[:]), [HK, T3], [HKB])
            act(lambda e: e.activation(T3[:], T2[:], AF.Exp), [T2], [T3])
            dve(lambda e: e.tensor_mul(HKH[:], HK[:], T3[:]), [HK, T3], [HKH])
            for src, dstT in ((HQB, HQT), (HKB, HKT)):
                to_fm(src, nt, bf=True, dst=dstT)
            for c in range(4):
                dve(lambda e: e.tensor_copy(QTM[c][:, :, 32 * c:32 * c + 32], HQT[:, :, 32 * c:32 * c + 32]), [HQT], [QTM[c]])
            db = PB[4]
            dbv = db[:, 0:32].rearrange("p (h c) -> p h c", h=8)
            for h in range(8):
                pe(lambda e: e.matmul(dbv[:, h, :], HLF[:, h * 128:(h + 1) * 128], CST[:, C_CI:C_CI + 4],
                                      start=True, stop=True), [HLF, CST], [db])
            act(lambda e: e.activation(HDEC[:], dbv, AF.Exp), [db], [HDEC])
            for c in range(4):
                dve(lambda e: e.tensor_scalar_mul(HKM[c][:], HKH[:], CST[:, C_CI + c:C_CI + c + 1]), [HKH, CST], [HKM[c]])
                act(lambda e: e.copy(HSB[c][:], HS[:]), [HS], [HSB[c]])
                ub = [PB[6], PB[7]]
                for h in range(8):
                    o = ub[h // 4][:, (h % 4) * 128:(h % 4) * 128 + 128]
                    pe(lambda e: e.matmul(o, HKM[c][:, h * 128:(h + 1) * 128], HV[:, h * 128:(h + 1) * 128],
                                          start=True, stop=True), [HKM[c], HV], [ub[h // 4]])
                dve(lambda e: e.tensor_mul(HS[:], HS[:], HDEC[:, :, c:c + 1].broadcast_to([128, 8, 128])), [HS, HDEC], [HS])
                for hf in range(2):
                    dve(lambda e: e.tensor_add(HS[:, 4 * hf:4 * hf + 4, :], HS[:, 4 * hf:4 * hf + 4, :],
                                               ub[hf][:, :].rearrange("p (h v) -> p h v", h=4)), [HS, ub[hf]], [HS])
            for hf in range(2):
                sbk = PB[4 + hf]
                sv = sbk[:].rearrange("p (h t) -> p h t", h=4)
                for j in range(4):
                    h = 4 * hf + j
                    pe(lambda e: e.matmul(sv[:, j, :], HKT[:, h, :], HQT[:, h, :], start=True, stop=True), [HKT, HQT], [sbk])
                dve(lambda e: e.tensor_mul(HAT[:, 4 * hf:4 * hf + 4, :], sv, cs(C_BU).unsqueeze(1).broadcast_to([128, 4, 128])),
                    [sbk, CST], [HAT])
            for h in range(8):
                bank = PB[6 + h % 2]
                o = bank[:, 0:128]
                pe(lambda e: e.matmul(o, HAT[:, h, :], HV[:, h * 128:(h + 1) * 128], start=True, stop=False), [HAT, HV], [bank])
                for c in range(4):
                    pe(lambda e: e.matmul(o, QTM[c][:, h, :], HSB[c][:, h, :], start=False, stop=(c == 3)),
                       [QTM[c], HSB[c]], [bank])
                act(lambda e: e.copy(T1[:, h * 128:(h + 1) * 128], o), [bank], [T1])
            rms_gate_out(T1, 8, 128, nt, HG_NG, HG, "hg_wout")
            if t == NT - 1:
                S.dma('sp', oS_p.rearrange("h d v -> d h v"), HS[:], reads=[HS])

        mixers = [mlstm_prompt, dil_prompt, hgrn_prompt, swa_prompt]
        for mt in range(NT // NSUB):
            for st in range(NSUB):
                t = mt * NSUB + st
                S.dma('sp', XS[st][:], xp[t * 128:(t + 1) * 128, :], writes=[XS[st]])
            for layer in range(4):
                for st in range(NSUB):
                    X.set(XS[st])
                    lin['i'] = 0
                    mixers[layer](mt * NSUB + st)
                    layer_norm(128, 0, layer)
                mlp(NSUB, 128, layer)
            for st in range(NSUB):
                t = mt * NSUB + st
                S.dma('sp', yp[t * 128:(t + 1) * 128, :], XS[st][:], reads=[XS[st]])
            issue_shift(-(-4 * NS // (NT // NSUB)))

        nt = NS
        SSA = sb("SSA", [128, 128], F32); SSB = sb("SSB", [128, 128], F32); SSC = sb("SSC", [128, 128], F32)
        EP = lambda s: CST[:, C_EP + s * 16:C_EP + s * 16 + NS]
        idNS = cs(C_ID)[0:NS, 0:NS]

        def shift_copy(src, dst, L, rowlen):
            for s in range(NS):
                a = src[s, 1:L].rearrange("l a h d -> (l a h d)").rearrange("(o i) -> o i", o=128)
                b = dst[s, 0:L - 1].rearrange("l a h d -> (l a h d)").rearrange("(o i) -> o i", o=128)
                S.dma('sp', b, a)

        def mlstm_sample():
            to_fm(X, nt)
            mlstm_proj(nt)
            S.dma('sp', mls[0:nt, 4:8], st_m, writes=[mls])
            dve(lambda e: e.tensor_add(mls[0:nt, 8:12], mls[0:nt, 0:4], mls[0:nt, 4:8]), [mls], [mls])
            dve(lambda e: e.tensor_max(mls[0:nt, 12:16], mls[0:nt, 8:12], GATES[0:nt, 0:4]), [mls, GATES], [mls])
            S.dma('sp', om_s, mls[0:nt, 12:16], reads=[mls])
            dve(lambda e: e.tensor_sub(mls[0:nt, 16:20], mls[0:nt, 8:12], mls[0:nt, 12:16]), [mls], [mls])
            act(lambda e: e.activation(mls[0:nt, 16:20], mls[0:nt, 16:20], AF.Exp), [mls], [mls])
            dve(lambda e: e.tensor_sub(mls[0:nt, 20:24], GATES[0:nt, 0:4], mls[0:nt, 12:16]), [mls, GATES], [mls])
            act(lambda e: e.activation(mls[0:nt, 20:24], mls[0:nt, 20:24], AF.Exp), [mls], [mls])
            act(lambda e: e.activation(mls[0:nt, 28:32], mls[0:nt, 12:16], AF.Exp, scale=-1.0), [mls], [mls])
            dve(lambda e: e.tensor_mul(T2[0:nt, 0:512].rearrange("p (h d) -> p h d", h=4),
                                       K_TM[0:nt, :].rearrange("p (h d) -> p h d", h=4),
                                       mls[0:nt, 20:24].unsqueeze(2).broadcast_to([nt, 4, 128])), [K_TM, mls], [T2])
            VA = T3[0:nt, 0:1024].rearrange("p (h d) -> p h d", h=4)
            dve(lambda e: e.tensor_copy(VA, V_AUG[0:nt, :, 0:256]), [V_AUG], [T3])
            dsel = T4[0:nt, 0:NS * 4].rearrange("p (s h) -> p s h", h=4)
            dve(lambda e: e.tensor_mul(dsel, idNS.unsqueeze(2).broadcast_to([nt, NS, 4]),
                                       mls[0:nt, 16:20].unsqueeze(1).broadcast_to([nt, NS, 4])), [CST, mls], [T4])
            pb = PB[4]
            pe(lambda e: e.matmul(pb[:, 0:NS * 4], cs(C_ONES)[0:NS, :], T4[0:nt, 0:NS * 4], start=True, stop=True),
               [CST, T4], [pb])
            WPB = SSA
            dve(lambda e: e.tensor_copy(WPB[:, 0:NS * 4], pb[:, 0:NS * 4]), [pb], [WPB])
            dve(lambda e: e.tensor_copy(T1[0:nt, 0:512], Q_TM[0:nt, :]), [Q_TM], [T1])
            tb = PB[2]
            tv = tb[:].rearrange("p (a b) -> p a b", a=4)
            for h in range(4):
                pe(lambda e: e.transpose(tv[:, h, 0:nt], T1[0:nt, h * 128:(h + 1) * 128], idNS), [T1, CST], [tb])
            QF = SSB
            qf = QF[:, 0:4 * NS].rearrange("p (h s) -> p h s", h=4)
            act(lambda e: e.copy(qf, tv[:, :, 0:nt]), [tb], [QF])
            acc = [PB[0], PB[1], PB[6], PB[7]]
            CS_ = HK
            CSN = HG
            for s in range(NS):
                KM = VF
                dve(lambda e: e.tensor_scalar_mul(KM[0:nt, 0:512], T2[0:nt, 0:512], cs(C_ID)[0:nt, s:s + 1]), [T2, CST], [KM])
                QM = SSC
                qm = QM[:, 0:4 * NS].rearrange("p (h s) -> p h s", h=4)
                dve(lambda e: e.tensor_mul(qm, qf, CST[:, C_EP + s * 16:C_EP + s * 16 + NS].unsqueeze(1).broadcast_to([128, 4, NS])),
                    [QF, CST], [QM])
                for h in range(4):
                    S.dma('sp', CS_[:, 0:256], st_C[s, h], writes=[CS_])
                    S.dma('sp', CS_[:, 256:257], st_n[s, h].rearrange("(d o) -> d o", o=1), writes=[CS_])
                    ub = PB[5]
                    pe(lambda e: e.matmul(ub[:, 0:256], KM[0:nt, h * 128:(h + 1) * 128], VA[:, h, :], start=True, stop=True),
                       [KM, T3], [ub])
                    pe(lambda e: e.matmul(ub[:, 256:257], KM[0:nt, h * 128:(h + 1) * 128], cs(C_ONES)[0:nt, 0:1], start=True, stop=True),
                       [KM, CST], [ub])
                    dve(lambda e: e.scalar_tensor_tensor(CSN[:, 0:257], CS_[:, 0:257], WPB[:, s * 4 + h:s * 4 + h + 1],
                                                         ub[:, 0:257], ALU.mult, ALU.add), [CS_, WPB, ub], [CSN])
                    S.dma('sp', oC_s[s, h], CSN[:, 0:256], reads=[CSN])
                    S.dma('sp', on_s[s, h].rearrange("(d o) -> d o", o=1), CSN[:, 256:257], reads=[CSN])
                    pe(lambda e: e.matmul(acc[h][0:nt, 0:257], qm[:, h, :], CSN[:, 0:257], start=(s == 0), stop=(s == NS - 1)),
                       [QM, CSN], [acc[h]])
            for h in range(4):
                act(lambda e: e.activation(mls[0:nt, 40 + h:41 + h], acc[h][0:nt, 256:257], AF.Abs), [acc[h]], [mls])
                dve(lambda e: e.tensor_max(mls[0:nt, 40 + h:41 + h], mls[0:nt, 40 + h:41 + h], mls[0:nt, 28 + h:29 + h]), [mls], [mls])
            dve(lambda e: e.reciprocal(mls[0:nt, 44:48], mls[0:nt, 40:44]), [mls], [mls])
            for h in range(4):
                act(lambda e: e.activation(T1[0:nt, h * 256:(h + 1) * 256], acc[h][0:nt, 0:256], AF.Copy,
                                           scale=mls[0:nt, 44 + h:45 + h]), [acc[h], mls], [T1])
            rms_gate_out(T1, 4, 256, nt, ML_NG, SIGO, "ml_wout")

        def hgrn_sample():
            to_fm(X, nt)
            hgrn_proj(nt)
            act(lambda e: e.activation(T2[0:nt, :], HLF[0:nt, :], AF.Exp), [HLF], [T2])
            FF, QF32 = SSA, SSB
            for src, dstb, dt_is_bf in ((T2, FF, False), (HQ, QF32, False)):
                for half in range(2):
                    tb = PB[2 + half]
                    tv = tb[:].rearrange("p (a b) -> p a b", a=4)
                    for j in range(4):
                        h = half * 4 + j
                        pe(lambda e: e.transpose(tv[:, j, 0:nt], src[0:nt, h * 128:(h + 1) * 128], idNS), [src, CST], [tb])
                    act(lambda e: e.copy(dstb[:, half * 4 * NS:(half + 1) * 4 * NS].rearrange("p (h s) -> p h s", h=4),
                                         tv[:, :, 0:nt]), [tb], [dstb])
            ffv = FF[:, 0:8 * NS].rearrange("p (h s) -> p h s", h=8)
            qfv = QF32[:, 0:8 * NS].rearrange("p (h s) -> p h s", h=8)
            dve(lambda e: e.tensor_copy(T4[0:nt, :], HV[0:nt, :]), [HV], [T4])
            acc = [PB[0], PB[1]]
            for s in range(NS):
                KM = HG if False else T1
                dve(lambda e: e.tensor_scalar_mul(KM[0:nt, :], HK[0:nt, :], cs(C_ID)[0:nt, s:s + 1]), [HK, CST], [KM])
                QM = SSC
                qm = QM[:, 0:8 * NS].rearrange("p (h s) -> p h s", h=8)
                dve(lambda e: e.tensor_mul(qm, qfv, CST[:, C_EP + s * 16:C_EP + s * 16 + NS].unsqueeze(1).broadcast_to([128, 8, NS])),
                    [QF32, CST], [QM])
                for h in range(8):
                    SS_, SN_ = HS, HAT
                    SSf = HS[:, 0, :]
                    S.dma('sp', SSf, st_S[s, h], writes=[HS])
                    ub = PB[4 + h % 2]
                    pe(lambda e: e.matmul(ub[:, 0:128], KM[0:nt, h * 128:(h + 1) * 128], T4[0:nt, h * 128:(h + 1) * 128],
                                          start=True, stop=True), [KM, T4], [ub])
                    SNf = HS[:, 1 + h % 2, :]
                    dve(lambda e: e.scalar_tensor_tensor(SNf, SSf, ffv[:, h, s:s + 1], ub[:, 0:128], ALU.mult, ALU.add),
                        [HS, FF, ub], [HS])
                    S.dma('sp', oS_s[s, h], SNf, reads=[HS])
                    a = acc[h // 4][0:nt, (h % 4) * 128:(h % 4) * 128 + 128]
                    pe(lambda e: e.matmul(a, qm[:, h, :], SNf, start=(s == 0 and h % 4 == 0), stop=(s == NS - 1 and h % 4 == 3),
                                          skip_group_check=True), [QM, HS], [acc[h // 4]])
            for hf in range(2):
                act(lambda e: e.copy(T1[0:nt, hf * 512:(hf + 1) * 512], acc[hf][0:nt, :]), [acc[hf]], [T1])
            rms_gate_out(T1, 8, 128, nt, HG_NG, HG, "hg_wout")

        KS = HS

        KSV = HS[:].rearrange("p h v -> p (h v)").rearrange("p (a c) -> p a c", a=2)
        DQb, DQo = [QR, QR, SIGO], [0, 512, 0]
        DKb, DKo = [SIGO, KR, VF], [512, 0, 0]
        DVb, DVo = [HK, HK, HG], [0, 512, 0]

        def dil_sample():
            to_fm(X, nt)
            S.dma('sp', ROPE[0:nt, :], ropet[SEQ].partition_broadcast(nt), writes=[ROPE])

            def cons(si, c0, w, bank, ap):
                g, j = si // 3, si % 3
                if j == 0:
                    dst_buf[0] = DQb[g]
                    rope_apply(DQb[g][0:nt, DQo[g]:DQo[g] + 512].rearrange("p (h d) -> p h d", h=8), ap, 8, nt, bank)
                elif j == 1:
                    dst_buf[0] = DKb[g]
                    rope_apply(DKb[g][0:nt, DKo[g]:DKo[g] + 512].rearrange("p (h d) -> p h d", h=8), ap, 8, nt, bank)
                else:
                    act(lambda e: e.copy(DVb[g][0:nt, DVo[g]:DVo[g] + 512], ap), [bank], [DVb[g]])
            linear_tm(XT, 8, "dl_win", 4608, nt, cons)
            for g in range(3):
                L = DIL[g][0]
                S.dma('sp', okv_s[g][:, L - 1, 0].rearrange("s h d -> s (h d)"), DKb[g][0:nt, DKo[g]:DKo[g] + 512], reads=[DKb[g]])
                S.dma('sp', okv_s[g][:, L - 1, 1].rearrange("s h d -> s (h d)"), DVb[g][0:nt, DVo[g]:DVo[g] + 512], reads=[DVb[g]])
            acc_o, acc_d = [PB[6]], PB[7]
            n = 0
            for s in range(NS):
                for g in range(3):
                    L, dil = DIL[g]
                    QR_save = None
                    cache_attn_g(s, g, L, dil, acc_o, acc_d, n == 0, n == NS * 3 - 1)
                    n += 1
            dve(lambda e: e.tensor_copy(OALL[0:nt, 0:8, 0:64], acc_o[0][0:nt, 0:512].rearrange("p (h d) -> p h d", h=8)),
                [acc_o[0]], [OALL])
            dve(lambda e: e.tensor_copy(OALL[0:nt, 0:8, 64:65], acc_d[0:nt, 0:8].unsqueeze(2)), [acc_d], [OALL])
            for g in range(3):
                qv = DQb[g][0:nt, DQo[g]:DQo[g] + 512].rearrange("p (h d) -> p h d", h=8)
                kv = DKb[g][0:nt, DKo[g]:DKo[g] + 512].rearrange("p (h d) -> p h d", h=8)
                vv = DVb[g][0:nt, DVo[g]:DVo[g] + 512].rearrange("p (h d) -> p h d", h=8)
                t1v = T1[0:nt, 0:512].rearrange("p (h d) -> p h d", h=8)
                dve(lambda e: e.tensor_mul(t1v, qv, kv), [DQb[g], DKb[g]], [T1])
                dve(lambda e: e.tensor_reduce(sm[0:nt, 32:40], t1v, AX.X, ALU.add), [T1], [sm])
                act(lambda e: e.activation(sm[0:nt, 32:40], sm[0:nt, 32:40], AF.Exp, scale=0.125), [sm], [sm])
                dve(lambda e: e.tensor_mul(t1v, vv, sm[0:nt, 32:40].unsqueeze(2).broadcast_to([nt, 8, 64])), [DVb[g], sm], [T1])
                dve(lambda e: e.tensor_add(OALL[0:nt, 0:8, 0:64], OALL[0:nt, 0:8, 0:64], t1v), [OALL, T1], [OALL])
                dve(lambda e: e.tensor_add(OALL[0:nt, 0:8, 64:65], OALL[0:nt, 0:8, 64:65], sm[0:nt, 32:40].unsqueeze(2)),
                    [OALL, sm], [OALL])
            dve(lambda e: e.reciprocal(sm[0:nt, 48:56], OALL[0:nt, 0:8, 64:65].rearrange("p h o -> p (h o)")), [OALL], [sm])
            dve(lambda e: e.tensor_mul(YB[0:nt, 0:512].rearrange("p (h d) -> p h d", h=8), OALL[0:nt, 0:8, 0:64],
                                       sm[0:nt, 48:56].unsqueeze(2).broadcast_to([nt, 8, 64])), [OALL, sm], [YB])
            prefetch_ln(0, 1)
            to_fm(YB, nt, KC=4, bf=True)
            linear_tm(XT, 4, "dl_wout", D, nt, resid_consumer(nt))

        def cache_attn_g(s, g, L, dil, acc_o, acc_d, first, last):
            S.dma('sp', KSV[:, :, :], kvin[g][s, 0:L:dil].rearrange("l a h d -> l a (h d)"), writes=[KS])
            qb = PB[4]
            dve(lambda e: e.tensor_scalar_mul(T4[0:nt, 512:1024], DQb[g][0:nt, DQo[g]:DQo[g] + 512], cs(C_ID)[0:nt, s:s + 1]), [DQb[g], CST], [T4])
            pe(lambda e: e.matmul(qb[:, 0:512], cs(C_ONES)[0:nt, :], T4[0:nt, 512:1024], start=True, stop=True), [CST, T4], [qb])
            dve(lambda e: e.tensor_mul(T1[:, 0:512], qb[:, 0:512], KSV[:, 0, :]), [qb, KS], [T1])
            dve(lambda e: e.tensor_reduce(sm[:, 0:8], T1[:, 0:512].rearrange("p (h d) -> p h d", h=8), AX.X, ALU.add), [T1], [sm])
            act(lambda e: e.activation(sm[:, 16:24], sm[:, 0:8], AF.Exp, scale=0.125), [sm], [sm])
            dve(lambda e: e.tensor_mul(T2[:, 0:512].rearrange("p (h d) -> p h d", h=8),
                                       KSV[:, 1, :].rearrange("p (h d) -> p h d", h=8),
                                       sm[:, 16:24].unsqueeze(2).broadcast_to([128, 8, 64])), [KS, sm], [T2])
            pe(lambda e: e.matmul(acc_o[0][0:nt, 0:512], EP(s), T2[:, 0:512], start=first, stop=last), [CST, T2], [acc_o[0]])
            pe(lambda e: e.matmul(acc_d[0:nt, 0:8], EP(s), sm[:, 16:24], start=first, stop=last), [CST, sm], [acc_d])

        def swa_sample():
            to_fm(X, nt)
            S.dma('sp', ROPE[0:nt, :], ropet[SEQ].partition_broadcast(nt), writes=[ROPE])

            def cons(si, c0, w, bank, ap):
                if si < 2:
                    dst_buf[0] = QR
                    rope_apply(QR[0:nt, si * 512:(si + 1) * 512].rearrange("p (h d) -> p h d", h=8), ap, 8, nt, bank)
                else:
                    dst_buf[0] = KR
                    rope_apply(KR[0:nt, 0:128].rearrange("p (h d) -> p h d", h=2), ap[:, 0:128], 2, nt, bank)
                    act(lambda e: e.copy(VF[0:nt, 0:128], ap[:, 128:256]), [bank], [VF])
            linear_tm(XT, 8, "sw_win", 1280, nt, cons)
            S.dma('sp', osw_s[:, 127, 0].rearrange("s h d -> s (h d)"), KR[0:nt, 0:128], reads=[KR])
            S.dma('sp', osw_s[:, 127, 1].rearrange("s h d -> s (h d)"), VF[0:nt, 0:128], reads=[VF])
            acc_o, acc_d = [PB[6], PB[5]], PB[7]
            for s in range(NS):
                S.dma('sp', KSV[:, :, 0:128], swain[s].rearrange("l a h d -> l a (h d)"), writes=[KS])
                for half in range(2):
                    qb = PB[4]
                    dve(lambda e: e.tensor_scalar_mul(T4[0:nt, 512:1024], QR[0:nt, half * 512:(half + 1) * 512], cs(C_ID)[0:nt, s:s + 1]), [QR, CST], [T4])
                    pe(lambda e: e.matmul(qb[:, 0:512], cs(C_ONES)[0:nt, :], T4[0:nt, 512:1024], start=True, stop=True), [CST, T4], [qb])
                    kb = KSV[:, 0, half * 64:(half + 1) * 64].unsqueeze(1).broadcast_to([128, 8, 64])
                    dve(lambda e: e.tensor_mul(T1[:, 0:512].rearrange("p (h d) -> p h d", h=8),
                                               qb[:, 0:512].rearrange("p (h d) -> p h d", h=8), kb), [qb, KS], [T1])
                    dve(lambda e: e.tensor_reduce(sm[:, 0:8], T1[:, 0:512].rearrange("p (h d) -> p h d", h=8), AX.X, ALU.add),
                        [T1], [sm])
                    act(lambda e: e.activation(sm[:, 16 + 8 * half:24 + 8 * half], sm[:, 0:8], AF.Exp, scale=0.125), [sm], [sm])
                    vb = KSV[:, 1, half * 64:(half + 1) * 64].unsqueeze(1).broadcast_to([128, 8, 64])
                    dve(lambda e: e.tensor_mul(T2[:, 0:512].rearrange("p (h d) -> p h d", h=8), vb,
                                               sm[:, 16 + 8 * half:24 + 8 * half].unsqueeze(2).broadcast_to([128, 8, 64])),
                        [KS, sm], [T2])
                    pe(lambda e: e.matmul(acc_o[half][0:nt, 0:512], EP(s), T2[:, 0:512], start=(s == 0), stop=(s == NS - 1)),
                       [CST, T2], [acc_o[half]])
                pe(lambda e: e.matmul(acc_d[0:nt, 0:16], EP(s), sm[:, 16:32], start=(s == 0), stop=(s == NS - 1)), [CST, sm], [acc_d])
            for half in range(2):
                dve(lambda e: e.tensor_copy(OALL[0:nt, 8 * half:8 * half + 8, 0:64],
                                            acc_o[half][0:nt, 0:512].rearrange("p (h d) -> p h d", h=8)), [acc_o[half]], [OALL])
            dve(lambda e: e.tensor_copy(OALL[0:nt, :, 64:65], acc_d[0:nt, 0:16].unsqueeze(2)), [acc_d], [OALL])
            qv = QR[0:nt, :].rearrange("p (k g d) -> p k g d", k=2, g=8)
            kv = KR[0:nt, 0:128].rearrange("p (k d) -> p k d", k=2).unsqueeze(2).broadcast_to([nt, 2, 8, 64])
            vv = VF[0:nt, 0:128].rearrange("p (k d) -> p k d", k=2).unsqueeze(2).broadcast_to([nt, 2, 8, 64])
            t1v = T1[0:nt, :].rearrange("p (k g d) -> p k g d", k=2, g=8)
            dve(lambda e: e.tensor_mul(t1v, qv, kv), [QR, KR], [T1])
            dve(lambda e: e.tensor_reduce(sm[0:nt, 32:48], T1[0:nt, :].rearrange("p (h d) -> p h d", h=16), AX.X, ALU.add), [T1], [sm])
            act(lambda e: e.activation(sm[0:nt, 32:48], sm[0:nt, 32:48], AF.Exp, scale=0.125), [sm], [sm])
            dve(lambda e: e.tensor_mul(t1v, vv, sm[0:nt, 32:48].rearrange("p (k g) -> p k g", k=2).unsqueeze(3).broadcast_to([nt, 2, 8, 64])),
                [VF, sm], [T1])
            dve(lambda e: e.tensor_add(OALL[0:nt, :, 0:64], OALL[0:nt, :, 0:64], T1[0:nt, :].rearrange("p (h d) -> p h d", h=16)),
                [OALL, T1], [OALL])
            dve(lambda e: e.tensor_add(OALL[0:nt, :, 64:65], OALL[0:nt, :, 64:65], sm[0:nt, 32:48].unsqueeze(2)), [OALL, sm], [OALL])
            swa_finish(nt)

        issue_shift(len(shift_jobs))
        X.set(XS[0])
        S.dma('sp', X[0:nt, :], xs, writes=[X])
        smix = [mlstm_sample, dil_sample, hgrn_sample, swa_sample]
        for layer in range(4):
            lin['i'] = 0
            smix[layer]()
            layer_norm(nt, 0, layer)
            mlp(1, nt, layer)
        S.dma('sp', ys, X[0:nt, :], reads=[X])
        S.finish()
    return nc


_cache = {}


def kernel(**inp):
    SEQ = inp["x_prompt"].shape[1]
    NB = inp["x_sample"].shape[0]
    NS = NB // 8
    key = (SEQ, NS)
    if key not in _cache:
        _cache[key] = build(SEQ, NS)
    nc = _cache[key]
    f = lambda a: np.ascontiguousarray(np.asarray(a, dtype=np.float32))
    cst, cstm = host_consts()
    rope = host_rope(SEQ)
    shared = dict(
        ml_win=f(inp["mlstm_w_in"][0]), ml_bg=f(inp["mlstm_b_gates"][0]), ml_ng=f(inp["mlstm_norm_g"][0]), ml_wout=f(inp["mlstm_w_out"][0]),
        dl_win=f(inp["dil_w_in"][0]), dl_wout=f(inp["dil_w_out"][0]),
        hg_win=f(inp["hgrn_w_in"][0]), hg_bf=f(inp["hgrn_b_f"][0]), hg_lb=f(inp["hgrn_lb_logits"]), hg_ng=f(inp["hgrn_norm_g"][0]),
        hg_wout=f(inp["hgrn_w_out"][0]),
        sw_win=f(inp["swa_w_in"][0]), sw_sink=f(inp["swa_sinks"][0]), sw_wout=f(inp["swa_w_out"][0]),
        ln1_g=f(inp["ln1_g"]), ln1_b=f(inp["ln1_b"]), ln2_g=f(inp["ln2_g"]), ln2_b=f(inp["ln2_b"]),
        w1=f(inp["mlp_w1"]), w2=f(inp["mlp_w2"]), cst=cst, cstm=cstm, rope=rope)
    in_maps = []
    for c in range(8):
        sl = slice(c * NS, (c + 1) * NS)
        m = dict(shared)
        m.update(
            xp=f(inp["x_prompt"][c % 2]), xs=f(inp["x_sample"][sl, 0]),
            st_C=f(inp["state_mlstm_C"][0, sl]), st_n=f(inp["state_mlstm_n"][0, sl]), st_m=f(inp["state_mlstm_m"][0, sl]),
            kv0=f(inp["cache_dil_kv0"][0, sl]), kv1=f(inp["cache_dil_kv1"][0, sl]), kv2=f(inp["cache_dil_kv2"][0, sl]),
            st_S=f(inp["state_hgrn_S"][0, sl]), swakv=f(inp["cache_swa_kv"][0, sl]))
        in_maps.append(m)
    res = run_bass_kernel_spmd(nc, in_maps, core_ids=list(range(8))).results
    cat = lambda k: np.concatenate([res[c][k] for c in range(8)], axis=0)
    two = lambda k: np.stack([res[0][k], res[1][k]], axis=0)
    outs = (
        two("yp"), cat("ys")[:, None, :],
        two("oC_p")[None], cat("oC_s")[None], two("on_p")[None], cat("on_s")[None],
        np.stack([res[0]["om_p"][0], res[1]["om_p"][0]])[None], cat("om_s")[None],
        two("okv0_p")[None], cat("okv0_s")[None], two("okv1_p")[None], cat("okv1_s")[None],
        two("okv2_p")[None], cat("okv2_s")[None],
        two("oS_p")[None], cat("oS_s")[None], two("osw_p")[None], cat("osw_s")[None])
    return tuple(np.ascontiguousarray(o.astype(np.float32)) for o in outs)
```
